# Optimizing a Trainium2 kernel written in Bass

```python
import jax, jax.numpy as jnp
from jax import lax
import numpy as np

D_MODEL = 1024
BATCH = 8
SEQ = 8192
DEPTH = 1

GRID_W = 64
CTX_LEN = 256

NA_HEADS = 8
NA_HEAD_DIM = 64
NA_WIN_ROWS = 8
NA_WIN_COLS = 16
NA_WIDTH = NA_HEADS * NA_HEAD_DIM

GLA_HEADS = 4
GLA_KEY_DIM = 64
GLA_VAL_DIM = 128
GLA_QK_WIDTH = GLA_HEADS * GLA_KEY_DIM
GLA_V_WIDTH = GLA_HEADS * GLA_VAL_DIM
GLA_GATE_RANK = 16
GLA_GATE_TAU = 16.0
GLA_CHUNK = 64

MIX_WIDTH = NA_WIDTH + GLA_V_WIDTH
IN_SPLITS = (NA_WIDTH, NA_WIDTH, NA_WIDTH, GLA_QK_WIDTH, GLA_QK_WIDTH,
             GLA_V_WIDTH, GLA_V_WIDTH, GLA_GATE_RANK, GLA_GATE_RANK)
IN_WIDTH = 3 * NA_WIDTH + 2 * GLA_QK_WIDTH + 2 * GLA_V_WIDTH + 2 * GLA_GATE_RANK

FFN_HIDDEN = ((8 * D_MODEL + 3 * 256 - 1) // (3 * 256)) * 256
ROPE_BASE = 10000.0
NORM_EPS = 1e-6

kernel_name = "hybrid_na_gla_dit_layer"


def rmsnorm(x, g):
    xf = x.astype(jnp.float32)
    y = xf * lax.rsqrt(jnp.mean(xf * xf, axis=-1, keepdims=True) + NORM_EPS)
    return (y * g.astype(jnp.float32)).astype(x.dtype)


def modulate(x, shift, scale):
    return x * (1 + scale) + shift


def ada_mod(cond, w_mod, b_mod):
    return jnp.split(jax.nn.silu(cond) @ w_mod + b_mod, 6, axis=-1)


def split_proj(p):
    offs = [int(o) for o in np.cumsum(IN_SPLITS)[:-1]]
    return jnp.split(p, offs, axis=-1)


def to_heads(t, n_heads):
    b, l, _ = t.shape
    return t.reshape(b, l, n_heads, -1).transpose(0, 2, 1, 3)


def from_heads(t):
    b, h, l, d = t.shape
    return t.transpose(0, 2, 1, 3).reshape(b, l, h * d)


def rope_1d(x, pos):
    half = x.shape[-1] // 2
    inv = ROPE_BASE ** (-jnp.arange(half, dtype=jnp.float32) / half)
    ang = pos.astype(jnp.float32)[:, None] * inv[None, :]
    cos, sin = jnp.cos(ang).astype(x.dtype), jnp.sin(ang).astype(x.dtype)
    x1, x2 = x[..., :half], x[..., half:]
    return jnp.concatenate([x1 * cos - x2 * sin, x1 * sin + x2 * cos], axis=-1)


def axial_rope(x):
    l = x.shape[2]
    t = jnp.arange(l)
    half = x.shape[-1] // 2
    return jnp.concatenate([rope_1d(x[..., :half], t // GRID_W),
                            rope_1d(x[..., half:], t % GRID_W)], axis=-1)


def neighbourhood_attention(q, k, v, k_ctx, v_ctx, rpb):
    b, h, l, dh = q.shape
    rows = l // GRID_W
    kr = min(NA_WIN_ROWS, rows)
    kw = NA_WIN_COLS
    grid = lambda t: t.reshape(b, h, rows, GRID_W, dh)
    qg, kg, vg = grid(q * dh ** -0.5), grid(k), grid(v)
    cols = np.arange(GRID_W)
    col_start = np.clip(cols - kw // 2, 0, GRID_W - kw)
    col_idx = col_start[:, None] + np.arange(kw)[None, :]
    col_off = col_idx - cols[:, None] + (kw - 1)
    rpb_cols = rpb[:, :, col_off]

    def one_row(r):
        rs = jnp.clip(r - kr // 2, 0, rows - kr)
        q_r = lax.dynamic_index_in_dim(qg, r, axis=2, keepdims=False)
        k_band = lax.dynamic_slice_in_dim(kg, rs, kr, axis=2)
        v_band = lax.dynamic_slice_in_dim(vg, rs, kr, axis=2)
        k_win = k_band[:, :, :, col_idx]
        v_win = v_band[:, :, :, col_idx]
        row_off = rs + jnp.arange(kr) - r + (NA_WIN_ROWS - 1)
        bias = jnp.take(rpb_cols, row_off, axis=1).transpose(0, 2, 1, 3)
        s_loc = (jnp.einsum('bhcd,bhrcwd->bhcrw', q_r, k_win).astype(jnp.float32)
                 + bias[None].astype(jnp.float32))
        s_ctx = jnp.einsum('bhcd,bhnd->bhcn', q_r, k_ctx).astype(jnp.float32)
        s = jnp.concatenate([s_loc.reshape(b, h, GRID_W, kr * kw), s_ctx], axis=-1)
        p = jax.nn.softmax(s, axis=-1).astype(v.dtype)
        p_loc = p[..., :kr * kw].reshape(b, h, GRID_W, kr, kw)
        p_ctx = p[..., kr * kw:]
        return (jnp.einsum('bhcrw,bhrcwd->bhcd', p_loc, v_win)
                + jnp.einsum('bhcn,bhnd->bhcd', p_ctx, v_ctx))

    out = lax.map(one_row, jnp.arange(rows))
    return out.transpose(1, 2, 0, 3, 4).reshape(b, h, l, dh)


def ctx_self_attention(q, k, v):
    s = jnp.einsum('bhqd,bhkd->bhqk', q * q.shape[-1] ** -0.5, k).astype(jnp.float32)
    return jnp.einsum('bhqk,bhkd->bhqd', jax.nn.softmax(s, axis=-1).astype(v.dtype), v)


def gla_chunked(q, k, v, logg, s0):
    b_, h, l, _ = q.shape
    dv = v.shape[-1]
    n = l // GLA_CHUNK
    ch = lambda t: t.astype(jnp.float32).reshape(b_, h, n, GLA_CHUNK, t.shape[-1])
    q, k, v, logg = ch(q), ch(k), ch(v), ch(logg)
    bcum = jnp.cumsum(logg, axis=3)
    b_end = bcum[:, :, :, -1:, :]
    q_dec = q * jnp.exp(bcum)
    k_inv = k * jnp.exp(-bcum)
    k_end = k * jnp.exp(b_end - bcum)
    lower = jnp.tril(jnp.ones((GLA_CHUNK, GLA_CHUNK), dtype=bool))
    attn = jnp.where(lower, jnp.einsum('bhncd,bhnsd->bhncs', q_dec, k_inv), 0.0)
    o_intra = jnp.einsum('bhncs,bhnse->bhnce', attn, v)
    kv_chunk = jnp.einsum('bhncd,bhnce->bhnde', k_end, v)
    decay = jnp.exp(b_end[:, :, :, 0, :])

    def step(s, inp):
        d, kv = inp
        return d[..., None] * s + kv, s

    _, s_prev = lax.scan(step, s0.astype(jnp.float32),
                         (jnp.moveaxis(decay, 2, 0), jnp.moveaxis(kv_chunk, 2, 0)))
    s_prev = jnp.moveaxis(s_prev, 0, 2)
    o_inter = jnp.einsum('bhncd,bhnde->bhnce', q_dec, s_prev)
    return (o_intra + o_inter).reshape(b_, h, l, dv)


def gla_final_state(k, v, logg):
    k, v, logg = k.astype(jnp.float32), v.astype(jnp.float32), logg.astype(jnp.float32)
    bcum = jnp.cumsum(logg, axis=2)
    return jnp.einsum('bhtd,bhte->bhde', k * jnp.exp(bcum[:, :, -1:] - bcum), v)


def gla_output(o, r, gain):
    o = o * lax.rsqrt(jnp.mean(o * o, axis=-1, keepdims=True) + NORM_EPS)
    y = from_heads(o) * gain.astype(jnp.float32) * jax.nn.silu(r.astype(jnp.float32))
    return y.astype(r.dtype)


def project(h, w_in, wa2_f, ba_f, wa2_b, ba_b):
    na_q, na_k, na_v, gq, gk, gv, gr, af, ab = split_proj(h @ w_in)
    logf = jax.nn.log_sigmoid((af @ wa2_f + ba_f).astype(jnp.float32)) / GLA_GATE_TAU
    logb = jax.nn.log_sigmoid((ab @ wa2_b + ba_b).astype(jnp.float32)) / GLA_GATE_TAU
    return (to_heads(na_q, NA_HEADS), to_heads(na_k, NA_HEADS), to_heads(na_v, NA_HEADS),
            to_heads(gq, GLA_HEADS), to_heads(gk, GLA_HEADS), to_heads(gv, GLA_HEADS), gr,
            to_heads(logf, GLA_HEADS), to_heads(logb, GLA_HEADS))


def mixer(h, hc, w_in, na_rpb, wa2_f, ba_f, wa2_b, ba_b, gla_norm, w_out, with_ctx_out):
    nq, nk, nv, gq, gk, gv, gr, lf, lb = project(h, w_in, wa2_f, ba_f, wa2_b, ba_b)
    cnq, cnk, cnv, cgq, cgk, cgv, cgr, clf, clb = project(hc, w_in, wa2_f, ba_f, wa2_b, ba_b)
    flip = lambda t: jnp.flip(t, axis=2)
    na_out = from_heads(neighbourhood_attention(nq, nk, nv, cnk, cnv, na_rpb))
    gq = axial_rope(gq) * GLA_KEY_DIM ** -0.5
    gk = axial_rope(gk)
    s_f = gla_final_state(cgk, cgv, clf)
    s_b = gla_final_state(flip(cgk), flip(cgv), flip(clb))
    o = (gla_chunked(gq, gk, gv, lf, s_f)
         + flip(gla_chunked(flip(gq), flip(gk), flip(gv), flip(lb), s_b)))
    gla_out = gla_output(o, gr, gla_norm)
    y = jnp.concatenate([na_out, gla_out.astype(na_out.dtype)], axis=-1) @ w_out
    if not with_ctx_out:
        return y, None
    cgq = cgq * GLA_KEY_DIM ** -0.5
    zero = jnp.zeros_like(s_f)
    co = (gla_chunked(cgq, cgk, cgv, clf, zero)
          + flip(gla_chunked(flip(cgq), flip(cgk), flip(cgv), flip(clb), zero)))
    na_c = from_heads(ctx_self_attention(cnq, cnk, cnv))
    yc = jnp.concatenate([na_c, gla_output(co, cgr, gla_norm).astype(na_c.dtype)], axis=-1) @ w_out
    return y, yc


def swiglu(h, w_gate_up, w_down):
    g, u = jnp.split(h @ w_gate_up, 2, axis=-1)
    return (jax.nn.silu(g) * u) @ w_down


def setup_inputs(seed: int = 0) -> dict:
    key = jax.random.key(seed)
    ks = jax.random.split(key, 20)
    nrm = lambda k, shape, scale: jax.random.normal(k, shape, jnp.float32) * scale
    gain = lambda k, width: 1.0 + nrm(k, (DEPTH, width), 0.05)
    return {
        "x": nrm(ks[0], (BATCH, SEQ, D_MODEL), 1.0),
        "c": nrm(ks[1], (BATCH, D_MODEL), 1.0),
        "ctx": nrm(ks[2], (BATCH, CTX_LEN, D_MODEL), 1.0),
        "c_ctx": nrm(ks[3], (D_MODEL,), 1.0),
        "w_mod": nrm(ks[4], (DEPTH, D_MODEL, 6 * D_MODEL), 0.5 * D_MODEL ** -0.5),
        "b_mod": nrm(ks[5], (DEPTH, 6 * D_MODEL), 0.02),
        "norm_pre_mix": gain(ks[6], D_MODEL),
        "norm_post_mix": gain(ks[7], D_MODEL),
        "norm_pre_ffn": gain(ks[8], D_MODEL),
        "norm_post_ffn": gain(ks[9], D_MODEL),
        "w_in": nrm(ks[10], (DEPTH, D_MODEL, IN_WIDTH), D_MODEL ** -0.5),
        "na_rpb": nrm(ks[11], (DEPTH, NA_HEADS, 2 * NA_WIN_ROWS - 1, 2 * NA_WIN_COLS - 1), 0.1),
        "gla_wa2_f": nrm(ks[12], (DEPTH, GLA_GATE_RANK, GLA_QK_WIDTH), GLA_GATE_RANK ** -0.5),
        "gla_ba_f": nrm(ks[13], (DEPTH, GLA_QK_WIDTH), 0.1),
        "gla_wa2_b": nrm(ks[14], (DEPTH, GLA_GATE_RANK, GLA_QK_WIDTH), GLA_GATE_RANK ** -0.5),
        "gla_ba_b": nrm(ks[15], (DEPTH, GLA_QK_WIDTH), 0.1),
        "gla_norm": gain(ks[16], GLA_V_WIDTH),
        "w_out": nrm(ks[17], (DEPTH, MIX_WIDTH, D_MODEL), MIX_WIDTH ** -0.5),
        "w_gate_up": nrm(ks[18], (DEPTH, D_MODEL, 2 * FFN_HIDDEN), D_MODEL ** -0.5),
        "w_down": nrm(ks[19], (DEPTH, FFN_HIDDEN, D_MODEL), FFN_HIDDEN ** -0.5),
    }


def reference(x, c, ctx, c_ctx, w_mod, b_mod, norm_pre_mix, norm_post_mix, norm_pre_ffn,
              norm_post_ffn, w_in, na_rpb, gla_wa2_f, gla_ba_f, gla_wa2_b, gla_ba_b,
              gla_norm, w_out, w_gate_up, w_down):
    for i in range(DEPTH):
        last = i == DEPTH - 1
        sh1, sc1, gt1, sh2, sc2, gt2 = [m[:, None, :] for m in ada_mod(c, w_mod[i], b_mod[i])]
        csh1, csc1, cgt1, csh2, csc2, cgt2 = ada_mod(c_ctx, w_mod[i], b_mod[i])
        h = modulate(rmsnorm(x, norm_pre_mix[i]), sh1, sc1)
        hc = modulate(rmsnorm(ctx, norm_pre_mix[i]), csh1, csc1)
        y, yc = mixer(h, hc, w_in[i], na_rpb[i], gla_wa2_f[i], gla_ba_f[i], gla_wa2_b[i],
                      gla_ba_b[i], gla_norm[i], w_out[i], not last)
        x = x + gt1 * rmsnorm(y, norm_post_mix[i])
        h = modulate(rmsnorm(x, norm_pre_ffn[i]), sh2, sc2)
        x = x + gt2 * rmsnorm(swiglu(h, w_gate_up[i], w_down[i]), norm_post_ffn[i])
        if not last:
            ctx = ctx + cgt1 * rmsnorm(yc, norm_post_mix[i])
            hc = modulate(rmsnorm(ctx, norm_pre_ffn[i]), csh2, csc2)
            ctx = ctx + cgt2 * rmsnorm(swiglu(hc, w_gate_up[i], w_down[i]), norm_post_ffn[i])
    return x
```

```python
import numpy as np
import ml_dtypes
from contextlib import ExitStack
import concourse.bass as bass
import concourse.mybir as mybir
from concourse.bass_utils import run_bass_kernel_spmd

F32 = mybir.dt.float32
BF16 = mybir.dt.bfloat16
AF = mybir.ActivationFunctionType
ALU = mybir.AluOpType
AX = mybir.AxisListType

D = 1024
NKC = 8
CTX = 256
FFN = 2816
NJ = FFN // 128
EPS = 1e-6
NEG = -30000.0


class Buf:
    __slots__ = ("name", "w", "rs")

    def __init__(self, name=""):
        self.name = name
        self.w = None
        self.rs = []


class Op:
    __slots__ = ("eng", "fn", "deps", "seq", "sig", "semval", "is_dma", "sem", "waits", "idx", "hz", "cost", "lat", "succ", "nin", "rt", "fin")


class Prog:
    ENGS = ["pe", "act", "dve", "pool", "sp"]

    def __init__(self, ndma_sems=14):
        self.ops = []
        self.eng_ops = {e: [] for e in self.ENGS}
        self.dma_count = {e: 0 for e in self.ENGS}
        self.dma_hist = {e: [] for e in self.ENGS}
        self.ndma = ndma_sems
        self.do_sched = True

    def add(self, eng, fn, reads=(), writes=(), dma=False, cost=300.0, lat=0.0):
        op = Op()
        op.eng = eng
        op.fn = fn
        op.is_dma = dma
        op.sig = False
        op.idx = len(self.ops)
        op.cost = cost
        op.lat = lat
        deps = []
        hz = []
        for b in reads:
            w = b.w
            if w is not None:
                hz.append(w)
                if w.is_dma or dma or w.eng != eng or eng != "pe":
                    deps.append(w)
        for b in writes:
            w = b.w
            if w is not None:
                hz.append(w)
                if w.is_dma or dma or w.eng != eng or eng != "pe":
                    deps.append(w)
            for r in b.rs:
                hz.append(r)
                if r.is_dma or dma or r.eng != eng or eng != "pe":
                    deps.append(r)
        op.deps = deps
        op.hz = hz
        for b in reads:
            b.rs.append(op)
        for b in writes:
            b.w = op
            b.rs = []
        self.ops.append(op)
        self.eng_ops[eng].append(op)
        return op

    def schedule(self, window=40, xlat=150.0):
        ops = self.ops
        for op in ops:
            op.succ = []
            op.rt = 0.0
        for op in ops:
            seen = set()
            n = 0
            for d in op.hz:
                if d is op or d.idx in seen:
                    continue
                seen.add(d.idx)
                d.succ.append(op)
                n += 1
            op.nin = n
        pend = {e: list(self.eng_ops[e]) for e in self.ENGS}
        head = {e: 0 for e in self.ENGS}
        done = [False] * len(ops)
        tcl = {e: 0.0 for e in self.ENGS}
        order = {e: [] for e in self.ENGS}
        remaining = len(ops)
        while remaining:
            best = None
            for e in self.ENGS:
                lst = pend[e]
                h = head[e]
                while h < len(lst) and done[lst[h].idx]:
                    h += 1
                head[e] = h
                cnt = 0
                k = h
                te = tcl[e]
                while k < len(lst) and cnt < window:
                    o = lst[k]
                    k += 1
                    if done[o.idx]:
                        continue
                    cnt += 1
                    if o.nin == 0:
                        st = o.rt if o.rt > te else te
                        if best is None or st < best[0] or (st == best[0] and o.idx < best[1].idx):
                            best = (st, o)
                            if st <= te and cnt == 1:
                                break
            st, o = best
            e = o.eng
            done[o.idx] = True
            remaining -= 1
            order[e].append(o)
            if o.is_dma:
                tcl[e] = st + 100.0
                o.fin = st + 100.0 + o.lat
            else:
                tcl[e] = st + o.cost
                o.fin = st + o.cost
            for sc in o.succ:
                sc.nin -= 1
                r = o.fin + (xlat if (sc.eng != e or o.is_dma) else 0.0)
                if r > sc.rt:
                    sc.rt = r
        self.eng_ops = order
        self.est_ns = max(tcl.values())

    def finalize(self):
        if self.do_sched:
            self.schedule()
        for e in self.ENGS:
            hist = []
            for i_, op in enumerate(self.eng_ops[e]):
                op.seq = i_
                if op.is_dma:
                    n = len(hist)
                    if n >= self.ndma:
                        op.deps.append(hist[n - self.ndma])
                    op.sem = n % self.ndma
                    op.semval = 16 * (n // self.ndma + 1)
                    hist.append(op)
            self.dma_count[e] = len(hist)
        waited = {e: {p: -1 for p in self.ENGS} for e in self.ENGS}
        waited_dma = {e: set() for e in self.ENGS}
        for op in [o for e in self.ENGS for o in self.eng_ops[e]]:
            need = {}
            need_dma = []
            for d in op.deps:
                if d is op:
                    continue
                if d.is_dma:
                    if d.idx not in waited_dma[op.eng]:
                        waited_dma[op.eng].add(d.idx)
                        need_dma.append(d)
                else:
                    if d.seq > waited[op.eng][d.eng]:
                        if d.eng not in need or need[d.eng].seq < d.seq:
                            need[d.eng] = d
            ws = []
            for p, d in need.items():
                waited[op.eng][p] = d.seq
                d.sig = True
                ws.append(d)
            ws.extend(need_dma)
            op.waits = ws
        for e in self.ENGS:
            c = 0
            for op in self.eng_ops[e]:
                if not op.is_dma and op.sig:
                    c += 1
                    op.semval = c

    def emit(self, nc, stack):
        self.finalize()
        esem = {e: stack.enter_context(nc.semaphore("s_" + e)) for e in self.ENGS}
        dsem = {e: [stack.enter_context(nc.semaphore(f"d_{e}{i}")) for i in range(self.ndma)]
                for e in self.ENGS if self.dma_count[e] > 0}
        block = stack.enter_context(nc.Block())

        def run(e, eng):
            for op in self.eng_ops[e]:
                for d in op.waits:
                    if d.is_dma:
                        eng.wait_ge(dsem[d.eng][d.sem], d.semval)
                    else:
                        eng.wait_ge(esem[d.eng], d.semval)
                ins = op.fn(eng)
                if op.is_dma:
                    ins.then_inc(dsem[e][op.sem], 16)
                elif op.sig:
                    ins.then_inc(esem[e], 1)

        @block.tensor
        def _(eng):
            run("pe", eng)

        @block.scalar
        def _(eng):
            run("act", eng)

        @block.vector
        def _(eng):
            run("dve", eng)

        @block.gpsimd
        def _(eng):
            run("pool", eng)

        @block.sync
        def _(eng):
            run("sp", eng)


def _rope_tables(L):
    half = 16
    inv = (10000.0 ** (-np.arange(half, dtype=np.float32) / half)).astype(np.float32)
    t = np.arange(L)
    cosT = np.zeros((128, L), np.float32)
    sinT = np.zeros((128, L), np.float32)
    for p in range(128):
        d = p % 64
        pos = (t // 64) if d < 32 else (t % 64)
        dd = d % 32
        j = dd % 16
        ang = pos.astype(np.float32) * inv[j]
        cosT[p] = np.cos(ang)
        s = np.sin(ang)
        sinT[p] = -s if dd < 16 else s
    pm = np.zeros((128, 128), np.float32)
    for m in range(128):
        d = m % 64
        dd = d % 32
        partner = m + 16 if dd < 16 else m - 16
        pm[partner, m] = 1.0
    return cosT, sinT, pm


def _et_tables(rpb):
    H = rpb.shape[0]
    out = np.full((H, 128, 2, 16, 64), NEG, np.float32)
    c = np.arange(64)
    cs = np.clip(c - 8, 0, 48)
    for p in range(128):
        half, kc = p // 64, p % 64
        colvalid = (kc >= cs) & (kc <= cs + 15)
        cidx = np.clip(kc - c + 15, 0, 30)
        for j in range(16):
            dr = 14 - j + half
            if dr < 0 or dr > 14:
                continue
            vals = rpb[:, dr, :][:, cidx]
            vals = np.where(colvalid[None, :], vals, NEG)
            out[:, p, 0, j, :] = vals
            if 3 <= dr <= 10:
                out[:, p, 1, j, :] = vals
    return out.reshape(H, 128, 2048)


def na_plan(ROWS, m):
    def rs_of(r):
        return min(max(r - 4, 0), ROWS - 8)
    plan = []
    for t in range(ROWS // 2):
        rows = [r for r in range(8 * m, 8 * m + 8) if not (2 * t + 1 < rs_of(r) or 2 * t > rs_of(r) + 7)]
        if not rows:
            continue
        runs = []
        for r in rows:
            ty = 1 if (4 <= r <= ROWS - 4) else 0
            if runs and runs[-1][0] == ty and runs[-1][2] == r - 1:
                runs[-1][2] = r
            else:
                runs.append([ty, r, r])
        plan.append((t, runs, rows[0], rows[-1]))
    return plan


def build(L, kstop=99):
    ROWS = L // 64
    NB = L // 512
    NT = L // 128
    nc = bass.Bass("TRN2", target_bir_lowering=False)
    dr = lambda name, shape, dt, kind="ExternalInput": nc.dram_tensor(name, shape, dt, kind=kind).ap()
    x_d = dr("x", [L, D], F32)
    ctx_d = dr("ctx", [CTX, D], F32)
    cc_d = dr("cc", [D, 2], F32)
    wmod_d = dr("w_mod", [D, 6 * D], F32)
    bmod_d = dr("b_mod", [1, 6 * D], F32)
    g4_d = dr("g4", [4, D], F32)
    win_d = dr("w_in", [D, 3104], F32)
    et_d = dr("ettab", [8, 128, 2048], F32)
    wa2_d = dr("wa2blk", [32, 512], F32)
    ba_d = dr("bacol", [128, 4], F32)
    gg_d = dr("glagain", [128, 4], F32)
    cos_d = dr("cosT", [128, L], F32)
    sin_d = dr("sinT", [128, L], F32)
    pm_d = dr("pm", [128, 128], F32)
    idf_d = dr("identf", [128, 128], F32)
    tri_d = dr("tri4", [2, 128, 512], F32)
    smask_d = dr("scanmask", [128, 512], F32)
    sel_d = dr("sel", [2, 128], F32)
    wout_d = dr("w_out", [D, D], F32)
    wgu_d = dr("w_gu", [D, 2 * FFN], F32)
    wdn_d = dr("w_down", [FFN, D], F32)
    out_d = dr("out", [L, D], F32, kind="ExternalOutput")
    sbscr_d = dr("sbscr", [NT, 128, 512], BF16, kind="Internal")
    etscr_d = dr("etscr", [8, 128, 2048], BF16, kind="Internal")
    gscr_d = dr("gscr", [128, 1024], F32, kind="Internal")

    P = Prog()
    st = ExitStack()
    with st:
        def sb(name, shape, dt):
            return st.enter_context(nc.sbuf_tensor("sb_" + name, shape, dt))

        ARENA = 67840
        arena = sb("arena", [128, ARENA], BF16)
        win = arena[:, 0:24832].rearrange("p (k n) -> p k n", k=8)
        wout = arena[:, 24832:33024].rearrange("p (k n) -> p k n", k=8)
        ETR = [arena[:, 33024 + i * 2048: 33024 + (i + 1) * 2048] for i in range(2)]
        wmring = [arena[:, 37120 + i * 8192: 37120 + (i + 1) * 8192].bitcast(F32).rearrange("p (k n) -> p k n", k=8)
                  for i in range(2)]
        VR = [arena[:, 37120 + i * 1024: 37120 + (i + 1) * 1024].rearrange("p (h t d) -> p h t d", h=8, t=2)
              for i in range(12)]
        o_ = 37120 + 12 * 1024
        KTR = [arena[:, o_ + i * 2048: o_ + (i + 1) * 2048].rearrange("p (j n) -> p j n", j=4) for i in range(3)]
        o_ += 3 * 2048
        QTR = [arena[:, o_ + i * 2048: o_ + (i + 1) * 2048].rearrange("p (j n) -> p j n", j=4) for i in range(2)]
        o_ += 2 * 2048
        qdec_v = [arena[:, o_ + i * 512: o_ + (i + 1) * 512] for i in range(4)]
        o_ += 2048
        kinv_v = [arena[:, o_ + i * 1024: o_ + (i + 1) * 1024].rearrange("p (h n) -> p h n", h=2) for i in range(4)]
        o_ += 4096
        kendT = arena[:, o_: o_ + 2048].rearrange("p (c n) -> p c n", c=4)
        o_ += 2048
        assert o_ <= ARENA, o_
        wgu = arena[:, 0:45056].rearrange("p (k n) -> p k n", k=8)
        wdn = arena[:, 45056:67584].rearrange("p (j n) -> p j n", j=NJ)
        B_win = [Buf() for _ in range(8)]
        B_wout = [Buf() for _ in range(8)]
        B_ETR = [Buf(), Buf()]
        B_etscr = [Buf() for _ in range(8)]
        B_gscr = Buf()
        B_wm = [Buf(), Buf()]
        B_VR = [Buf() for _ in range(12)]
        B_KTR = [Buf() for _ in range(3)]
        B_QTR = [Buf() for _ in range(2)]
        qdec = [(qdec_v[i], Buf()) for i in range(4)]
        kinv = [(kinv_v[i], Buf()) for i in range(4)]
        B_kendT = Buf()
        B_arenaM = (B_win + B_wout + B_ETR + B_wm + B_VR + B_KTR + B_QTR + [b for _, b in qdec + kinv] + [B_kendT])
        B_wgu = [Buf() for _ in range(8)]
        B_wdn = [Buf() for _ in range(NJ)]

        ps = [st.enter_context(nc.psum_tensor(f"ps{i}", [128, 512], F32)) for i in range(8)]
        B_ps = [Buf(f"ps{i}") for i in range(8)]

        def T(name, shape, dt):
            return sb(name, shape, dt), Buf(name)

        identf, B_identf = T("identf", [128, 128], F32)
        identb, B_identb = T("identb", [128, 128], BF16)
        pm, B_pm = T("pm", [128, 128], BF16)
        onesb, B_onesb = T("onesb", [128, 128], BF16)
        tri, B_tri = T("tri", [128, 2, 512], BF16)
        smask, B_smask = T("smask", [128, 512], BF16)
        sel, B_sel = T("sel", [2, 128], F32)
        wa2, B_wa2 = T("wa2", [32, 512], BF16)
        negba, B_negba = T("negba", [128, 4], F32)
        ggain, B_ggain = T("ggain", [128, 4], F32)
        ccs, B_ccs = T("ccs", [128, 8, 2], F32)
        modcol, B_modcol = T("modcol", [128, 48, 2], F32)
        gcol, B_gcol = T("gcol", [128, 4, 8], F32)
        gm1, B_gm1 = T("gm1", [128, 8], F32)
        cgm1, B_cgm1 = T("cgm1", [128, 8], F32)
        gm2, B_gm2 = T("gm2", [128, 8], F32)
        sh1, B_sh1 = T("sh1", [128, 8], F32)
        csh1, B_csh1 = T("csh1", [128, 8], F32)
        sh2, B_sh2 = T("sh2", [128, 8], F32)
        gate1, B_gate1 = T("gate1", [128, 1024], F32)
        gate2, B_gate2 = gate1, B_gate1
        xt = [T(f"xt{i}", [128, 1024], F32) for i in range(1)] * 2
        xn, B_xn = T("xn", [128, 1024], F32)
        modrow, B_modrow = xt[0][0][0:2, 0:512], xt[0][1]
        bmg, B_bmg = xt[0][0][0:2, 512:1024], xt[0][1]
        colt, B_colt = T("colt", [128, 8], F32)
        hT = [T(f"hT{i}", [128, 8, 512], BF16) for i in range(1)] * 2
        wk = sb("wk", [128, 11264], BF16)
        mixN = (wk[:, 0:2048].rearrange("p (k n) -> p k n", k=4), Buf())
        mixG1_t = sb("mixG1", [128, 4, 512], BF16)
        mixG = [(wk[:, 2048:4096].rearrange("p (k n) -> p k n", k=4), Buf()), (mixG1_t, Buf())]
        srT, B_srT = wk[:, 4096:6144].rearrange("p (k n) -> p k n", k=4), Buf()
        gvt, B_gvt = wk[:, 6144:8192].rearrange("p (k n) -> p k n", k=4), Buf()
        KcT, B_KcT = wk[:, 8192:9216].rearrange("p (k n) -> p k n", k=4), Buf()
        Vc = [(wk[:, 9216 + i * 1024: 9216 + (i + 1) * 1024].rearrange("p (h t d) -> p h t d", h=8, t=2), Buf()) for i in range(2)]
        B_wk = [mixN[1], mixG[0][1], B_srT, B_gvt, B_KcT, Vc[0][1], Vc[1][1]]
        wk2 = sb("wk2", [128, 13 * 512], BF16)
        _w2 = [(wk2[:, i * 512:(i + 1) * 512], Buf()) for i in range(13)]
        Pt = [_w2[0], _w2[1]] * 2
        graw = [_w2[2]] * 4
        grope = [_w2[3], _w2[4], _w2[5], _w2[6]]
        B_wk2 = [b for _, b in _w2]
        f1, B_f1 = T("f1", [128, 512], F32)
        f2, B_f2 = T("f2", [128, 512], F32)
        rd, B_rd = xn, B_xn
        E1, B_E1 = T("E1", [128, 512], F32)
        E2, B_E2 = E1, B_E1
        cosb, B_cosb = E1, B_E1
        sinb, B_sinb = f2, B_f2
        afT, B_afT = T("afT", [32, 512], BF16)
        kend = [_w2[7]] * 4
        dec, B_dec = T("dec", [128, 4, 4], F32)
        Sst = [T(f"Sst{i}", [128, 128], F32) for i in range(4)]
        SbfZ, B_SbfZ = _w2[8]
        Asb = [_w2[9], _w2[10]]
        sq, B_sq = Asb[0]

        def _n(ap):
            sh = ap.shape
            n = 1
            for v in sh[1:]:
                n *= int(v)
            return n

        def _ec(eng, n):
            return {"act": 220.0 + 0.75 * n, "dve": 70.0 + 1.25 * n, "pool": 150.0 + 1.9 * n}[eng]

        def dma(eng, out, in_, reads, writes, **kw):
            nbytes = _n(out) * int(out.shape[0]) * 4
            return P.add(eng, lambda e: e.dma_start(out=out, in_=in_, **kw), reads, writes, dma=True, lat=2000.0 + nbytes / 150.0)

        def mm(out, lhsT, rhs, start, stop, reads, writes):
            c = max(64, _n(rhs)) * 0.42 + 8.0
            if rhs.dtype == F32:
                c *= 4.0
            return P.add("pe", lambda e: e.matmul(out=out, lhsT=lhsT, rhs=rhs, start=start, stop=stop), reads, writes, cost=c)

        def tr(out, in_, ident, reads, writes):
            return P.add("pe", lambda e: e.transpose(out=out, in_=in_, identity=ident), reads, writes, cost=130.0)

        def act(out, in_, func, reads, writes, scale=1.0, bias=0.0):
            return P.add("act", lambda e: e.activation(out=out, in_=in_, func=func, bias=bias, scale=scale), reads, writes,
                         cost=_ec("act", _n(out)))

        def tt(eng, out, in0, in1, op, reads, writes):
            return P.add(eng, lambda e: e.tensor_tensor(out=out, in0=in0, in1=in1, op=op), reads, writes, cost=_ec(eng, _n(out)))

        def ts(eng, out, in0, s1, s2, op0, op1, reads, writes):
            return P.add(eng, lambda e: e.tensor_scalar(out=out, in0=in0, scalar1=s1, scalar2=s2, op0=op0, op1=op1), reads, writes,
                         cost=_ec(eng, _n(out)))

        def stt(out, in0, scalar, in1, op0, op1, reads, writes):
            return P.add("dve", lambda e: e.scalar_tensor_tensor(out=out, in0=in0, scalar=scalar, in1=in1, op0=op0, op1=op1), reads, writes,
                         cost=_ec("dve", _n(out)))

        def cp(eng, out, in_, reads, writes):
            if eng == "act":
                return P.add("act", lambda e: e.copy(out=out, in_=in_), reads, writes, cost=_ec("act", _n(out)))
            return P.add(eng, lambda e: e.tensor_copy(out=out, in_=in_), reads, writes, cost=_ec(eng, _n(out)))

        def memset(eng, ap, val, writes):
            return P.add(eng, lambda e: e.memset(ap, val), (), writes, cost=_ec(eng, _n(ap)) * 0.5)

        def rstd_col(out_col, ss_col, n, reads_b, tmp_col, tmpB, outB):
            act(tmp_col, ss_col, AF.Ln, reads_b + [tmpB], [tmpB], scale=1.0 / n, bias=EPS)
            act(out_col, tmp_col, AF.Exp, [tmpB], [outB], scale=-0.5)

        dma("sp", identf[:], idf_d[:, :], [], [B_identf])
        dma("pool", identb[:], idf_d[:, :], [], [B_identb])
        dma("pool", pm[:], pm_d[:, :], [], [B_pm])
        memset("pool", onesb[:], 1.0, [B_onesb])
        dma("pool", tri[:, 0, :], tri_d[0], [], [B_tri])
        dma("pool", tri[:, 1, :], tri_d[1], [], [B_tri])
        dma("pool", smask[:], smask_d[:, :], [], [B_smask])
        dma("sp", sel[:], sel_d[:, :], [], [B_sel])
        dma("pool", wa2[:], wa2_d[:, :], [], [B_wa2])
        dma("sp", negba[:], ba_d[:, :], [], [B_negba])
        ts("dve", negba[:], negba[:], -1.0, None, ALU.mult, ALU.bypass, [B_negba], [B_negba])
        dma("sp", ggain[:], gg_d[:, :], [], [B_ggain])
        dma("sp", ccs[:], cc_d.rearrange("(k p) j -> p k j", p=128), [], [B_ccs], allow_slow_non_contiguous=True)
        dma("sp", gcol[:], g4_d.rearrange("a (k p) -> p a k", p=128), [], [B_gcol], allow_slow_non_contiguous=True)
        act(ccs[:], ccs[:], AF.Silu, [B_ccs], [B_ccs])
        for v_, b_ in Vc:
            memset("pool", v_[:], 1.0, [b_])

        for k in range(8):
            for hf in range(2):
                dma("pool", win[:, k, hf * 1552:(hf + 1) * 1552], win_d[k * 128:(k + 1) * 128, hf * 1552:(hf + 1) * 1552],
                    [], [B_win[k]] if hf == 0 else [B_win[k]])
        for k in range(8):
            dma("pool", wout[:, k, :], wout_d[k * 128:(k + 1) * 128, :], [], [B_wout[k]])

        B_modc_ps = B_ps[7]
        modc_ps = ps[7]
        for g in range(12):
            wm, bwm = wmring[g % 2], B_wm[g % 2]
            dma("sp", wm, wmod_d[:, g * 512:(g + 1) * 512].rearrange("(k p) n -> p k n", p=128), [], [bwm])
            dma("sp", bmg, bmod_d[0:1, g * 512:(g + 1) * 512].partition_broadcast(2), [], [B_bmg])
            for k in range(8):
                mm(ps[0][0:2, :], ccs[:, k, :], wm[:, k, :], k == 0, k == 7, [B_ccs, bwm], [B_ps[0]])
            tt("dve", modrow, ps[0][0:2, :], bmg, ALU.add, [B_ps[0], B_bmg], [B_modrow])
            for s in range(4):
                ci = g * 4 + s
                mm(modc_ps[:, ci * 2:ci * 2 + 2], modrow[:, s * 128:(s + 1) * 128], identf[0:2, 0:2], True, True,
                   [B_modrow, B_identf], [B_modc_ps])
            if g in (4, 5, 10, 11):
                hf = g % 2
                gi = 1 if g < 8 else 3
                mm(ps[1][:, :], sel[0:2, :], modrow, True, True, [B_sel, B_modrow], [B_ps[1]])
                dma("sp", xn[:, 0:512], g4_d[gi:gi + 1, hf * 512:(hf + 1) * 512].partition_broadcast(128), [], [B_xn])
                if g < 8:
                    tt("dve", gate1[:, hf * 512:(hf + 1) * 512], ps[1][:, :], xn[:, 0:512], ALU.mult, [B_ps[1], B_xn], [B_gate1])
                else:
                    tt("dve", xn[:, 512:1024], ps[1][:, :], xn[:, 0:512], ALU.mult, [B_ps[1], B_xn], [B_xn])
                    dma("sp", gscr_d[:, hf * 512:(hf + 1) * 512], xn[:, 512:1024], [B_xn], [B_gscr])
        cp("dve", modcol[:].rearrange("p a b -> p (a b)"), modc_ps[:, 0:96], [B_modc_ps], [B_modcol])
        cp("dve", sh1[:], modcol[:, 0:8, 0], [B_modcol], [B_sh1])
        cp("dve", csh1[:], modcol[:, 0:8, 1], [B_modcol], [B_csh1])
        cp("dve", sh2[:], modcol[:, 24:32, 0], [B_modcol], [B_sh2])
        stt(gm1[:], modcol[:, 8:16, 0], 1.0, gcol[:, 0, :], ALU.add, ALU.mult, [B_modcol, B_gcol], [B_gm1])
        stt(cgm1[:], modcol[:, 8:16, 1], 1.0, gcol[:, 0, :], ALU.add, ALU.mult, [B_modcol, B_gcol], [B_cgm1])
        stt(gm2[:], modcol[:, 32:40, 0], 1.0, gcol[:, 2, :], ALU.add, ALU.mult, [B_modcol, B_gcol], [B_gm2])

        for h in range(8):
            stg = wmring[h % 2].rearrange("p k n -> p (k n)")[:, 0:2048]
            dma("sp", stg, et_d[h], [], [B_wm[h % 2]])
            act(ETR[h % 2], stg, AF.Exp, [B_wm[h % 2]], [B_ETR[h % 2]])
            dma("sp", etscr_d[h], ETR[h % 2], [B_ETR[h % 2]], [B_etscr[h]])

        for dj_ in range(4):
            memset("pool", kinv[dj_][0], 0.0, [kinv[dj_][1]])
        memset("pool", SbfZ[:], 0.0, [B_SbfZ])
        for i_ in range(12):
            P.add("pool", lambda e, i_=i_: e.memset(VR[i_][:, :, 1, :], 1.0), [], [B_VR[i_]] + B_wm + B_KTR + B_QTR)

        xcnt = [0]

        def norm_transpose(src_rows_ap, gm, bgm, shc, bsh, dstT, bdst, col0):
            xtile, bx = xt[xcnt[0] % 2]
            xcnt[0] += 1
            dma("sp", xtile[:], src_rows_ap, [], [bx])
            act(xn[:], xtile[:], AF.Square, [bx], [B_xn])
            P.add("dve", lambda e: e.tensor_reduce(out=colt[:, 0:1], in_=xn[:], axis=AX.X, op=ALU.add), [B_xn], [B_colt], cost=1200.0)
            rstd_col(colt[:, 2:3], colt[:, 0:1], float(D), [B_colt], colt[:, 1:2], B_colt, B_colt)
            ts("pool", xn[:], xtile[:], colt[:, 2:3], 0.0, ALU.mult, ALU.add, [bx, B_colt], [B_xn])
            for half in range(2):
                pb, bpb = ps[2 + half], B_ps[2 + half]
                for kk in range(4):
                    k = half * 4 + kk
                    tr(pb[:, kk * 128:(kk + 1) * 128], xn[:, k * 128:(k + 1) * 128], identf[:], [B_xn, B_identf], [bpb])
                for kk in range(4):
                    k = half * 4 + kk
                    if kk % 2 == 0:
                        act(dstT[:, k, col0:col0 + 128], pb[:, kk * 128:(kk + 1) * 128], AF.Identity,
                            [bpb, bgm, bsh], [bdst], scale=gm[:, k:k + 1], bias=shc[:, k:k + 1])
                    else:
                        ts("dve", dstT[:, k, col0:col0 + 128], pb[:, kk * 128:(kk + 1) * 128], gm[:, k:k + 1], shc[:, k:k + 1],
                           ALU.mult, ALU.add, [bpb, bgm, bsh], [bdst])
            return xtile, bx

        pcnt = [0]

        def proj_fm(hTt, bh, col, M, N):
            pb, bpb = ps[pcnt[0] % 2], B_ps[pcnt[0] % 2]
            pcnt[0] += 1
            for k in range(8):
                mm(pb[0:M, 0:N], win[:, k, col:col + M], hTt[:, k, 0:N], k == 0, k == 7, [B_win[k], bh], [bpb])
            return pb, bpb

        def proj_tm(hTt, bh, tcol, col):
            pb, bpb = ps[pcnt[0] % 2], B_ps[pcnt[0] % 2]
            pcnt[0] += 1
            for k in range(8):
                mm(pb[:, :], hTt[:, k, tcol:tcol + 128], win[:, k, col:col + 512], k == 0, k == 7, [B_win[k], bh], [bpb])
            return pb, bpb

        def gla_gates(dj, N, backward, afsrc=None):
            af_, baf_ = afsrc if afsrc is not None else (afT, B_afT)
            pz, bpz = ps[7], B_ps[7]
            mm(pz[:, 0:N], wa2[:, dj * 128:(dj + 1) * 128], af_[:, 0:N], True, True, [B_wa2, baf_], [bpz])
            act(f1[:, 0:N], pz[:, 0:N], AF.Exp, [bpz, B_negba], [B_f1], scale=-1.0, bias=negba[:, dj:dj + 1])
            act(f1[:, 0:N], f1[:, 0:N], AF.Ln, [B_f1], [B_f1], scale=1.0, bias=1.0)
            P.add("dve", lambda e: e.tensor_tensor_scan(out=f2[:, 0:N], data0=smask[:, 0:N], data1=f1[:, 0:N], initial=0.0,
                                                         op0=ALU.mult, op1=ALU.add), [B_smask, B_f1], [B_f2], cost=70.0 + 2.35 * N)
            X, BX = f2, B_f2
            if backward:
                tt("pool", f1[:, 0:N], f1[:, 0:N], f2[:, 0:N], ALU.subtract, [B_f1, B_f2], [B_f1])
                for c in range(N // 128):
                    ts("pool", f1[:, c * 128:(c + 1) * 128], f1[:, c * 128:(c + 1) * 128], f2[:, c * 128 + 127:c * 128 + 128], 1.0,
                       ALU.add, ALU.mult, [B_f1, B_f2], [B_f1])
                X, BX = f1, B_f1
            act(E1[:, 0:N], X[:, 0:N], AF.Exp, [BX], [B_E1], scale=-1.0 / 16.0)
            for c in range(N // 128):
                col = c * 128 if backward else c * 128 + 127
                cp("dve", dec[:, dj, c:c + 1], E1[:, col:col + 1], [B_E1], [B_dec])
            return X, BX

        def gla_k(dj, N, ksrc, bk, want_kinv, kend_too=True, X=None, BX=None):
            act(E2[:, 0:N], X[:, 0:N], AF.Exp, [BX], [B_E2], scale=1.0 / 16.0)
            if want_kinv:
                for hh in range(2):
                    tt("dve", kinv[dj][0][hh * 64:(hh + 1) * 64, hh, 0:N], ksrc[hh * 64:(hh + 1) * 64, 0:N], E2[hh * 64:(hh + 1) * 64, 0:N],
                       ALU.mult, [bk, B_E2], [kinv[dj][1]])
            if not kend_too:
                return
            for c in range(N // 128):
                stt(kend[dj][0][:, c * 128:(c + 1) * 128], ksrc[:, c * 128:(c + 1) * 128], dec[:, dj, c:c + 1],
                    E2[:, c * 128:(c + 1) * 128], ALU.mult, ALU.mult, [bk, B_dec, B_E2], [kend[dj][1]])
            gla_kendT1(dj, N)

        def gla_kendT(djs, N):
            pass

        def gla_kendT1(dj, N):
            pb, bpb = ps[5], B_ps[5]
            pbb = pb[:, :].bitcast(BF16)
            for c in range(N // 128):
                tr(pbb[:, c * 128:(c + 1) * 128], kend[dj][0][:, c * 128:(c + 1) * 128], identb[:], [kend[dj][1], B_identb], [bpb])
            for c in range(N // 128):
                cp("act" if c % 2 == 0 else "dve", kendT[:, c, dj * 128:(dj + 1) * 128], pbb[:, c * 128:(c + 1) * 128], [bpb], [B_kendT])

        def gla_kv(c, djs, vtile, bv):
            pb, bpb = ps[7], B_ps[7]
            for dj in djs:
                pair = dj % 2
                for hh in range(2):
                    head = pair * 2 + hh
                    mm(pb[hh * 64:(hh + 1) * 64, dj * 128:(dj + 1) * 128], kendT[:, c, dj * 128 + hh * 64: dj * 128 + hh * 64 + 64],
                       vtile[:, head * 128:(head + 1) * 128], True, True, [B_kendT, bv], [bpb])
            return pb, bpb

        def scan_update(dj, c, pkv, bpkv):
            s_, bs = Sst[dj]
            stt(s_[:], s_[:], dec[:, dj, c:c + 1], pkv[:, dj * 128:(dj + 1) * 128], ALU.mult, ALU.add, [bs, B_dec, bpkv], [bs])

        B_out = [Buf() for _ in range(NT)]
        if kstop <= 1:
            P.add("sp", lambda e: e.nop(), B_out, [])
            P.emit(nc, st)
            return nc
        hcT, B_hcT = hT[1]
        for tt_ in range(2):
            norm_transpose(ctx_d[tt_ * 128:(tt_ + 1) * 128, :], cgm1, B_cgm1, csh1, B_csh1, hcT, B_hcT, tt_ * 128)
        for j in range(4):
            pb, bpb = proj_fm(hcT, B_hcT, 512 + j * 128, 128, 256)
            cp("act", KcT[:, j, :], pb[:, 0:256], [bpb], [B_KcT])
        for tt_ in range(2):
            pb, bpb = proj_tm(hcT, B_hcT, tt_ * 128, 1024)
            cp("act", Vc[tt_][0][:, :, 0, :], pb[:, :].rearrange("p (h d) -> p h d", h=8), [bpb], [Vc[tt_][1]])
            pb, bpb = proj_tm(hcT, B_hcT, tt_ * 128, 2048)
            cp("dve", gvt[:, tt_, :], pb[:, :], [bpb], [B_gvt])
        for j in range(2):
            pb, bpb = proj_fm(hcT, B_hcT, 1792 + j * 128, 128, 256)
            cp("act", grope[2 + j][0][:, 0:256], pb[:, 0:256], [bpb], [grope[2 + j][1]])
        pb, bpb = proj_fm(hcT, B_hcT, 3072, 32, 256)
        cp("act", afT[:, 0:256], pb[0:32, 0:256], [bpb], [B_afT])
        for dj in range(4):
            memset("pool", Sst[dj][0][:], 0.0, [Sst[dj][1]])
        for dj in range(4):
            X_, BX_ = gla_gates(dj, 256, dj >= 2)
            gla_k(dj, 256, grope[2 + dj % 2][0], grope[2 + dj % 2][1], False, X=X_, BX=BX_)
        gla_kendT([0, 1, 2, 3], 256)
        for c in range(2):
            pkv, bpkv = gla_kv(c, [0, 1], gvt[:, c, :], B_gvt)
            for dj in (0, 1):
                scan_update(dj, c, pkv, bpkv)
        for c in (1, 0):
            pkv, bpkv = gla_kv(c, [2, 3], gvt[:, c, :], B_gvt)
            for dj in (2, 3):
                scan_update(dj, c, pkv, bpkv)

        def run_interleaved(ga, gb, ratio):
            a_live, b_live = ga is not None, gb is not None
            while a_live or b_live:
                if a_live:
                    for _ in range(ratio):
                        try:
                            next(ga)
                        except StopIteration:
                            a_live = False
                            break
                if b_live:
                    try:
                        next(gb)
                    except StopIteration:
                        b_live = False

        def block_front(i, slot, full, bset=None, parts=("norm", "proj", "rope")):
            gvt_, bgvt_, afT_, bafT_, gro_ = bset if bset is not None else (gvt, B_gvt, afT, B_afT, grope)
            hTt, bh = hT[slot]
            if "norm" in parts:
                for t4 in range(4):
                    r0 = i * 512 + t4 * 128
                    norm_transpose(x_d[r0:r0 + 128, :], gm1, B_gm1, sh1, B_sh1, hTt, bh, t4 * 128)
                    yield
            if "proj" in parts:
                yield from _bf_proj(i, full, hTt, bh, gvt_, bgvt_, afT_, bafT_)
            if "rope" in parts:
                yield from _bf_rope(i, full, hTt, bh, gro_)

        def _bf_proj(i, full, hTt, bh, gvt_, bgvt_, afT_, bafT_):
            if full:
                qt, bqt = QTR[i % 2], B_QTR[i % 2]
                kt, bkt = KTR[i % 3], B_KTR[i % 3]
                for j in range(4):
                    pb, bpb = proj_fm(hTt, bh, j * 128, 128, 512)
                    act(qt[:, j, :], pb[:, :], AF.Identity, [bpb], [bqt], scale=0.125)
                for j in range(4):
                    pb, bpb = proj_fm(hTt, bh, 512 + j * 128, 128, 512)
                    cp("dve", kt[:, j, :], pb[:, :], [bpb], [bkt])
                for j in range(4):
                    pb, bpb = proj_fm(hTt, bh, 2560 + j * 128, 128, 512)
                    act(srT[:, j, :], pb[:, :], AF.Silu, [bpb], [B_srT])
                for t4 in range(4):
                    vi = (i * 4 + t4) % 12
                    pb, bpb = proj_tm(hTt, bh, t4 * 128, 1024)
                    cp("act", VR[vi][:, :, 0, :], pb[:, :].rearrange("p (h d) -> p h d", h=8), [bpb], [B_VR[vi]])
            yield
            for t4 in range(4):
                pb, bpb = proj_tm(hTt, bh, t4 * 128, 2048)
                cp("dve", gvt_[:, t4, :], pb[:, :], [bpb], [bgvt_])
                yield
            pb, bpb = proj_fm(hTt, bh, 3072, 32, 512)
            cp("act", afT_[:, :], pb[0:32, :], [bpb], [bafT_])
            yield

        def _bf_rope(i, full, hTt, bh, gro_):
            dma("sp", cosb[:], cos_d[:, i * 512:(i + 1) * 512], [], [B_cosb])
            for idx in ([0, 1, 2, 3] if full else [2, 3]):
                col = (1536 if idx < 2 else 1792) + (idx % 2) * 128
                pb, bpb = proj_fm(hTt, bh, col, 128, 512)
                rw, brw = graw[idx]
                cp("act", rw[:], pb[:, :], [bpb], [brw])
                pr, bpr = ps[7], B_ps[7]
                mm(pr[:, :], pm[:], rw[:], True, True, [B_pm, brw], [bpr])
                dma("sp", sinb[:], sin_d[:, i * 512:(i + 1) * 512], [], [B_sinb])
                tt("pool", f1[:], rw[:], cosb[:], ALU.mult, [brw, B_cosb], [B_f1])
                tt("dve", f2[:], pr[:, :], sinb[:], ALU.mult, [bpr, B_sinb], [B_f2])
                tt("pool", gro_[idx][0][:], f1[:], f2[:], ALU.add, [B_f1, B_f2], [gro_[idx][1]])
                yield

        if kstop <= 2:
            P.add("sp", lambda e: e.nop(), B_out, [])
            P.emit(nc, st)
            return nc
        sblr = [_w2[11], _w2[12]]
        stg, B_stg = sblr[0]
        memset("pool", stg[:], 0.0, [B_stg])
        B_scrch = [Buf() for _ in range(NT)]
        mg1, bmg1 = mixG[1]
        set0 = (gvt, B_gvt, afT, B_afT, grope)
        gro1 = [None, None, (mg1[:, 1, :], bmg1), (mg1[:, 2, :], bmg1)]
        set1 = (srT, B_srT, mg1[0:32, 0, :], bmg1, gro1)
        psets = [set0, set1]

        def pre_b(i):
            gvt_, bgvt_, afT_, bafT_, gro_ = psets[i % 2]
            for dj in (2, 3):
                X_, BX_ = gla_gates(dj, 512, True, afsrc=(afT_, bafT_))
                yield
                gla_k(dj, 512, gro_[2 + dj % 2][0], gro_[2 + dj % 2][1], False, X=X_, BX=BX_)
                yield
            for c in (3, 2, 1, 0):
                ch = i * 4 + c
                for dj in (2, 3):
                    for hh in range(2):
                        head = (dj - 2) * 2 + hh
                        cp("act", stg[hh * 64:(hh + 1) * 64, head * 128:(head + 1) * 128], Sst[dj][0][hh * 64:(hh + 1) * 64, :],
                           [Sst[dj][1]], [B_stg])
                dma("sp", sbscr_d[ch], stg[:], [B_stg], [B_scrch[ch]])
                pkv, bpkv = gla_kv(c, [2, 3], gvt_[:, c, :], bgvt_)
                for dj in (2, 3):
                    scan_update(dj, c, pkv, bpkv)
                yield

        for _ in block_front(NB - 1, 0, False, psets[(NB - 1) % 2]):
            pass
        for i in range(NB - 1, -1, -1):
            ga = pre_b(i)
            gb = block_front(i - 1, 0, False, psets[(i - 1) % 2]) if i >= 1 else None
            run_interleaved(ga, gb, 1)
        if kstop <= 3:
            P.add("sp", lambda e: e.nop(), B_out, [])
            P.emit(nc, st)
            return nc

        def gla_main(i, slot):
            for dj in range(4):
                X_, BX_ = gla_gates(dj, 512, dj >= 2)
                pair = dj % 2
                stt(qdec[dj][0][:], grope[pair][0][:], 0.125, E1[:], ALU.mult, ALU.mult, [grope[pair][1], B_E1], [qdec[dj][1]])
                yield
                gla_k(dj, 512, grope[2 + pair][0], grope[2 + pair][1], True, kend_too=(dj < 2), X=X_, BX=BX_)
                yield
            gla_kendT([0, 1], 512)
            mx, bmx = mixG[slot]
            for c in range(4):
                ch = i * 4 + c
                sbl, B_sbl = sblr[ch % 2]
                dma("sp", sbl[:], sbscr_d[ch], [B_scrch[ch]], [B_sbl])
                for d in range(2):
                    pa, bpa = ps[(5, 7)[d]], B_ps[(5, 7)[d]]
                    for head in range(4):
                        pair, hh = head // 2, head % 2
                        dj = d * 2 + pair
                        mm(pa[:, head * 128:(head + 1) * 128], kinv[dj][0][:, hh, c * 128:(c + 1) * 128],
                           qdec[dj][0][:, c * 128:(c + 1) * 128], True, True, [kinv[dj][1], qdec[dj][1]], [bpa])
                    tt("dve", Asb[d][0][:], pa[:, :], tri[:, d, :], ALU.mult, [bpa, B_tri], [Asb[d][1]])
                    yield
                for dj in (0, 1):
                    for hh in range(2):
                        head = dj * 2 + hh
                        cp("act", SbfZ[hh * 64:(hh + 1) * 64, head * 128:(head + 1) * 128], Sst[dj][0][hh * 64:(hh + 1) * 64, :],
                           [Sst[dj][1]], [B_SbfZ])
                po, bpo = ps[6], B_ps[6]
                for head in range(4):
                    pair, hh = head // 2, head % 2
                    osl = po[:, head * 128:(head + 1) * 128]
                    mm(osl, gvt[:, c, head * 128:(head + 1) * 128], Asb[0][0][:, head * 128:(head + 1) * 128], True, False, [B_gvt, Asb[0][1]], [bpo])
                    mm(osl, gvt[:, c, head * 128:(head + 1) * 128], Asb[1][0][:, head * 128:(head + 1) * 128], False, False, [B_gvt, Asb[1][1]], [bpo])
                    mm(osl, SbfZ[:, head * 128:(head + 1) * 128], qdec[pair][0][:, c * 128:(c + 1) * 128], False, False,
                       [B_SbfZ, qdec[pair][1]], [bpo])
                    mm(osl, sbl[:, head * 128:(head + 1) * 128], qdec[2 + pair][0][:, c * 128:(c + 1) * 128],
                       False, True, [B_sbl, qdec[2 + pair][1]], [bpo])
                yield
                act(sq[:], po[:, :], AF.Square, [bpo], [B_sq])
                pn, bpn = ps[5], B_ps[5]
                mm(pn[:, :], onesb[:], sq[:], True, True, [B_onesb, B_sq], [bpn])
                act(f1[:], pn[:, :], AF.Ln, [bpn], [B_f1], scale=1.0 / 128.0, bias=EPS)
                act(f1[:], f1[:], AF.Exp, [B_f1], [B_f1], scale=-0.5)
                tt("dve", f2[:], po[:, :], f1[:], ALU.mult, [bpo, B_f1], [B_f2])
                for head in range(4):
                    stt(mx[:, head, c * 128:(c + 1) * 128], f2[:, head * 128:(head + 1) * 128], ggain[:, head:head + 1],
                        srT[:, head, c * 128:(c + 1) * 128], ALU.mult, ALU.mult, [B_f2, B_ggain, B_srT], [bmx])
                yield
                pkv, bpkv = gla_kv(c, [0, 1], gvt[:, c, :], B_gvt)
                for dj in (0, 1):
                    scan_update(dj, c, pkv, bpkv)
                yield

        def na_block(m, slot):
            mx, bmx = mixN
            qt, bqt = QTR[m % 2], B_QTR[m % 2]
            plan = na_plan(ROWS, m)
            n_pt = 0
            for h in range(8):
                j, hh = h // 2, h % 2
                qh = qt[hh * 64:(hh + 1) * 64, j, :]
                po, bpo = ps[4], B_ps[4]
                etb, betb = ETR[h % 2], B_ETR[h % 2]
                dma("sp", etb, etscr_d[h], [B_etscr[h]], [betb])
                items = [("c", 0), ("c", 1)] + [("l", e) for e in plan]
                pend = None
                for n, (kind, e) in enumerate(items):
                    bi = (n % 2) if hh == 0 else (2 + n % 2)
                    psS, bpsS = ps[bi], B_ps[bi]
                    pt, bpt = Pt[n_pt % 2]
                    n_pt += 1
                    if kind == "c":
                        c0, c1 = 0, 512
                        kop = KcT[hh * 64:(hh + 1) * 64, j, e * 128:(e + 1) * 128]
                        bk = B_KcT
                        vt, bv = Vc[e]
                    else:
                        t, runs, rlo, rhi = e
                        c0, c1 = (rlo - 8 * m) * 64, (rhi - 8 * m + 1) * 64
                        blk, tin = t // 4, t % 4
                        kop = KTR[blk % 3][hh * 64:(hh + 1) * 64, j, tin * 128:(tin + 1) * 128]
                        bk = B_KTR[blk % 3]
                        vt, bv = VR[t % 12], B_VR[t % 12]
                    mm(psS[:, c0:c1], kop, qh[:, c0:c1], True, True, [bk, bqt], [bpsS])
                    act(pt[:, c0:c1], psS[:, c0:c1], AF.Exp, [bpsS], [bpt])
                    if kind == "l":
                        for ty, ra, rb in runs:
                            a0, a1 = (ra - 8 * m) * 64, (rb - 8 * m + 1) * 64
                            j0 = 7 - 2 * t + ra
                            tb = etb[:, ty * 1024 + j0 * 64: ty * 1024 + j0 * 64 + (a1 - a0)]
                            tt("pool" if n % 2 == 0 else "dve", pt[:, a0:a1], pt[:, a0:a1], tb, ALU.mult, [bpt, betb], [bpt])
                    if pend is not None:
                        pend()
                    def _pv(po=po, vt=vt, pt=pt, c0=c0, c1=c1, n=n, bv=bv, bpt=bpt, bpo=bpo, nitems=len(items), h=h):
                        mm(po[:, c0:c1], vt[:, h, :, :].rearrange("p t d -> p (t d)"), pt[:, c0:c1], n == 0, n == nitems - 1,
                           [bv, bpt], [bpo])
                    pend = _pv
                    yield
                pend()
                act(rd[64:128, 0:512], po[64:128, :], AF.Ln, [bpo], [B_rd])
                act(rd[64:128, 0:512], rd[64:128, 0:512], AF.Exp, [B_rd], [B_rd], scale=-1.0)
                tt("dve", mx[hh * 64:(hh + 1) * 64, j, :], po[0:64, :], rd[64:128, 0:512], ALU.mult, [bpo, B_rd], [bmx])
                yield

        def out_block(m, slot):
            mg, bmg_ = mixG[slot]
            mn, bmn = mixN
            for t4 in range(4):
                r0 = m * 512 + t4 * 128
                xtile, bx = xt[xcnt[0] % 2]
                xcnt[0] += 1
                dma("sp", xtile[:], x_d[r0:r0 + 128, :], [], [bx])
                epilogue(lambda k, hf: ((mn[:, k, t4 * 128:(t4 + 1) * 128] if k < 4 else mg[:, k - 4, t4 * 128:(t4 + 1) * 128]),
                                        wout[:, k, hf * 512:(hf + 1) * 512], [bmn if k < 4 else bmg_, B_wout[k]]), 8,
                         xtile, bx, gate1, B_gate1, r0, pbase=(0 if t4 % 2 == 0 else 2))
                yield

        def epilogue(opnd, nk, xtile, bx, gate, bgate, r0, pbase=0):
            for hf in range(2):
                pb, bpb = ps[pbase + hf], B_ps[pbase + hf]
                for k in range(nk):
                    l_, r_, rb_ = opnd(k, hf)
                    mm(pb[:, :], l_, r_, k == 0, k == nk - 1, rb_, [bpb])
                act(xn[:, hf * 512:(hf + 1) * 512], pb[:, :], AF.Square, [bpb], [B_xn])
            P.add("dve", lambda e: e.tensor_reduce(out=colt[:, 4:5], in_=xn[:], axis=AX.X, op=ALU.add), [B_xn], [B_colt], cost=1200.0)
            rstd_col(colt[:, 6:7], colt[:, 4:5], float(D), [B_colt], colt[:, 5:6], B_colt, B_colt)
            for hf in range(2):
                stt(xn[:, hf * 512:(hf + 1) * 512], ps[pbase + hf][:, :], colt[:, 6:7], gate[:, hf * 512:(hf + 1) * 512], ALU.mult, ALU.mult,
                    [B_ps[pbase + hf], B_colt, bgate], [B_xn])
            tt("pool", xtile[:], xtile[:], xn[:], ALU.add, [bx, B_xn], [bx])
            dma("sp", out_d[r0:r0 + 128, :], xtile[:], [bx], [B_out[r0 // 128]])

        def stream_a(m):
            yield from na_block(m, m % 2)
            yield from out_block(m, m % 2)

        import os as _os2
        RA_ = int(_os2.environ.get('K_RA', '3'))
        EC_ = int(_os2.environ.get('K_EC', '6'))

        def run3(ga, gb, gc, flag, ra, every_c):
            live = [ga is not None, gb is not None, gc is not None]
            rnd = 0
            while any(live):
                if live[0]:
                    for _ in range(ra):
                        try:
                            next(ga)
                        except StopIteration:
                            live[0] = False
                            break
                if live[1]:
                    try:
                        next(gb)
                    except StopIteration:
                        live[1] = False
                if live[2] and (flag[0] or not live[1]) and (rnd % every_c == 0 or not (live[0] or live[1])):
                    try:
                        next(gc)
                    except StopIteration:
                        live[2] = False
                rnd += 1

        for _ in block_front(0, 0, True, parts=("norm",)):
            pass
        for i in range(NB + 1):
            if i < NB:
                for _ in block_front(i, 0, True, parts=("proj",)):
                    pass
            flag = [False]
            ga = stream_a(i - 1) if i >= 1 else None
            gb = None
            if i < NB:
                def stream_b(i=i, flag=flag):
                    yield from block_front(i, 0, True, parts=("rope",))
                    flag[0] = True
                    yield from gla_main(i, i % 2)
                gb = stream_b()
            gc = block_front(i + 1, 0, True, parts=("norm",)) if i + 1 < NB else None
            run3(ga, gb, gc, flag, RA_, EC_)

        if kstop <= 4:
            P.add("sp", lambda e: e.nop(), B_out, [])
            P.emit(nc, st)
            return nc
        B_wguq = [[Buf() for _ in range(4)] for _ in range(8)]
        B_fenceF = Buf()
        P.add("pool", lambda e: e.nop(), [], [B_fenceF] + B_arenaM)
        for q in (0, 2, 1, 3):
            for k in range(8):
                dma("pool", wgu[:, k, q * 1408:(q + 1) * 1408], wgu_d[k * 128:(k + 1) * 128, q * 1408:(q + 1) * 1408],
                    [B_fenceF], [B_wguq[k][q]])
        for j in range(NJ):
            dma("pool", wdn[:, j, :], wdn_d[j * 128:(j + 1) * 128, :], [B_fenceF], [B_wdn[j]])
        aT, B_aT = wk[:, :].rearrange("p (j n) -> p j n", j=NJ), Buf()
        h2b, B_h2b = wk2[:, 0:4096].rearrange("p (k n) -> p k n", k=8), Buf()
        xt2, B_xt2 = wk2[:, 4096:6144].bitcast(F32), Buf()
        P.add("dve", lambda e: e.memset(aT[:, 0, 0:2], 0.0), [], [B_aT, B_h2b, B_xt2] + B_wk + B_wk2)
        dma("sp", gate2[:], gscr_d[:, :], [B_gscr], [B_gate2])
        H2 = [hT[0], (h2b, B_h2b)]
        B_colt_n = Buf()

        def f_norm(i):
            h2, bh2 = H2[i % 2]
            for t4 in range(4):
                r0 = i * 512 + t4 * 128
                dma("sp", xt2, out_d[r0:r0 + 128, :], [B_out[r0 // 128]], [B_xt2])
                act(xn[:], xt2, AF.Square, [B_xt2], [B_xn])
                P.add("dve", lambda e: e.tensor_reduce(out=colt[:, 0:1], in_=xn[:], axis=AX.X, op=ALU.add), [B_xn], [B_colt_n], cost=1200.0)
                rstd_col(colt[:, 2:3], colt[:, 0:1], float(D), [B_colt_n], colt[:, 1:2], B_colt_n, B_colt_n)
                ts("dve", xt2, xt2, colt[:, 2:3], None, ALU.mult, ALU.bypass, [B_xt2, B_colt_n], [B_xt2])
                yield
                for half in range(2):
                    pb, bpb = ps[2 + half], B_ps[2 + half]
                    for kk in range(4):
                        k = half * 4 + kk
                        tr(pb[:, kk * 128:(kk + 1) * 128], xt2[:, k * 128:(k + 1) * 128], identf[:], [B_xt2, B_identf], [bpb])
                    for kk in range(4):
                        k = half * 4 + kk
                        if kk % 2 == 0:
                            act(h2[:, k, t4 * 128:(t4 + 1) * 128], pb[:, kk * 128:(kk + 1) * 128], AF.Identity,
                                [bpb, B_gm2, B_sh2], [bh2], scale=gm2[:, k:k + 1], bias=sh2[:, k:k + 1])
                        else:
                            ts("dve", h2[:, k, t4 * 128:(t4 + 1) * 128], pb[:, kk * 128:(kk + 1) * 128], gm2[:, k:k + 1], sh2[:, k:k + 1],
                               ALU.mult, ALU.add, [bpb, B_gm2, B_sh2], [bh2])
                    yield

        def f_mm(i):
            h2, bh2 = H2[i % 2]
            for j in range(NJ):
                pg, bpg = ps[4 + (j % 2) * 2], B_ps[4 + (j % 2) * 2]
                pu, bpu = ps[5 + (j % 2) * 2], B_ps[5 + (j % 2) * 2]
                for k in range(8):
                    mm(pg[:, :], wgu[:, k, j * 128:(j + 1) * 128], h2[:, k, :], k == 0, k == 7, [B_wguq[k][(j * 128) // 1408], bh2], [bpg])
                for k in range(8):
                    mm(pu[:, :], wgu[:, k, FFN + j * 128:FFN + (j + 1) * 128], h2[:, k, :], k == 0, k == 7,
                       [B_wguq[k][(FFN + j * 128) // 1408], bh2], [bpu])
                fb, bfb = (f1, B_f1) if j % 2 == 0 else (f2, B_f2)
                act(fb[:], pg[:, :], AF.Silu, [bpg], [bfb])
                tt("dve", aT[:, j, :], pu[:, :], fb[:], ALU.mult, [bpu, bfb], [B_aT])
                yield
            for t4 in range(4):
                r0 = i * 512 + t4 * 128
                xtile, bx = xt[0]
                dma("sp", xtile[:], out_d[r0:r0 + 128, :], [B_out[r0 // 128]], [bx])
                epilogue(lambda k, hf: (aT[:, k, t4 * 128:(t4 + 1) * 128], wdn[:, k, hf * 512:(hf + 1) * 512], [B_aT, B_wdn[k]]), NJ,
                         xtile, bx, gate2, B_gate2, r0, pbase=(0 if t4 % 2 == 0 else 2))
                yield

        for _ in f_norm(0):
            pass
        for i in range(NB):
            run_interleaved(f_mm(i), f_norm(i + 1) if i + 1 < NB else None, 2)

        P.add("sp", lambda e: e.nop(), B_out, [])
        print("sbuf bytes remaining:", nc.sbuf_bytes_remaining)
        P.emit(nc, st)
    return nc


def host_inputs(L, x, c, ctx, c_ctx, w_mod, b_mod, norm_pre_mix, norm_post_mix, norm_pre_ffn, norm_post_ffn, w_in, na_rpb,
                gla_wa2_f, gla_ba_f, gla_wa2_b, gla_ba_b, gla_norm, w_out, w_gate_up, w_down):
    f = lambda a: np.ascontiguousarray(np.asarray(a, dtype=np.float32))
    B = x.shape[0]
    cosT, sinT, pm = _rope_tables(L)
    wa2blk = np.zeros((32, 512), np.float32)
    wa2blk[0:16, 0:256] = f(gla_wa2_f)[0]
    wa2blk[16:32, 256:512] = f(gla_wa2_b)[0]
    ba = np.concatenate([f(gla_ba_f)[0], f(gla_ba_b)[0]])
    bacol = np.ascontiguousarray(ba.reshape(4, 128).T)
    ggain = np.ascontiguousarray(f(gla_norm)[0].reshape(4, 128).T)
    tri = np.zeros((2, 128, 512), np.float32)
    s = np.arange(128)[:, None]
    t = np.arange(128)[None, :]
    tri[0] = np.tile((s <= t).astype(np.float32), (1, 4))
    tri[1] = np.tile((s >= t).astype(np.float32), (1, 4))
    smask = np.ones((128, 512), np.float32)
    smask[:, 0::128] = 0.0
    sel = np.zeros((2, 128), np.float32)
    sel[0, :] = 1.0
    common = {
        "w_mod": f(w_mod)[0], "b_mod": f(b_mod)[0][None, :],
        "g4": np.stack([f(norm_pre_mix)[0], f(norm_post_mix)[0], f(norm_pre_ffn)[0], f(norm_post_ffn)[0]]),
        "w_in": f(w_in)[0], "ettab": _et_tables(f(na_rpb)[0]), "wa2blk": wa2blk, "bacol": bacol, "glagain": ggain,
        "cosT": cosT, "sinT": sinT, "pm": pm, "identf": np.eye(128, dtype=np.float32), "tri4": tri, "scanmask": smask,
        "sel": sel, "w_out": f(w_out)[0], "w_gu": f(w_gate_up)[0], "w_down": f(w_down)[0],
    }
    maps = []
    for b in range(B):
        m = dict(common)
        m["x"] = f(x[b])
        m["ctx"] = f(ctx[b])
        m["cc"] = np.ascontiguousarray(np.stack([f(c[b]), f(c_ctx)], axis=1))
        maps.append(m)
    return maps


def kernel(**inputs):
    x = np.asarray(inputs["x"])
    L = x.shape[1]
    maps = host_inputs(L, **inputs)
    nc = build(L)
    res = run_bass_kernel_spmd(nc, maps, core_ids=list(range(len(maps))))
    return np.stack([np.asarray(r["out"], dtype=np.float32) for r in res.results], axis=0)
```

```python
import numpy as np
import ml_dtypes
from contextlib import ExitStack
import concourse.bass as bass
import concourse.mybir as mybir
from concourse.bass_utils import run_bass_kernel_spmd

F32 = mybir.dt.float32
BF16 = mybir.dt.bfloat16
AF = mybir.ActivationFunctionType
ALU = mybir.AluOpType
AX = mybir.AxisListType

D = 1024
NKC = 8
CTX = 256
FFN = 2816
NJ = FFN // 128
EPS = 1e-6
NEG = -30000.0


class Buf:
    __slots__ = ("name", "w", "rs")

    def __init__(self, name=""):
        self.name = name
        self.w = None
        self.rs = []


class Op:
    __slots__ = ("eng", "fn", "deps", "seq", "sig", "semval", "is_dma", "sem", "waits", "idx", "hz", "cost", "lat", "succ", "nin", "rt", "fin")


class Prog:
    ENGS = ["pe", "act", "dve", "pool", "sp"]

    def __init__(self, ndma_sems=14):
        self.ops = []
        self.eng_ops = {e: [] for e in self.ENGS}
        self.dma_count = {e: 0 for e in self.ENGS}
        self.dma_hist = {e: [] for e in self.ENGS}
        self.ndma = ndma_sems
        self.do_sched = True

    def add(self, eng, fn, reads=(), writes=(), dma=False, cost=300.0, lat=0.0):
        op = Op()
        op.eng = eng
        op.fn = fn
        op.is_dma = dma
        op.sig = False
        op.idx = len(self.ops)
        op.cost = cost
        op.lat = lat
        deps = []
        hz = []
        for b in reads:
            w = b.w
            if w is not None:
                hz.append(w)
                if w.is_dma or dma or w.eng != eng or eng != "pe":
                    deps.append(w)
        for b in writes:
            w = b.w
            if w is not None:
                hz.append(w)
                if w.is_dma or dma or w.eng != eng or eng != "pe":
                    deps.append(w)
            for r in b.rs:
                hz.append(r)
                if r.is_dma or dma or r.eng != eng or eng != "pe":
                    deps.append(r)
        op.deps = deps
        op.hz = hz
        for b in reads:
            b.rs.append(op)
        for b in writes:
            b.w = op
            b.rs = []
        self.ops.append(op)
        self.eng_ops[eng].append(op)
        return op

    def schedule(self, window=40, xlat=150.0):
        ops = self.ops
        for op in ops:
            op.succ = []
            op.rt = 0.0
        for op in ops:
            seen = set()
            n = 0
            for d in op.hz:
                if d is op or d.idx in seen:
                    continue
                seen.add(d.idx)
                d.succ.append(op)
                n += 1
            op.nin = n
        pend = {e: list(self.eng_ops[e]) for e in self.ENGS}
        head = {e: 0 for e in self.ENGS}
        done = [False] * len(ops)
        tcl = {e: 0.0 for e in self.ENGS}
        order = {e: [] for e in self.ENGS}
        remaining = len(ops)
        while remaining:
            best = None
            for e in self.ENGS:
                lst = pend[e]
                h = head[e]
                while h < len(lst) and done[lst[h].idx]:
                    h += 1
                head[e] = h
                cnt = 0
                k = h
                te = tcl[e]
                while k < len(lst) and cnt < window:
                    o = lst[k]
                    k += 1
                    if done[o.idx]:
                        continue
                    cnt += 1
                    if o.nin == 0:
                        st = o.rt if o.rt > te else te
                        if best is None or st < best[0] or (st == best[0] and o.idx < best[1].idx):
                            best = (st, o)
                            if st <= te and cnt == 1:
                                break
            st, o = best
            e = o.eng
            done[o.idx] = True
            remaining -= 1
            order[e].append(o)
            if o.is_dma:
                tcl[e] = st + 100.0
                o.fin = st + 100.0 + o.lat
            else:
                tcl[e] = st + o.cost
                o.fin = st + o.cost
            for sc in o.succ:
                sc.nin -= 1
                r = o.fin + (xlat if (sc.eng != e or o.is_dma) else 0.0)
                if r > sc.rt:
                    sc.rt = r
        self.eng_ops = order
        self.est_ns = max(tcl.values())

    def finalize(self):
        if self.do_sched:
            self.schedule()
        for e in self.ENGS:
            hist = []
            for i_, op in enumerate(self.eng_ops[e]):
                op.seq = i_
                if op.is_dma:
                    n = len(hist)
                    if n >= self.ndma:
                        op.deps.append(hist[n - self.ndma])
                    op.sem = n % self.ndma
                    op.semval = 16 * (n // self.ndma + 1)
                    hist.append(op)
            self.dma_count[e] = len(hist)
        waited = {e: {p: -1 for p in self.ENGS} for e in self.ENGS}
        waited_dma = {e: set() for e in self.ENGS}
        for op in [o for e in self.ENGS for o in self.eng_ops[e]]:
            need = {}
            need_dma = []
            for d in op.deps:
                if d is op:
                    continue
                if d.is_dma:
                    if d.idx not in waited_dma[op.eng]:
                        waited_dma[op.eng].add(d.idx)
                        need_dma.append(d)
                else:
                    if d.seq > waited[op.eng][d.eng]:
                        if d.eng not in need or need[d.eng].seq < d.seq:
                            need[d.eng] = d
            ws = []
            for p, d in need.items():
                waited[op.eng][p] = d.seq
                d.sig = True
                ws.append(d)
            ws.extend(need_dma)
            op.waits = ws
        for e in self.ENGS:
            c = 0
            for op in self.eng_ops[e]:
                if not op.is_dma and op.sig:
                    c += 1
                    op.semval = c

    def emit(self, nc, stack):
        self.finalize()
        esem = {e: stack.enter_context(nc.semaphore("s_" + e)) for e in self.ENGS}
        dsem = {e: [stack.enter_context(nc.semaphore(f"d_{e}{i}")) for i in range(self.ndma)]
                for e in self.ENGS if self.dma_count[e] > 0}
        block = stack.enter_context(nc.Block())

        def run(e, eng):
            for op in self.eng_ops[e]:
                for d in op.waits:
                    if d.is_dma:
                        eng.wait_ge(dsem[d.eng][d.sem], d.semval)
                    else:
                        eng.wait_ge(esem[d.eng], d.semval)
                ins = op.fn(eng)
                if op.is_dma:
                    ins.then_inc(dsem[e][op.sem], 16)
                elif op.sig:
                    ins.then_inc(esem[e], 1)

        @block.tensor
        def _(eng):
            run("pe", eng)

        @block.scalar
        def _(eng):
            run("act", eng)

        @block.vector
        def _(eng):
            run("dve", eng)

        @block.gpsimd
        def _(eng):
            run("pool", eng)

        @block.sync
        def _(eng):
            run("sp", eng)


def _rope_tables(L):
    half = 16
    inv = (10000.0 ** (-np.arange(half, dtype=np.float32) / half)).astype(np.float32)
    t = np.arange(L)
    cosT = np.zeros((128, L), np.float32)
    sinT = np.zeros((128, L), np.float32)
    for p in range(128):
        d = p % 64
        pos = (t // 64) if d < 32 else (t % 64)
        dd = d % 32
        j = dd % 16
        ang = pos.astype(np.float32) * inv[j]
        cosT[p] = np.cos(ang)
        s = np.sin(ang)
        sinT[p] = -s if dd < 16 else s
    pm = np.zeros((128, 128), np.float32)
    for m in range(128):
        d = m % 64
        dd = d % 32
        partner = m + 16 if dd < 16 else m - 16
        pm[partner, m] = 1.0
    return cosT, sinT, pm


def _et_tables(rpb):
    H = rpb.shape[0]
    out = np.full((H, 128, 2, 16, 64), NEG, np.float32)
    c = np.arange(64)
    cs = np.clip(c - 8, 0, 48)
    for p in range(128):
        half, kc = p // 64, p % 64
        colvalid = (kc >= cs) & (kc <= cs + 15)
        cidx = np.clip(kc - c + 15, 0, 30)
        for j in range(16):
            dr = 14 - j + half
            if dr < 0 or dr > 14:
                continue
            vals = rpb[:, dr, :][:, cidx]
            vals = np.where(colvalid[None, :], vals, NEG)
            out[:, p, 0, j, :] = vals
            if 3 <= dr <= 10:
                out[:, p, 1, j, :] = vals
    return out.reshape(H, 128, 2048)


def na_plan(ROWS, m):
    def rs_of(r):
        return min(max(r - 4, 0), ROWS - 8)
    plan = []
    for t in range(ROWS // 2):
        rows = [r for r in range(8 * m, 8 * m + 8) if not (2 * t + 1 < rs_of(r) or 2 * t > rs_of(r) + 7)]
        if not rows:
            continue
        runs = []
        for r in rows:
            ty = 1 if (4 <= r <= ROWS - 4) else 0
            if runs and runs[-1][0] == ty and runs[-1][2] == r - 1:
                runs[-1][2] = r
            else:
                runs.append([ty, r, r])
        plan.append((t, runs, rows[0], rows[-1]))
    return plan


def build(L, kstop=99):
    ROWS = L // 64
    NB = L // 512
    NT = L // 128
    nc = bass.Bass("TRN2", target_bir_lowering=False)
    dr = lambda name, shape, dt, kind="ExternalInput": nc.dram_tensor(name, shape, dt, kind=kind).ap()
    x_d = dr("x", [L, D], F32)
    ctx_d = dr("ctx", [CTX, D], F32)
    cc_d = dr("cc", [D, 2], F32)
    wmod_d = dr("w_mod", [D, 6 * D], F32)
    bmod_d = dr("b_mod", [1, 6 * D], F32)
    g4_d = dr("g4", [4, D], F32)
    win_d = dr("w_in", [D, 3104], F32)
    et_d = dr("ettab", [8, 128, 2048], F32)
    wa2_d = dr("wa2blk", [32, 512], F32)
    ba_d = dr("bacol", [128, 4], F32)
    gg_d = dr("glagain", [128, 4], F32)
    cos_d = dr("cosT", [128, L], F32)
    sin_d = dr("sinT", [128, L], F32)
    pm_d = dr("pm", [128, 128], F32)
    idf_d = dr("identf", [128, 128], F32)
    tri_d = dr("tri4", [2, 128, 512], F32)
    smask_d = dr("scanmask", [128, 512], F32)
    sel_d = dr("sel", [2, 128], F32)
    wout_d = dr("w_out", [D, D], F32)
    wgu_d = dr("w_gu", [D, 2 * FFN], F32)
    wdn_d = dr("w_down", [FFN, D], F32)
    out_d = dr("out", [L, D], F32, kind="ExternalOutput")
    sbscr_d = dr("sbscr", [NT, 128, 512], BF16, kind="Internal")
    etscr_d = dr("etscr", [8, 128, 2048], BF16, kind="Internal")
    gscr_d = dr("gscr", [128, 1024], F32, kind="Internal")
    wguscr_d = dr("wguscr", [D, 2 * FFN], BF16, kind="Internal")
    wdnscr_d = dr("wdnscr", [FFN, D], BF16, kind="Internal")

    P = Prog()
    st = ExitStack()
    with st:
        def sb(name, shape, dt):
            return st.enter_context(nc.sbuf_tensor("sb_" + name, shape, dt))

        ARENA = 67840
        arena = sb("arena", [128, ARENA], BF16)
        win = arena[:, 0:24832].rearrange("p (k n) -> p k n", k=8)
        wout = arena[:, 24832:33024].rearrange("p (k n) -> p k n", k=8)
        ETR = [arena[:, 33024 + i * 2048: 33024 + (i + 1) * 2048] for i in range(2)]
        wmring = [arena[:, 37120 + i * 8192: 37120 + (i + 1) * 8192].bitcast(F32).rearrange("p (k n) -> p k n", k=8)
                  for i in range(2)]
        VR = [arena[:, 37120 + i * 1024: 37120 + (i + 1) * 1024].rearrange("p (h t d) -> p h t d", h=8, t=2)
              for i in range(12)]
        o_ = 37120 + 12 * 1024
        KTR = [arena[:, o_ + i * 2048: o_ + (i + 1) * 2048].rearrange("p (j n) -> p j n", j=4) for i in range(3)]
        o_ += 3 * 2048
        QTR = [arena[:, o_ + i * 2048: o_ + (i + 1) * 2048].rearrange("p (j n) -> p j n", j=4) for i in range(2)]
        o_ += 2 * 2048
        qdec_v = [arena[:, o_ + i * 512: o_ + (i + 1) * 512] for i in range(4)]
        o_ += 2048
        kinv_v = [arena[:, o_ + i * 1024: o_ + (i + 1) * 1024].rearrange("p (h n) -> p h n", h=2) for i in range(4)]
        o_ += 4096
        kendT = arena[:, o_: o_ + 2048].rearrange("p (c n) -> p c n", c=4)
        o_ += 2048
        assert o_ <= ARENA, o_
        wgu = arena[:, 0:45056].rearrange("p (k n) -> p k n", k=8)
        wdn = arena[:, 45056:67584].rearrange("p (j n) -> p j n", j=NJ)
        B_win = [Buf() for _ in range(8)]
        B_wout = [Buf() for _ in range(8)]
        B_ETR = [Buf(), Buf()]
        B_etscr = [Buf() for _ in range(8)]
        B_gscr = Buf()
        B_wm = [Buf(), Buf()]
        B_VR = [Buf() for _ in range(12)]
        B_KTR = [Buf() for _ in range(3)]
        B_QTR = [Buf() for _ in range(2)]
        qdec = [(qdec_v[i], Buf()) for i in range(4)]
        kinv = [(kinv_v[i], Buf()) for i in range(4)]
        B_kendT = Buf()
        B_arenaM = (B_win + B_wout + B_ETR + B_wm + B_VR + B_KTR + B_QTR + [b for _, b in qdec + kinv] + [B_kendT])
        B_wgu = [Buf() for _ in range(8)]
        B_wdn = [Buf() for _ in range(NJ)]

        ps = [st.enter_context(nc.psum_tensor(f"ps{i}", [128, 512], F32)) for i in range(8)]
        B_ps = [Buf(f"ps{i}") for i in range(8)]

        def T(name, shape, dt):
            return sb(name, shape, dt), Buf(name)

        identf, B_identf = T("identf", [128, 128], F32)
        identb, B_identb = T("identb", [128, 128], BF16)
        pm, B_pm = T("pm", [128, 128], BF16)
        onesb, B_onesb = T("onesb", [128, 128], BF16)
        tri, B_tri = T("tri", [128, 2, 512], BF16)
        smask, B_smask = T("smask", [128, 512], BF16)
        sel, B_sel = T("sel", [2, 128], F32)
        wa2, B_wa2 = T("wa2", [32, 512], BF16)
        negba, B_negba = T("negba", [128, 4], F32)
        ggain, B_ggain = T("ggain", [128, 4], F32)
        ccs, B_ccs = T("ccs", [128, 8, 2], F32)
        modcol, B_modcol = T("modcol", [128, 48, 2], F32)
        gcol, B_gcol = T("gcol", [128, 4, 8], F32)
        gm1, B_gm1 = T("gm1", [128, 8], F32)
        cgm1, B_cgm1 = T("cgm1", [128, 8], F32)
        gm2, B_gm2 = T("gm2", [128, 8], F32)
        sh1, B_sh1 = T("sh1", [128, 8], F32)
        csh1, B_csh1 = T("csh1", [128, 8], F32)
        sh2, B_sh2 = T("sh2", [128, 8], F32)
        gate1, B_gate1 = T("gate1", [128, 1024], F32)
        gate2, B_gate2 = gate1, B_gate1
        xt = [T(f"xt{i}", [128, 1024], F32) for i in range(1)] * 2
        xn, B_xn = T("xn", [128, 1024], F32)
        modrow, B_modrow = xt[0][0][0:2, 0:512], xt[0][1]
        bmg, B_bmg = xt[0][0][0:2, 512:1024], xt[0][1]
        colt, B_colt = T("colt", [128, 8], F32)
        hT = [T(f"hT{i}", [128, 8, 512], BF16) for i in range(1)] * 2
        wk = sb("wk", [128, 11264], BF16)
        mixN = (wk[:, 0:2048].rearrange("p (k n) -> p k n", k=4), Buf())
        mixG1_t = sb("mixG1", [128, 4, 512], BF16)
        mixG = [(wk[:, 2048:4096].rearrange("p (k n) -> p k n", k=4), Buf()), (mixG1_t, Buf())]
        srT, B_srT = wk[:, 4096:6144].rearrange("p (k n) -> p k n", k=4), Buf()
        gvt, B_gvt = wk[:, 6144:8192].rearrange("p (k n) -> p k n", k=4), Buf()
        KcT, B_KcT = wk[:, 8192:9216].rearrange("p (k n) -> p k n", k=4), Buf()
        Vc = [(wk[:, 9216 + i * 1024: 9216 + (i + 1) * 1024].rearrange("p (h t d) -> p h t d", h=8, t=2), Buf()) for i in range(2)]
        B_wk = [mixN[1], mixG[0][1], B_srT, B_gvt, B_KcT, Vc[0][1], Vc[1][1]]
        wk2 = sb("wk2", [128, 13 * 512], BF16)
        _w2 = [(wk2[:, i * 512:(i + 1) * 512], Buf()) for i in range(13)]
        Pt = [_w2[0], _w2[1]] * 2
        graw = [_w2[2]] * 4
        grope = [_w2[3], _w2[4], _w2[5], _w2[6]]
        B_wk2 = [b for _, b in _w2]
        f1, B_f1 = T("f1", [128, 512], F32)
        f2, B_f2 = T("f2", [128, 512], F32)
        rd, B_rd = xn, B_xn
        E1, B_E1 = T("E1", [128, 512], F32)
        E2, B_E2 = E1, B_E1
        cosb, B_cosb = E1, B_E1
        sinb, B_sinb = f2, B_f2
        afT, B_afT = T("afT", [32, 512], BF16)
        kend = [_w2[7]] * 4
        dec, B_dec = T("dec", [128, 4, 4], F32)
        Sst = [T(f"Sst{i}", [128, 128], F32) for i in range(4)]
        SbfZ, B_SbfZ = _w2[8]
        Asb = [_w2[9], _w2[10]]
        sq, B_sq = Asb[0]

        def _n(ap):
            sh = ap.shape
            n = 1
            for v in sh[1:]:
                n *= int(v)
            return n

        def _ec(eng, n):
            return {"act": 220.0 + 0.75 * n, "dve": 70.0 + 1.25 * n, "pool": 150.0 + 1.9 * n}[eng]

        def dma(eng, out, in_, reads, writes, **kw):
            nbytes = _n(out) * int(out.shape[0]) * 4
            return P.add(eng, lambda e: e.dma_start(out=out, in_=in_, **kw), reads, writes, dma=True, lat=2000.0 + nbytes / 150.0)

        def mm(out, lhsT, rhs, start, stop, reads, writes):
            c = max(64, _n(rhs)) * 0.42 + 8.0
            if rhs.dtype == F32:
                c *= 4.0
            return P.add("pe", lambda e: e.matmul(out=out, lhsT=lhsT, rhs=rhs, start=start, stop=stop), reads, writes, cost=c)

        def tr(out, in_, ident, reads, writes):
            return P.add("pe", lambda e: e.transpose(out=out, in_=in_, identity=ident), reads, writes, cost=130.0)

        def act(out, in_, func, reads, writes, scale=1.0, bias=0.0):
            return P.add("act", lambda e: e.activation(out=out, in_=in_, func=func, bias=bias, scale=scale), reads, writes,
                         cost=_ec("act", _n(out)))

        def tt(eng, out, in0, in1, op, reads, writes):
            return P.add(eng, lambda e: e.tensor_tensor(out=out, in0=in0, in1=in1, op=op), reads, writes, cost=_ec(eng, _n(out)))

        def ts(eng, out, in0, s1, s2, op0, op1, reads, writes):
            return P.add(eng, lambda e: e.tensor_scalar(out=out, in0=in0, scalar1=s1, scalar2=s2, op0=op0, op1=op1), reads, writes,
                         cost=_ec(eng, _n(out)))

        def stt(out, in0, scalar, in1, op0, op1, reads, writes):
            return P.add("dve", lambda e: e.scalar_tensor_tensor(out=out, in0=in0, scalar=scalar, in1=in1, op0=op0, op1=op1), reads, writes,
                         cost=_ec("dve", _n(out)))

        def cp(eng, out, in_, reads, writes):
            if eng == "act":
                return P.add("act", lambda e: e.copy(out=out, in_=in_), reads, writes, cost=_ec("act", _n(out)))
            return P.add(eng, lambda e: e.tensor_copy(out=out, in_=in_), reads, writes, cost=_ec(eng, _n(out)))

        def memset(eng, ap, val, writes):
            return P.add(eng, lambda e: e.memset(ap, val), (), writes, cost=_ec(eng, _n(ap)) * 0.5)

        def rstd_col(out_col, ss_col, n, reads_b, tmp_col, tmpB, outB):
            act(tmp_col, ss_col, AF.Ln, reads_b + [tmpB], [tmpB], scale=1.0 / n, bias=EPS)
            act(out_col, tmp_col, AF.Exp, [tmpB], [outB], scale=-0.5)

        dma("sp", identf[:], idf_d[:, :], [], [B_identf])
        dma("pool", identb[:], idf_d[:, :], [], [B_identb])
        dma("pool", pm[:], pm_d[:, :], [], [B_pm])
        memset("pool", onesb[:], 1.0, [B_onesb])
        dma("pool", tri[:, 0, :], tri_d[0], [], [B_tri])
        dma("pool", tri[:, 1, :], tri_d[1], [], [B_tri])
        dma("pool", smask[:], smask_d[:, :], [], [B_smask])
        dma("sp", sel[:], sel_d[:, :], [], [B_sel])
        dma("pool", wa2[:], wa2_d[:, :], [], [B_wa2])
        dma("sp", negba[:], ba_d[:, :], [], [B_negba])
        ts("dve", negba[:], negba[:], -1.0, None, ALU.mult, ALU.bypass, [B_negba], [B_negba])
        dma("sp", ggain[:], gg_d[:, :], [], [B_ggain])
        dma("sp", ccs[:], cc_d.rearrange("(k p) j -> p k j", p=128), [], [B_ccs], allow_slow_non_contiguous=True)
        dma("sp", gcol[:], g4_d.rearrange("a (k p) -> p a k", p=128), [], [B_gcol], allow_slow_non_contiguous=True)
        act(ccs[:], ccs[:], AF.Silu, [B_ccs], [B_ccs])
        for v_, b_ in Vc:
            memset("pool", v_[:], 1.0, [b_])

        for k in range(8):
            for hf in range(2):
                dma("pool", win[:, k, hf * 1552:(hf + 1) * 1552], win_d[k * 128:(k + 1) * 128, hf * 1552:(hf + 1) * 1552],
                    [], [B_win[k]] if hf == 0 else [B_win[k]])
        for k in range(8):
            dma("pool", wout[:, k, :], wout_d[k * 128:(k + 1) * 128, :], [], [B_wout[k]])

        B_modc_ps = B_ps[7]
        modc_ps = ps[7]
        for g in range(12):
            wm, bwm = wmring[g % 2], B_wm[g % 2]
            dma("sp", wm, wmod_d[:, g * 512:(g + 1) * 512].rearrange("(k p) n -> p k n", p=128), [], [bwm])
            dma("sp", bmg, bmod_d[0:1, g * 512:(g + 1) * 512].partition_broadcast(2), [], [B_bmg])
            for k in range(8):
                mm(ps[0][0:2, :], ccs[:, k, :], wm[:, k, :], k == 0, k == 7, [B_ccs, bwm], [B_ps[0]])
            tt("dve", modrow, ps[0][0:2, :], bmg, ALU.add, [B_ps[0], B_bmg], [B_modrow])
            for s in range(4):
                ci = g * 4 + s
                mm(modc_ps[:, ci * 2:ci * 2 + 2], modrow[:, s * 128:(s + 1) * 128], identf[0:2, 0:2], True, True,
                   [B_modrow, B_identf], [B_modc_ps])
            if g in (4, 5, 10, 11):
                hf = g % 2
                gi = 1 if g < 8 else 3
                mm(ps[1][:, :], sel[0:2, :], modrow, True, True, [B_sel, B_modrow], [B_ps[1]])
                dma("sp", xn[:, 0:512], g4_d[gi:gi + 1, hf * 512:(hf + 1) * 512].partition_broadcast(128), [], [B_xn])
                if g < 8:
                    tt("dve", gate1[:, hf * 512:(hf + 1) * 512], ps[1][:, :], xn[:, 0:512], ALU.mult, [B_ps[1], B_xn], [B_gate1])
                else:
                    tt("dve", xn[:, 512:1024], ps[1][:, :], xn[:, 0:512], ALU.mult, [B_ps[1], B_xn], [B_xn])
                    dma("sp", gscr_d[:, hf * 512:(hf + 1) * 512], xn[:, 512:1024], [B_xn], [B_gscr])
        cp("dve", modcol[:].rearrange("p a b -> p (a b)"), modc_ps[:, 0:96], [B_modc_ps], [B_modcol])
        cp("dve", sh1[:], modcol[:, 0:8, 0], [B_modcol], [B_sh1])
        cp("dve", csh1[:], modcol[:, 0:8, 1], [B_modcol], [B_csh1])
        cp("dve", sh2[:], modcol[:, 24:32, 0], [B_modcol], [B_sh2])
        stt(gm1[:], modcol[:, 8:16, 0], 1.0, gcol[:, 0, :], ALU.add, ALU.mult, [B_modcol, B_gcol], [B_gm1])
        stt(cgm1[:], modcol[:, 8:16, 1], 1.0, gcol[:, 0, :], ALU.add, ALU.mult, [B_modcol, B_gcol], [B_cgm1])
        stt(gm2[:], modcol[:, 32:40, 0], 1.0, gcol[:, 2, :], ALU.add, ALU.mult, [B_modcol, B_gcol], [B_gm2])

        for h in range(8):
            stg = wmring[h % 2].rearrange("p k n -> p (k n)")[:, 0:2048]
            dma("sp", stg, et_d[h], [], [B_wm[h % 2]])
            act(ETR[h % 2], stg, AF.Exp, [B_wm[h % 2]], [B_ETR[h % 2]])
            dma("sp", etscr_d[h], ETR[h % 2], [B_ETR[h % 2]], [B_etscr[h]])

        for dj_ in range(4):
            memset("pool", kinv[dj_][0], 0.0, [kinv[dj_][1]])
        memset("pool", SbfZ[:], 0.0, [B_SbfZ])
        for i_ in range(12):
            P.add("pool", lambda e, i_=i_: e.memset(VR[i_][:, :, 1, :], 1.0), [], [B_VR[i_]] + B_wm + B_KTR + B_QTR)

        xcnt = [0]

        def norm_transpose(src_rows_ap, gm, bgm, shc, bsh, dstT, bdst, col0):
            xtile, bx = xt[xcnt[0] % 2]
            xcnt[0] += 1
            dma("sp", xtile[:], src_rows_ap, [], [bx])
            act(xn[:], xtile[:], AF.Square, [bx], [B_xn])
            P.add("dve", lambda e: e.tensor_reduce(out=colt[:, 0:1], in_=xn[:], axis=AX.X, op=ALU.add), [B_xn], [B_colt], cost=1200.0)
            rstd_col(colt[:, 2:3], colt[:, 0:1], float(D), [B_colt], colt[:, 1:2], B_colt, B_colt)
            ts("dve", xn[:], xtile[:], colt[:, 2:3], None, ALU.mult, ALU.bypass, [bx, B_colt], [B_xn])
            for half in range(2):
                pb, bpb = ps[2 + half], B_ps[2 + half]
                for kk in range(4):
                    k = half * 4 + kk
                    tr(pb[:, kk * 128:(kk + 1) * 128], xn[:, k * 128:(k + 1) * 128], identf[:], [B_xn, B_identf], [bpb])
                for kk in range(4):
                    k = half * 4 + kk
                    if kk % 2 == 0:
                        act(dstT[:, k, col0:col0 + 128], pb[:, kk * 128:(kk + 1) * 128], AF.Identity,
                            [bpb, bgm, bsh], [bdst], scale=gm[:, k:k + 1], bias=shc[:, k:k + 1])
                    else:
                        ts("dve", dstT[:, k, col0:col0 + 128], pb[:, kk * 128:(kk + 1) * 128], gm[:, k:k + 1], shc[:, k:k + 1],
                           ALU.mult, ALU.add, [bpb, bgm, bsh], [bdst])
            return xtile, bx

        pcnt = [0]

        def proj_fm(hTt, bh, col, M, N):
            pb, bpb = ps[pcnt[0] % 2], B_ps[pcnt[0] % 2]
            pcnt[0] += 1
            for k in range(8):
                mm(pb[0:M, 0:N], win[:, k, col:col + M], hTt[:, k, 0:N], k == 0, k == 7, [B_win[k], bh], [bpb])
            return pb, bpb

        def proj_tm(hTt, bh, tcol, col):
            pb, bpb = ps[pcnt[0] % 2], B_ps[pcnt[0] % 2]
            pcnt[0] += 1
            for k in range(8):
                mm(pb[:, :], hTt[:, k, tcol:tcol + 128], win[:, k, col:col + 512], k == 0, k == 7, [B_win[k], bh], [bpb])
            return pb, bpb

        def gla_gates(dj, N, backward, afsrc=None):
            af_, baf_ = afsrc if afsrc is not None else (afT, B_afT)
            pz, bpz = ps[7], B_ps[7]
            mm(pz[:, 0:N], wa2[:, dj * 128:(dj + 1) * 128], af_[:, 0:N], True, True, [B_wa2, baf_], [bpz])
            act(f1[:, 0:N], pz[:, 0:N], AF.Exp, [bpz, B_negba], [B_f1], scale=-1.0, bias=negba[:, dj:dj + 1])
            act(f1[:, 0:N], f1[:, 0:N], AF.Ln, [B_f1], [B_f1], scale=1.0, bias=1.0)
            P.add("dve", lambda e: e.tensor_tensor_scan(out=f2[:, 0:N], data0=smask[:, 0:N], data1=f1[:, 0:N], initial=0.0,
                                                         op0=ALU.mult, op1=ALU.add), [B_smask, B_f1], [B_f2], cost=70.0 + 2.35 * N)
            X, BX = f2, B_f2
            if backward:
                tt("dve", f1[:, 0:N], f1[:, 0:N], f2[:, 0:N], ALU.subtract, [B_f1, B_f2], [B_f1])
                for c in range(N // 128):
                    ts("dve", f1[:, c * 128:(c + 1) * 128], f1[:, c * 128:(c + 1) * 128], f2[:, c * 128 + 127:c * 128 + 128], None,
                       ALU.add, ALU.bypass, [B_f1, B_f2], [B_f1])
                X, BX = f1, B_f1
            act(E1[:, 0:N], X[:, 0:N], AF.Exp, [BX], [B_E1], scale=-1.0 / 16.0)
            for c in range(N // 128):
                col = c * 128 if backward else c * 128 + 127
                cp("dve", dec[:, dj, c:c + 1], E1[:, col:col + 1], [B_E1], [B_dec])
            return X, BX

        def gla_k(dj, N, ksrc, bk, want_kinv, kend_too=True, X=None, BX=None):
            act(E2[:, 0:N], X[:, 0:N], AF.Exp, [BX], [B_E2], scale=1.0 / 16.0)
            if want_kinv:
                for hh in range(2):
                    tt("dve", kinv[dj][0][hh * 64:(hh + 1) * 64, hh, 0:N], ksrc[hh * 64:(hh + 1) * 64, 0:N], E2[hh * 64:(hh + 1) * 64, 0:N],
                       ALU.mult, [bk, B_E2], [kinv[dj][1]])
            if not kend_too:
                return
            for c in range(N // 128):
                stt(kend[dj][0][:, c * 128:(c + 1) * 128], ksrc[:, c * 128:(c + 1) * 128], dec[:, dj, c:c + 1],
                    E2[:, c * 128:(c + 1) * 128], ALU.mult, ALU.mult, [bk, B_dec, B_E2], [kend[dj][1]])
            gla_kendT1(dj, N)

        def gla_kendT(djs, N):
            pass

        def gla_kendT1(dj, N):
            pb, bpb = ps[5], B_ps[5]
            pbb = pb[:, :].bitcast(BF16)
            for c in range(N // 128):
                tr(pbb[:, c * 128:(c + 1) * 128], kend[dj][0][:, c * 128:(c + 1) * 128], identb[:], [kend[dj][1], B_identb], [bpb])
            for c in range(N // 128):
                cp("act" if c % 2 == 0 else "dve", kendT[:, c, dj * 128:(dj + 1) * 128], pbb[:, c * 128:(c + 1) * 128], [bpb], [B_kendT])

        def gla_kv(c, djs, vtile, bv):
            pb, bpb = ps[7], B_ps[7]
            for dj in djs:
                pair = dj % 2
                for hh in range(2):
                    head = pair * 2 + hh
                    mm(pb[hh * 64:(hh + 1) * 64, dj * 128:(dj + 1) * 128], kendT[:, c, dj * 128 + hh * 64: dj * 128 + hh * 64 + 64],
                       vtile[:, head * 128:(head + 1) * 128], True, True, [B_kendT, bv], [bpb])
            return pb, bpb

        def scan_update(dj, c, pkv, bpkv):
            s_, bs = Sst[dj]
            stt(s_[:], s_[:], dec[:, dj, c:c + 1], pkv[:, dj * 128:(dj + 1) * 128], ALU.mult, ALU.add, [bs, B_dec, bpkv], [bs])

        B_out = [Buf() for _ in range(NT)]
        if kstop <= 1:
            P.add("sp", lambda e: e.nop(), B_out, [])
            P.emit(nc, st)
            return nc
        hcT, B_hcT = hT[1]
        for tt_ in range(2):
            norm_transpose(ctx_d[tt_ * 128:(tt_ + 1) * 128, :], cgm1, B_cgm1, csh1, B_csh1, hcT, B_hcT, tt_ * 128)
        for j in range(4):
            pb, bpb = proj_fm(hcT, B_hcT, 512 + j * 128, 128, 256)
            cp("act", KcT[:, j, :], pb[:, 0:256], [bpb], [B_KcT])
        for tt_ in range(2):
            pb, bpb = proj_tm(hcT, B_hcT, tt_ * 128, 1024)
            cp("act", Vc[tt_][0][:, :, 0, :], pb[:, :].rearrange("p (h d) -> p h d", h=8), [bpb], [Vc[tt_][1]])
            pb, bpb = proj_tm(hcT, B_hcT, tt_ * 128, 2048)
            cp("dve", gvt[:, tt_, :], pb[:, :], [bpb], [B_gvt])
        for j in range(2):
            pb, bpb = proj_fm(hcT, B_hcT, 1792 + j * 128, 128, 256)
            cp("act", grope[2 + j][0][:, 0:256], pb[:, 0:256], [bpb], [grope[2 + j][1]])
        pb, bpb = proj_fm(hcT, B_hcT, 3072, 32, 256)
        cp("act", afT[:, 0:256], pb[0:32, 0:256], [bpb], [B_afT])
        for dj in range(4):
            memset("pool", Sst[dj][0][:], 0.0, [Sst[dj][1]])
        for dj in range(4):
            X_, BX_ = gla_gates(dj, 256, dj >= 2)
            gla_k(dj, 256, grope[2 + dj % 2][0], grope[2 + dj % 2][1], False, X=X_, BX=BX_)
        gla_kendT([0, 1, 2, 3], 256)
        for c in range(2):
            pkv, bpkv = gla_kv(c, [0, 1], gvt[:, c, :], B_gvt)
            for dj in (0, 1):
                scan_update(dj, c, pkv, bpkv)
        for c in (1, 0):
            pkv, bpkv = gla_kv(c, [2, 3], gvt[:, c, :], B_gvt)
            for dj in (2, 3):
                scan_update(dj, c, pkv, bpkv)

        def run_interleaved(ga, gb, ratio):
            a_live, b_live = ga is not None, gb is not None
            while a_live or b_live:
                if a_live:
                    for _ in range(ratio):
                        try:
                            next(ga)
                        except StopIteration:
                            a_live = False
                            break
                if b_live:
                    try:
                        next(gb)
                    except StopIteration:
                        b_live = False

        def block_front(i, slot, full, bset=None, parts=("norm", "proj", "rope")):
            gvt_, bgvt_, afT_, bafT_, gro_ = bset if bset is not None else (gvt, B_gvt, afT, B_afT, grope)
            hTt, bh = hT[slot]
            if "norm" in parts:
                for t4 in range(4):
                    r0 = i * 512 + t4 * 128
                    norm_transpose(x_d[r0:r0 + 128, :], gm1, B_gm1, sh1, B_sh1, hTt, bh, t4 * 128)
                    yield
            if "proj" in parts:
                yield from _bf_proj(i, full, hTt, bh, gvt_, bgvt_, afT_, bafT_)
            if "rope" in parts:
                yield from _bf_rope(i, full, hTt, bh, gro_)

        def _bf_proj(i, full, hTt, bh, gvt_, bgvt_, afT_, bafT_):
            if full:
                qt, bqt = QTR[i % 2], B_QTR[i % 2]
                kt, bkt = KTR[i % 3], B_KTR[i % 3]
                for j in range(4):
                    pb, bpb = proj_fm(hTt, bh, j * 128, 128, 512)
                    act(qt[:, j, :], pb[:, :], AF.Identity, [bpb], [bqt], scale=0.125)
                for j in range(4):
                    pb, bpb = proj_fm(hTt, bh, 512 + j * 128, 128, 512)
                    cp("dve", kt[:, j, :], pb[:, :], [bpb], [bkt])
                for j in range(4):
                    pb, bpb = proj_fm(hTt, bh, 2560 + j * 128, 128, 512)
                    act(srT[:, j, :], pb[:, :], AF.Silu, [bpb], [B_srT])
                for t4 in range(4):
                    vi = (i * 4 + t4) % 12
                    pb, bpb = proj_tm(hTt, bh, t4 * 128, 1024)
                    cp("act", VR[vi][:, :, 0, :], pb[:, :].rearrange("p (h d) -> p h d", h=8), [bpb], [B_VR[vi]])
            yield
            for t4 in range(4):
                pb, bpb = proj_tm(hTt, bh, t4 * 128, 2048)
                cp("dve", gvt_[:, t4, :], pb[:, :], [bpb], [bgvt_])
                yield
            pb, bpb = proj_fm(hTt, bh, 3072, 32, 512)
            cp("act", afT_[:, :], pb[0:32, :], [bpb], [bafT_])
            yield

        def _bf_rope(i, full, hTt, bh, gro_):
            dma("sp", cosb[:], cos_d[:, i * 512:(i + 1) * 512], [], [B_cosb])
            for idx in ([0, 1, 2, 3] if full else [2, 3]):
                col = (1536 if idx < 2 else 1792) + (idx % 2) * 128
                pb, bpb = proj_fm(hTt, bh, col, 128, 512)
                rw, brw = graw[idx]
                cp("act", rw[:], pb[:, :], [bpb], [brw])
                pr, bpr = ps[7], B_ps[7]
                mm(pr[:, :], pm[:], rw[:], True, True, [B_pm, brw], [bpr])
                dma("sp", sinb[:], sin_d[:, i * 512:(i + 1) * 512], [], [B_sinb])
                tt("pool", f1[:], rw[:], cosb[:], ALU.mult, [brw, B_cosb], [B_f1])
                tt("dve", f2[:], pr[:, :], sinb[:], ALU.mult, [bpr, B_sinb], [B_f2])
                tt("dve", gro_[idx][0][:], f1[:], f2[:], ALU.add, [B_f1, B_f2], [gro_[idx][1]])
                yield

        if kstop <= 2:
            P.add("sp", lambda e: e.nop(), B_out, [])
            P.emit(nc, st)
            return nc
        sblr = [_w2[11], _w2[12]]
        stg, B_stg = sblr[0]
        memset("pool", stg[:], 0.0, [B_stg])
        B_scrch = [Buf() for _ in range(NT)]
        mg1, bmg1 = mixG[1]
        set0 = (gvt, B_gvt, afT, B_afT, grope)
        gro1 = [None, None, (mg1[:, 1, :], bmg1), (mg1[:, 2, :], bmg1)]
        set1 = (srT, B_srT, mg1[0:32, 0, :], bmg1, gro1)
        psets = [set0, set1]

        def pre_b(i):
            gvt_, bgvt_, afT_, bafT_, gro_ = psets[i % 2]
            for dj in (2, 3):
                X_, BX_ = gla_gates(dj, 512, True, afsrc=(afT_, bafT_))
                yield
                gla_k(dj, 512, gro_[2 + dj % 2][0], gro_[2 + dj % 2][1], False, X=X_, BX=BX_)
                yield
            for c in (3, 2, 1, 0):
                ch = i * 4 + c
                for dj in (2, 3):
                    for hh in range(2):
                        head = (dj - 2) * 2 + hh
                        cp("act", stg[hh * 64:(hh + 1) * 64, head * 128:(head + 1) * 128], Sst[dj][0][hh * 64:(hh + 1) * 64, :],
                           [Sst[dj][1]], [B_stg])
                dma("sp", sbscr_d[ch], stg[:], [B_stg], [B_scrch[ch]])
                pkv, bpkv = gla_kv(c, [2, 3], gvt_[:, c, :], bgvt_)
                for dj in (2, 3):
                    scan_update(dj, c, pkv, bpkv)
                yield

        for _ in block_front(NB - 1, 0, False, psets[(NB - 1) % 2]):
            pass
        for i in range(NB - 1, -1, -1):
            ga = pre_b(i)
            gb = block_front(i - 1, 0, False, psets[(i - 1) % 2]) if i >= 1 else None
            run_interleaved(ga, gb, 1)
        B_wgs = [[Buf() for _ in range(4)] for _ in range(8)]
        B_wds = [Buf() for _ in range(NJ)]
        for q in (0, 2, 1, 3):
            for k in range(8):
                dma("pool", wguscr_d[k * 128:(k + 1) * 128, q * 1408:(q + 1) * 1408], wgu_d[k * 128:(k + 1) * 128, q * 1408:(q + 1) * 1408],
                    [], [B_wgs[k][q]])
        for j in range(NJ):
            dma("pool", wdnscr_d[j * 128:(j + 1) * 128, :], wdn_d[j * 128:(j + 1) * 128, :], [], [B_wds[j]])
        if kstop <= 3:
            P.add("sp", lambda e: e.nop(), B_out, [])
            P.emit(nc, st)
            return nc

        def gla_main(i, slot):
            for dj in range(4):
                X_, BX_ = gla_gates(dj, 512, dj >= 2)
                pair = dj % 2
                stt(qdec[dj][0][:], grope[pair][0][:], 0.125, E1[:], ALU.mult, ALU.mult, [grope[pair][1], B_E1], [qdec[dj][1]])
                yield
                gla_k(dj, 512, grope[2 + pair][0], grope[2 + pair][1], True, kend_too=(dj < 2), X=X_, BX=BX_)
                yield
            gla_kendT([0, 1], 512)
            mx, bmx = mixG[slot]
            for c in range(4):
                ch = i * 4 + c
                sbl, B_sbl = sblr[ch % 2]
                dma("sp", sbl[:], sbscr_d[ch], [B_scrch[ch]], [B_sbl])
                for d in range(2):
                    pa, bpa = ps[(5, 7)[d]], B_ps[(5, 7)[d]]
                    for head in range(4):
                        pair, hh = head // 2, head % 2
                        dj = d * 2 + pair
                        mm(pa[:, head * 128:(head + 1) * 128], kinv[dj][0][:, hh, c * 128:(c + 1) * 128],
                           qdec[dj][0][:, c * 128:(c + 1) * 128], True, True, [kinv[dj][1], qdec[dj][1]], [bpa])
                    tt("dve", Asb[d][0][:], pa[:, :], tri[:, d, :], ALU.mult, [bpa, B_tri], [Asb[d][1]])
                    yield
                for dj in (0, 1):
                    for hh in range(2):
                        head = dj * 2 + hh
                        cp("act", SbfZ[hh * 64:(hh + 1) * 64, head * 128:(head + 1) * 128], Sst[dj][0][hh * 64:(hh + 1) * 64, :],
                           [Sst[dj][1]], [B_SbfZ])
                po, bpo = ps[6], B_ps[6]
                for head in range(4):
                    pair, hh = head // 2, head % 2
                    osl = po[:, head * 128:(head + 1) * 128]
                    mm(osl, gvt[:, c, head * 128:(head + 1) * 128], Asb[0][0][:, head * 128:(head + 1) * 128], True, False, [B_gvt, Asb[0][1]], [bpo])
                    mm(osl, gvt[:, c, head * 128:(head + 1) * 128], Asb[1][0][:, head * 128:(head + 1) * 128], False, False, [B_gvt, Asb[1][1]], [bpo])
                    mm(osl, SbfZ[:, head * 128:(head + 1) * 128], qdec[pair][0][:, c * 128:(c + 1) * 128], False, False,
                       [B_SbfZ, qdec[pair][1]], [bpo])
                    mm(osl, sbl[:, head * 128:(head + 1) * 128], qdec[2 + pair][0][:, c * 128:(c + 1) * 128],
                       False, True, [B_sbl, qdec[2 + pair][1]], [bpo])
                yield
                act(sq[:], po[:, :], AF.Square, [bpo], [B_sq])
                pn, bpn = ps[5], B_ps[5]
                mm(pn[:, :], onesb[:], sq[:], True, True, [B_onesb, B_sq], [bpn])
                act(f1[:], pn[:, :], AF.Ln, [bpn], [B_f1], scale=1.0 / 128.0, bias=EPS)
                act(f1[:], f1[:], AF.Exp, [B_f1], [B_f1], scale=-0.5)
                tt("dve", f2[:], po[:, :], f1[:], ALU.mult, [bpo, B_f1], [B_f2])
                for head in range(4):
                    stt(mx[:, head, c * 128:(c + 1) * 128], f2[:, head * 128:(head + 1) * 128], ggain[:, head:head + 1],
                        srT[:, head, c * 128:(c + 1) * 128], ALU.mult, ALU.mult, [B_f2, B_ggain, B_srT], [bmx])
                yield
                pkv, bpkv = gla_kv(c, [0, 1], gvt[:, c, :], B_gvt)
                for dj in (0, 1):
                    scan_update(dj, c, pkv, bpkv)
                yield

        def na_block(m, slot):
            mx, bmx = mixN
            qt, bqt = QTR[m % 2], B_QTR[m % 2]
            plan = na_plan(ROWS, m)
            n_pt = 0
            for h in range(8):
                j, hh = h // 2, h % 2
                qh = qt[hh * 64:(hh + 1) * 64, j, :]
                po, bpo = ps[4], B_ps[4]
                etb, betb = ETR[h % 2], B_ETR[h % 2]
                dma("sp", etb, etscr_d[h], [B_etscr[h]], [betb])
                items = [("c", 0), ("c", 1)] + [("l", e) for e in plan]
                pend = None
                for n, (kind, e) in enumerate(items):
                    bi = (n % 2) if hh == 0 else (2 + n % 2)
                    psS, bpsS = ps[bi], B_ps[bi]
                    pt, bpt = Pt[n_pt % 2]
                    n_pt += 1
                    if kind == "c":
                        c0, c1 = 0, 512
                        kop = KcT[hh * 64:(hh + 1) * 64, j, e * 128:(e + 1) * 128]
                        bk = B_KcT
                        vt, bv = Vc[e]
                    else:
                        t, runs, rlo, rhi = e
                        c0, c1 = (rlo - 8 * m) * 64, (rhi - 8 * m + 1) * 64
                        blk, tin = t // 4, t % 4
                        kop = KTR[blk % 3][hh * 64:(hh + 1) * 64, j, tin * 128:(tin + 1) * 128]
                        bk = B_KTR[blk % 3]
                        vt, bv = VR[t % 12], B_VR[t % 12]
                    mm(psS[:, c0:c1], kop, qh[:, c0:c1], True, True, [bk, bqt], [bpsS])
                    act(pt[:, c0:c1], psS[:, c0:c1], AF.Exp, [bpsS], [bpt])
                    if kind == "l":
                        for ty, ra, rb in runs:
                            a0, a1 = (ra - 8 * m) * 64, (rb - 8 * m + 1) * 64
                            j0 = 7 - 2 * t + ra
                            tb = etb[:, ty * 1024 + j0 * 64: ty * 1024 + j0 * 64 + (a1 - a0)]
                            tt("pool" if n % 2 == 0 else "dve", pt[:, a0:a1], pt[:, a0:a1], tb, ALU.mult, [bpt, betb], [bpt])
                    if pend is not None:
                        pend()
                    def _pv(po=po, vt=vt, pt=pt, c0=c0, c1=c1, n=n, bv=bv, bpt=bpt, bpo=bpo, nitems=len(items), h=h):
                        mm(po[:, c0:c1], vt[:, h, :, :].rearrange("p t d -> p (t d)"), pt[:, c0:c1], n == 0, n == nitems - 1,
                           [bv, bpt], [bpo])
                    pend = _pv
                    yield
                pend()
                act(rd[64:128, 0:512], po[64:128, :], AF.Ln, [bpo], [B_rd])
                act(rd[64:128, 0:512], rd[64:128, 0:512], AF.Exp, [B_rd], [B_rd], scale=-1.0)
                tt("dve", mx[hh * 64:(hh + 1) * 64, j, :], po[0:64, :], rd[64:128, 0:512], ALU.mult, [bpo, B_rd], [bmx])
                yield

        def out_block(m, slot):
            mg, bmg_ = mixG[slot]
            mn, bmn = mixN
            for t4 in range(4):
                r0 = m * 512 + t4 * 128
                xtile, bx = xt[xcnt[0] % 2]
                xcnt[0] += 1
                dma("sp", xtile[:], x_d[r0:r0 + 128, :], [], [bx])
                epilogue(lambda k, hf: ((mn[:, k, t4 * 128:(t4 + 1) * 128] if k < 4 else mg[:, k - 4, t4 * 128:(t4 + 1) * 128]),
                                        wout[:, k, hf * 512:(hf + 1) * 512], [bmn if k < 4 else bmg_, B_wout[k]]), 8,
                         xtile, bx, gate1, B_gate1, r0, pbase=(0 if t4 % 2 == 0 else 2))
                yield

        def epilogue(opnd, nk, xtile, bx, gate, bgate, r0, pbase=0):
            for hf in range(2):
                pb, bpb = ps[pbase + hf], B_ps[pbase + hf]
                for k in range(nk):
                    l_, r_, rb_ = opnd(k, hf)
                    mm(pb[:, :], l_, r_, k == 0, k == nk - 1, rb_, [bpb])
                act(xn[:, hf * 512:(hf + 1) * 512], pb[:, :], AF.Square, [bpb], [B_xn])
            P.add("dve", lambda e: e.tensor_reduce(out=colt[:, 4:5], in_=xn[:], axis=AX.X, op=ALU.add), [B_xn], [B_colt], cost=1200.0)
            rstd_col(colt[:, 6:7], colt[:, 4:5], float(D), [B_colt], colt[:, 5:6], B_colt, B_colt)
            for hf in range(2):
                stt(xn[:, hf * 512:(hf + 1) * 512], ps[pbase + hf][:, :], colt[:, 6:7], gate[:, hf * 512:(hf + 1) * 512], ALU.mult, ALU.mult,
                    [B_ps[pbase + hf], B_colt, bgate], [B_xn])
            tt("pool", xtile[:], xtile[:], xn[:], ALU.add, [bx, B_xn], [bx])
            dma("sp", out_d[r0:r0 + 128, :], xtile[:], [bx], [B_out[r0 // 128]])

        def stream_a(m):
            yield from na_block(m, m % 2)
            yield from out_block(m, m % 2)

        import os as _os2
        RA_ = int(_os2.environ.get('K_RA', '3'))
        EC_ = int(_os2.environ.get('K_EC', '6'))

        def run3(ga, gb, gc, flag, ra, every_c):
            live = [ga is not None, gb is not None, gc is not None]
            rnd = 0
            while any(live):
                if live[0]:
                    for _ in range(ra):
                        try:
                            next(ga)
                        except StopIteration:
                            live[0] = False
                            break
                if live[1]:
                    try:
                        next(gb)
                    except StopIteration:
                        live[1] = False
                if live[2] and (flag[0] or not live[1]) and (rnd % every_c == 0 or not (live[0] or live[1])):
                    try:
                        next(gc)
                    except StopIteration:
                        live[2] = False
                rnd += 1

        for _ in block_front(0, 0, True, parts=("norm",)):
            pass
        for i in range(NB + 1):
            if i < NB:
                for _ in block_front(i, 0, True, parts=("proj",)):
                    pass
            flag = [False]
            ga = stream_a(i - 1) if i >= 1 else None
            gb = None
            if i < NB:
                def stream_b(i=i, flag=flag):
                    yield from block_front(i, 0, True, parts=("rope",))
                    flag[0] = True
                    yield from gla_main(i, i % 2)
                gb = stream_b()
            gc = block_front(i + 1, 0, True, parts=("norm",)) if i + 1 < NB else None
            run3(ga, gb, gc, flag, RA_, EC_)

        if kstop <= 4:
            P.add("sp", lambda e: e.nop(), B_out, [])
            P.emit(nc, st)
            return nc
        B_wguq = [[Buf() for _ in range(4)] for _ in range(8)]
        B_fenceF = Buf()
        P.add("pool", lambda e: e.nop(), [], [B_fenceF] + B_arenaM)
        for q in (0, 2, 1, 3):
            for k in range(8):
                dma("sp", wgu[:, k, q * 1408:(q + 1) * 1408], wguscr_d[k * 128:(k + 1) * 128, q * 1408:(q + 1) * 1408],
                    [B_fenceF, B_wgs[k][q]], [B_wguq[k][q]])
        for j in range(NJ):
            dma("sp", wdn[:, j, :], wdnscr_d[j * 128:(j + 1) * 128, :], [B_fenceF, B_wds[j]], [B_wdn[j]])
        aT, B_aT = wk[:, :].rearrange("p (j n) -> p j n", j=NJ), Buf()
        h2b, B_h2b = wk2[:, 0:4096].rearrange("p (k n) -> p k n", k=8), Buf()
        xt2, B_xt2 = wk2[:, 4096:6144].bitcast(F32), Buf()
        P.add("dve", lambda e: e.memset(aT[:, 0, 0:2], 0.0), [], [B_aT, B_h2b, B_xt2] + B_wk + B_wk2)
        dma("sp", gate2[:], gscr_d[:, :], [B_gscr], [B_gate2])
        H2 = [hT[0], (h2b, B_h2b)]
        B_colt_n = Buf()

        def f_norm(i):
            h2, bh2 = H2[i % 2]
            for t4 in range(4):
                r0 = i * 512 + t4 * 128
                dma("sp", xt2, out_d[r0:r0 + 128, :], [B_out[r0 // 128]], [B_xt2])
                act(xn[:], xt2, AF.Square, [B_xt2], [B_xn])
                P.add("dve", lambda e: e.tensor_reduce(out=colt[:, 0:1], in_=xn[:], axis=AX.X, op=ALU.add), [B_xn], [B_colt_n], cost=1200.0)
                rstd_col(colt[:, 2:3], colt[:, 0:1], float(D), [B_colt_n], colt[:, 1:2], B_colt_n, B_colt_n)
                ts("dve", xt2, xt2, colt[:, 2:3], None, ALU.mult, ALU.bypass, [B_xt2, B_colt_n], [B_xt2])
                yield
                for half in range(2):
                    pb, bpb = ps[2 + half], B_ps[2 + half]
                    for kk in range(4):
                        k = half * 4 + kk
                        tr(pb[:, kk * 128:(kk + 1) * 128], xt2[:, k * 128:(k + 1) * 128], identf[:], [B_xt2, B_identf], [bpb])
                    for kk in range(4):
                        k = half * 4 + kk
                        if kk % 2 == 0:
                            act(h2[:, k, t4 * 128:(t4 + 1) * 128], pb[:, kk * 128:(kk + 1) * 128], AF.Identity,
                                [bpb, B_gm2, B_sh2], [bh2], scale=gm2[:, k:k + 1], bias=sh2[:, k:k + 1])
                        else:
                            ts("dve", h2[:, k, t4 * 128:(t4 + 1) * 128], pb[:, kk * 128:(kk + 1) * 128], gm2[:, k:k + 1], sh2[:, k:k + 1],
                               ALU.mult, ALU.add, [bpb, B_gm2, B_sh2], [bh2])
                    yield

        def f_mm(i):
            h2, bh2 = H2[i % 2]
            for j in range(NJ):
                pg, bpg = ps[4 + (j % 2) * 2], B_ps[4 + (j % 2) * 2]
                pu, bpu = ps[5 + (j % 2) * 2], B_ps[5 + (j % 2) * 2]
                for k in range(8):
                    mm(pg[:, :], wgu[:, k, j * 128:(j + 1) * 128], h2[:, k, :], k == 0, k == 7, [B_wguq[k][(j * 128) // 1408], bh2], [bpg])
                for k in range(8):
                    mm(pu[:, :], wgu[:, k, FFN + j * 128:FFN + (j + 1) * 128], h2[:, k, :], k == 0, k == 7,
                       [B_wguq[k][(FFN + j * 128) // 1408], bh2], [bpu])
                fb, bfb = (f1, B_f1) if j % 2 == 0 else (f2, B_f2)
                act(fb[:], pg[:, :], AF.Silu, [bpg], [bfb])
                tt("dve", aT[:, j, :], pu[:, :], fb[:], ALU.mult, [bpu, bfb], [B_aT])
                yield
            for t4 in range(4):
                r0 = i * 512 + t4 * 128
                xtile, bx = xt[0]
                dma("sp", xtile[:], out_d[r0:r0 + 128, :], [B_out[r0 // 128]], [bx])
                epilogue(lambda k, hf: (aT[:, k, t4 * 128:(t4 + 1) * 128], wdn[:, k, hf * 512:(hf + 1) * 512], [B_aT, B_wdn[k]]), NJ,
                         xtile, bx, gate2, B_gate2, r0, pbase=(0 if t4 % 2 == 0 else 2))
                yield

        for _ in f_norm(0):
            pass
        for i in range(NB):
            run_interleaved(f_mm(i), f_norm(i + 1) if i + 1 < NB else None, 2)

        P.add("sp", lambda e: e.nop(), B_out, [])
        print("sbuf bytes remaining:", nc.sbuf_bytes_remaining)
        P.emit(nc, st)
    return nc


def host_inputs(L, x, c, ctx, c_ctx, w_mod, b_mod, norm_pre_mix, norm_post_mix, norm_pre_ffn, norm_post_ffn, w_in, na_rpb,
                gla_wa2_f, gla_ba_f, gla_wa2_b, gla_ba_b, gla_norm, w_out, w_gate_up, w_down):
    f = lambda a: np.ascontiguousarray(np.asarray(a, dtype=np.float32))
    B = x.shape[0]
    cosT, sinT, pm = _rope_tables(L)
    wa2blk = np.zeros((32, 512), np.float32)
    wa2blk[0:16, 0:256] = f(gla_wa2_f)[0]
    wa2blk[16:32, 256:512] = f(gla_wa2_b)[0]
    ba = np.concatenate([f(gla_ba_f)[0], f(gla_ba_b)[0]])
    bacol = np.ascontiguousarray(ba.reshape(4, 128).T)
    ggain = np.ascontiguousarray(f(gla_norm)[0].reshape(4, 128).T)
    tri = np.zeros((2, 128, 512), np.float32)
    s = np.arange(128)[:, None]
    t = np.arange(128)[None, :]
    tri[0] = np.tile((s <= t).astype(np.float32), (1, 4))
    tri[1] = np.tile((s >= t).astype(np.float32), (1, 4))
    smask = np.ones((128, 512), np.float32)
    smask[:, 0::128] = 0.0
    sel = np.zeros((2, 128), np.float32)
    sel[0, :] = 1.0
    common = {
        "w_mod": f(w_mod)[0], "b_mod": f(b_mod)[0][None, :],
        "g4": np.stack([f(norm_pre_mix)[0], f(norm_post_mix)[0], f(norm_pre_ffn)[0], f(norm_post_ffn)[0]]),
        "w_in": f(w_in)[0], "ettab": _et_tables(f(na_rpb)[0]), "wa2blk": wa2blk, "bacol": bacol, "glagain": ggain,
        "cosT": cosT, "sinT": sinT, "pm": pm, "identf": np.eye(128, dtype=np.float32), "tri4": tri, "scanmask": smask,
        "sel": sel, "w_out": f(w_out)[0], "w_gu": f(w_gate_up)[0], "w_down": f(w_down)[0],
    }
    maps = []
    for b in range(B):
        m = dict(common)
        m["x"] = f(x[b])
        m["ctx"] = f(ctx[b])
        m["cc"] = np.ascontiguousarray(np.stack([f(c[b]), f(c_ctx)], axis=1))
        maps.append(m)
    return maps


def kernel(**inputs):
    x = np.asarray(inputs["x"])
    L = x.shape[1]
    maps = host_inputs(L, **inputs)
    nc = build(L)
    res = run_bass_kernel_spmd(nc, maps, core_ids=list(range(len(maps))))
    return np.stack([np.asarray(r["out"], dtype=np.float32) for r in res.results], axis=0)
```

```python
import numpy as np
import ml_dtypes
from contextlib import ExitStack
import concourse.bass as bass
import concourse.mybir as mybir
from concourse.bass_utils import run_bass_kernel_spmd

F32 = mybir.dt.float32
BF16 = mybir.dt.bfloat16
AF = mybir.ActivationFunctionType
ALU = mybir.AluOpType
AX = mybir.AxisListType

D = 1024
NKC = 8
CTX = 256
FFN = 2816
NJ = FFN // 128
EPS = 1e-6
NEG = -30000.0


class Buf:
    __slots__ = ("name", "w", "rs")

    def __init__(self, name=""):
        self.name = name
        self.w = None
        self.rs = []


class Op:
    __slots__ = ("eng", "fn", "deps", "seq", "sig", "semval", "is_dma", "sem", "waits", "idx", "hz", "cost", "lat", "succ", "nin", "rt", "fin")


class Prog:
    ENGS = ["pe", "act", "dve", "pool", "sp"]

    def __init__(self, ndma_sems=14):
        self.ops = []
        self.eng_ops = {e: [] for e in self.ENGS}
        self.dma_count = {e: 0 for e in self.ENGS}
        self.dma_hist = {e: [] for e in self.ENGS}
        self.ndma = ndma_sems
        self.do_sched = True

    def add(self, eng, fn, reads=(), writes=(), dma=False, cost=300.0, lat=0.0):
        op = Op()
        op.eng = eng
        op.fn = fn
        op.is_dma = dma
        op.sig = False
        op.idx = len(self.ops)
        op.cost = cost
        op.lat = lat
        deps = []
        hz = []
        for b in reads:
            w = b.w
            if w is not None:
                hz.append(w)
                if w.is_dma or dma or w.eng != eng or eng != "pe":
                    deps.append(w)
        for b in writes:
            w = b.w
            if w is not None:
                hz.append(w)
                if w.is_dma or dma or w.eng != eng or eng != "pe":
                    deps.append(w)
            for r in b.rs:
                hz.append(r)
                if r.is_dma or dma or r.eng != eng or eng != "pe":
                    deps.append(r)
        op.deps = deps
        op.hz = hz
        for b in reads:
            b.rs.append(op)
        for b in writes:
            b.w = op
            b.rs = []
        self.ops.append(op)
        self.eng_ops[eng].append(op)
        return op

    def schedule(self, window=40, xlat=150.0):
        ops = self.ops
        for op in ops:
            op.succ = []
            op.rt = 0.0
        for op in ops:
            seen = set()
            n = 0
            for d in op.hz:
                if d is op or d.idx in seen:
                    continue
                seen.add(d.idx)
                d.succ.append(op)
                n += 1
            op.nin = n
        pend = {e: list(self.eng_ops[e]) for e in self.ENGS}
        head = {e: 0 for e in self.ENGS}
        done = [False] * len(ops)
        tcl = {e: 0.0 for e in self.ENGS}
        order = {e: [] for e in self.ENGS}
        remaining = len(ops)
        while remaining:
            best = None
            for e in self.ENGS:
                lst = pend[e]
                h = head[e]
                while h < len(lst) and done[lst[h].idx]:
                    h += 1
                head[e] = h
                cnt = 0
                k = h
                te = tcl[e]
                while k < len(lst) and cnt < window:
                    o = lst[k]
                    k += 1
                    if done[o.idx]:
                        continue
                    cnt += 1
                    if o.nin == 0:
                        st = o.rt if o.rt > te else te
                        if best is None or st < best[0] or (st == best[0] and o.idx < best[1].idx):
                            best = (st, o)
                            if st <= te and cnt == 1:
                                break
            st, o = best
            e = o.eng
            done[o.idx] = True
            remaining -= 1
            order[e].append(o)
            if o.is_dma:
                tcl[e] = st + 100.0
                o.fin = st + 100.0 + o.lat
            else:
                tcl[e] = st + o.cost
                o.fin = st + o.cost
            for sc in o.succ:
                sc.nin -= 1
                r = o.fin + (xlat if (sc.eng != e or o.is_dma) else 0.0)
                if r > sc.rt:
                    sc.rt = r
        self.eng_ops = order
        self.est_ns = max(tcl.values())

    def finalize(self):
        if self.do_sched:
            self.schedule()
        for e in self.ENGS:
            hist = []
            for i_, op in enumerate(self.eng_ops[e]):
                op.seq = i_
                if op.is_dma:
                    n = len(hist)
                    if n >= self.ndma:
                        op.deps.append(hist[n - self.ndma])
                    op.sem = n % self.ndma
                    op.semval = 16 * (n // self.ndma + 1)
                    hist.append(op)
            self.dma_count[e] = len(hist)
        waited = {e: {p: -1 for p in self.ENGS} for e in self.ENGS}
        waited_dma = {e: set() for e in self.ENGS}
        for op in [o for e in self.ENGS for o in self.eng_ops[e]]:
            need = {}
            need_dma = []
            for d in op.deps:
                if d is op:
                    continue
                if d.is_dma:
                    if d.idx not in waited_dma[op.eng]:
                        waited_dma[op.eng].add(d.idx)
                        need_dma.append(d)
                else:
                    if d.seq > waited[op.eng][d.eng]:
                        if d.eng not in need or need[d.eng].seq < d.seq:
                            need[d.eng] = d
            ws = []
            for p, d in need.items():
                waited[op.eng][p] = d.seq
                d.sig = True
                ws.append(d)
            ws.extend(need_dma)
            op.waits = ws
        for e in self.ENGS:
            c = 0
            for op in self.eng_ops[e]:
                if not op.is_dma and op.sig:
                    c += 1
                    op.semval = c

    def emit(self, nc, stack):
        self.finalize()
        esem = {e: stack.enter_context(nc.semaphore("s_" + e)) for e in self.ENGS}
        dsem = {e: [stack.enter_context(nc.semaphore(f"d_{e}{i}")) for i in range(self.ndma)]
                for e in self.ENGS if self.dma_count[e] > 0}
        block = stack.enter_context(nc.Block())

        def run(e, eng):
            for op in self.eng_ops[e]:
                for d in op.waits:
                    if d.is_dma:
                        eng.wait_ge(dsem[d.eng][d.sem], d.semval)
                    else:
                        eng.wait_ge(esem[d.eng], d.semval)
                ins = op.fn(eng)
                if op.is_dma:
                    ins.then_inc(dsem[e][op.sem], 16)
                elif op.sig:
                    ins.then_inc(esem[e], 1)

        @block.tensor
        def _(eng):
            run("pe", eng)

        @block.scalar
        def _(eng):
            run("act", eng)

        @block.vector
        def _(eng):
            run("dve", eng)

        @block.gpsimd
        def _(eng):
            run("pool", eng)

        @block.sync
        def _(eng):
            run("sp", eng)


def _rope_tables(L):
    half = 16
    inv = (10000.0 ** (-np.arange(half, dtype=np.float32) / half)).astype(np.float32)
    t = np.arange(L)
    cosT = np.zeros((128, L), np.float32)
    sinT = np.zeros((128, L), np.float32)
    for p in range(128):
        d = p % 64
        pos = (t // 64) if d < 32 else (t % 64)
        dd = d % 32
        j = dd % 16
        ang = pos.astype(np.float32) * inv[j]
        cosT[p] = np.cos(ang)
        s = np.sin(ang)
        sinT[p] = -s if dd < 16 else s
    pm = np.zeros((128, 128), np.float32)
    for m in range(128):
        d = m % 64
        dd = d % 32
        partner = m + 16 if dd < 16 else m - 16
        pm[partner, m] = 1.0
    return cosT, sinT, pm


def _et_tables(rpb):
    H = rpb.shape[0]
    out = np.full((H, 128, 2, 16, 64), NEG, np.float32)
    c = np.arange(64)
    cs = np.clip(c - 8, 0, 48)
    for p in range(128):
        half, kc = p // 64, p % 64
        colvalid = (kc >= cs) & (kc <= cs + 15)
        cidx = np.clip(kc - c + 15, 0, 30)
        for j in range(16):
            dr = 14 - j + half
            if dr < 0 or dr > 14:
                continue
            vals = rpb[:, dr, :][:, cidx]
            vals = np.where(colvalid[None, :], vals, NEG)
            out[:, p, 0, j, :] = vals
            if 3 <= dr <= 10:
                out[:, p, 1, j, :] = vals
    return out.reshape(H, 128, 2048)


def na_plan(ROWS, m):
    def rs_of(r):
        return min(max(r - 4, 0), ROWS - 8)
    plan = []
    for t in range(ROWS // 2):
        rows = [r for r in range(8 * m, 8 * m + 8) if not (2 * t + 1 < rs_of(r) or 2 * t > rs_of(r) + 7)]
        if not rows:
            continue
        runs = []
        for r in rows:
            ty = 1 if (4 <= r <= ROWS - 4) else 0
            if runs and runs[-1][0] == ty and runs[-1][2] == r - 1:
                runs[-1][2] = r
            else:
                runs.append([ty, r, r])
        plan.append((t, runs, rows[0], rows[-1]))
    return plan


def build(L, kstop=99):
    ROWS = L // 64
    NB = L // 512
    NT = L // 128
    nc = bass.Bass("TRN2", target_bir_lowering=False)
    dr = lambda name, shape, dt, kind="ExternalInput": nc.dram_tensor(name, shape, dt, kind=kind).ap()
    x_d = dr("x", [L, D], F32)
    ctx_d = dr("ctx", [CTX, D], F32)
    cc_d = dr("cc", [D, 2], F32)
    wmod_d = dr("w_mod", [D, 6 * D], F32)
    bmod_d = dr("b_mod", [1, 6 * D], F32)
    g4_d = dr("g4", [4, D], F32)
    win_d = dr("w_in", [D, 3104], F32)
    et_d = dr("ettab", [8, 128, 2048], F32)
    wa2_d = dr("wa2blk", [32, 512], F32)
    ba_d = dr("bacol", [128, 4], F32)
    gg_d = dr("glagain", [128, 4], F32)
    cos_d = dr("cosT", [128, L], F32)
    sin_d = dr("sinT", [128, L], F32)
    pm_d = dr("pm", [128, 128], F32)
    idf_d = dr("identf", [128, 128], F32)
    tri_d = dr("tri4", [2, 128, 512], F32)
    smask_d = dr("scanmask", [128, 512], F32)
    sel_d = dr("sel", [2, 128], F32)
    wout_d = dr("w_out", [D, D], F32)
    wgu_d = dr("w_gu", [D, 2 * FFN], F32)
    wdn_d = dr("w_down", [FFN, D], F32)
    out_d = dr("out", [L, D], F32, kind="ExternalOutput")
    sbscr_d = dr("sbscr", [NT, 128, 512], BF16, kind="Internal")
    etscr_d = dr("etscr", [8, 128, 2048], BF16, kind="Internal")
    gscr_d = dr("gscr", [128, 1024], F32, kind="Internal")

    P = Prog()
    st = ExitStack()
    with st:
        def sb(name, shape, dt):
            return st.enter_context(nc.sbuf_tensor("sb_" + name, shape, dt))

        ARENA = 67840
        arena = sb("arena", [128, ARENA], BF16)
        win = arena[:, 0:24832].rearrange("p (k n) -> p k n", k=8)
        wout = arena[:, 24832:33024].rearrange("p (k n) -> p k n", k=8)
        ETR = [arena[:, 33024 + i * 2048: 33024 + (i + 1) * 2048] for i in range(2)]
        wmring = [arena[:, 37120 + i * 8192: 37120 + (i + 1) * 8192].bitcast(F32).rearrange("p (k n) -> p k n", k=8)
                  for i in range(2)]
        VR = [arena[:, 37120 + i * 1024: 37120 + (i + 1) * 1024].rearrange("p (h t d) -> p h t d", h=8, t=2)
              for i in range(12)]
        o_ = 37120 + 12 * 1024
        KTR = [arena[:, o_ + i * 2048: o_ + (i + 1) * 2048].rearrange("p (j n) -> p j n", j=4) for i in range(3)]
        o_ += 3 * 2048
        QTR = [arena[:, o_ + i * 2048: o_ + (i + 1) * 2048].rearrange("p (j n) -> p j n", j=4) for i in range(2)]
        o_ += 2 * 2048
        qdec_v = [arena[:, o_ + i * 512: o_ + (i + 1) * 512] for i in range(4)]
        o_ += 2048
        kinv_v = [arena[:, o_ + i * 1024: o_ + (i + 1) * 1024].rearrange("p (h n) -> p h n", h=2) for i in range(4)]
        o_ += 4096
        kendT = arena[:, o_: o_ + 2048].rearrange("p (c n) -> p c n", c=4)
        o_ += 2048
        assert o_ <= ARENA, o_
        wgu = arena[:, 0:45056].rearrange("p (k n) -> p k n", k=8)
        wdn = arena[:, 45056:67584].rearrange("p (j n) -> p j n", j=NJ)
        B_win = [Buf() for _ in range(8)]
        B_wout = [Buf() for _ in range(8)]
        B_ETR = [Buf(), Buf()]
        B_etscr = [Buf() for _ in range(8)]
        B_gscr = Buf()
        B_wm = [Buf(), Buf()]
        B_VR = [Buf() for _ in range(12)]
        B_KTR = [Buf() for _ in range(3)]
        B_QTR = [Buf() for _ in range(2)]
        qdec = [(qdec_v[i], Buf()) for i in range(4)]
        kinv = [(kinv_v[i], Buf()) for i in range(4)]
        B_kendT = Buf()
        B_arenaM = (B_win + B_wout + B_ETR + B_wm + B_VR + B_KTR + B_QTR + [b for _, b in qdec + kinv] + [B_kendT])
        B_wgu = [Buf() for _ in range(8)]
        B_wdn = [Buf() for _ in range(NJ)]

        ps = [st.enter_context(nc.psum_tensor(f"ps{i}", [128, 512], F32)) for i in range(8)]
        B_ps = [Buf(f"ps{i}") for i in range(8)]

        def T(name, shape, dt):
            return sb(name, shape, dt), Buf(name)

        identf, B_identf = T("identf", [128, 128], F32)
        identb, B_identb = T("identb", [128, 128], BF16)
        pm, B_pm = T("pm", [128, 128], BF16)
        onesb, B_onesb = T("onesb", [128, 128], BF16)
        tri, B_tri = T("tri", [128, 2, 512], BF16)
        smask, B_smask = T("smask", [128, 512], BF16)
        sel, B_sel = T("sel", [2, 128], F32)
        wa2, B_wa2 = T("wa2", [32, 512], BF16)
        negba, B_negba = T("negba", [128, 4], F32)
        ggain, B_ggain = T("ggain", [128, 4], F32)
        ccs, B_ccs = T("ccs", [128, 8, 2], F32)
        modcol, B_modcol = T("modcol", [128, 48, 2], F32)
        gcol, B_gcol = T("gcol", [128, 4, 8], F32)
        gm1, B_gm1 = T("gm1", [128, 8], F32)
        cgm1, B_cgm1 = T("cgm1", [128, 8], F32)
        gm2, B_gm2 = T("gm2", [128, 8], F32)
        sh1, B_sh1 = T("sh1", [128, 8], F32)
        csh1, B_csh1 = T("csh1", [128, 8], F32)
        sh2, B_sh2 = T("sh2", [128, 8], F32)
        gate1, B_gate1 = T("gate1", [128, 1024], F32)
        gate2, B_gate2 = gate1, B_gate1
        xt = [T(f"xt{i}", [128, 1024], F32) for i in range(1)] * 2
        xn, B_xn = T("xn", [128, 1024], F32)
        modrow, B_modrow = xt[0][0][0:2, 0:512], xt[0][1]
        bmg, B_bmg = xt[0][0][0:2, 512:1024], xt[0][1]
        colt, B_colt = T("colt", [128, 8], F32)
        hT = [T(f"hT{i}", [128, 8, 512], BF16) for i in range(1)] * 2
        wk = sb("wk", [128, 11264], BF16)
        mixN = (wk[:, 0:2048].rearrange("p (k n) -> p k n", k=4), Buf())
        mixG1_t = sb("mixG1", [128, 4, 512], BF16)
        mixG = [(wk[:, 2048:4096].rearrange("p (k n) -> p k n", k=4), Buf()), (mixG1_t, Buf())]
        srT, B_srT = wk[:, 4096:6144].rearrange("p (k n) -> p k n", k=4), Buf()
        gvt, B_gvt = wk[:, 6144:8192].rearrange("p (k n) -> p k n", k=4), Buf()
        KcT, B_KcT = wk[:, 8192:9216].rearrange("p (k n) -> p k n", k=4), Buf()
        Vc = [(wk[:, 9216 + i * 1024: 9216 + (i + 1) * 1024].rearrange("p (h t d) -> p h t d", h=8, t=2), Buf()) for i in range(2)]
        B_wk = [mixN[1], mixG[0][1], B_srT, B_gvt, B_KcT, Vc[0][1], Vc[1][1]]
        wk2 = sb("wk2", [128, 13 * 512], BF16)
        _w2 = [(wk2[:, i * 512:(i + 1) * 512], Buf()) for i in range(13)]
        Pt = [_w2[0], _w2[1]] * 2
        graw = [_w2[2]] * 4
        grope = [_w2[3], _w2[4], _w2[5], _w2[6]]
        B_wk2 = [b for _, b in _w2]
        f1, B_f1 = T("f1", [128, 512], F32)
        f2, B_f2 = T("f2", [128, 512], F32)
        rd, B_rd = xn, B_xn
        E1, B_E1 = T("E1", [128, 512], F32)
        E2, B_E2 = E1, B_E1
        cosb, B_cosb = E1, B_E1
        sinb, B_sinb = f2, B_f2
        afT, B_afT = T("afT", [32, 512], BF16)
        kend = [_w2[7]] * 4
        dec, B_dec = T("dec", [128, 4, 4], F32)
        Sst = [T(f"Sst{i}", [128, 128], F32) for i in range(4)]
        SbfZ, B_SbfZ = _w2[8]
        Asb = [_w2[9], _w2[10]]
        sq, B_sq = Asb[0]

        def _n(ap):
            sh = ap.shape
            n = 1
            for v in sh[1:]:
                n *= int(v)
            return n

        def _ec(eng, n):
            return {"act": 220.0 + 0.75 * n, "dve": 70.0 + 1.25 * n, "pool": 150.0 + 1.9 * n}[eng]

        def dma(eng, out, in_, reads, writes, **kw):
            nbytes = _n(out) * int(out.shape[0]) * 4
            return P.add(eng, lambda e: e.dma_start(out=out, in_=in_, **kw), reads, writes, dma=True, lat=2000.0 + nbytes / 150.0)

        def mm(out, lhsT, rhs, start, stop, reads, writes):
            c = max(64, _n(rhs)) * 0.42 + 8.0
            if rhs.dtype == F32:
                c *= 4.0
            return P.add("pe", lambda e: e.matmul(out=out, lhsT=lhsT, rhs=rhs, start=start, stop=stop), reads, writes, cost=c)

        def tr(out, in_, ident, reads, writes):
            return P.add("pe", lambda e: e.transpose(out=out, in_=in_, identity=ident), reads, writes, cost=130.0)

        def act(out, in_, func, reads, writes, scale=1.0, bias=0.0):
            return P.add("act", lambda e: e.activation(out=out, in_=in_, func=func, bias=bias, scale=scale), reads, writes,
                         cost=_ec("act", _n(out)))

        def tt(eng, out, in0, in1, op, reads, writes):
            return P.add(eng, lambda e: e.tensor_tensor(out=out, in0=in0, in1=in1, op=op), reads, writes, cost=_ec(eng, _n(out)))

        def ts(eng, out, in0, s1, s2, op0, op1, reads, writes):
            return P.add(eng, lambda e: e.tensor_scalar(out=out, in0=in0, scalar1=s1, scalar2=s2, op0=op0, op1=op1), reads, writes,
                         cost=_ec(eng, _n(out)))

        def stt(out, in0, scalar, in1, op0, op1, reads, writes):
            return P.add("dve", lambda e: e.scalar_tensor_tensor(out=out, in0=in0, scalar=scalar, in1=in1, op0=op0, op1=op1), reads, writes,
                         cost=_ec("dve", _n(out)))

        def cp(eng, out, in_, reads, writes):
            if eng == "act":
                return P.add("act", lambda e: e.copy(out=out, in_=in_), reads, writes, cost=_ec("act", _n(out)))
            return P.add(eng, lambda e: e.tensor_copy(out=out, in_=in_), reads, writes, cost=_ec(eng, _n(out)))

        def memset(eng, ap, val, writes):
            return P.add(eng, lambda e: e.memset(ap, val), (), writes, cost=_ec(eng, _n(ap)) * 0.5)

        def rstd_col(out_col, ss_col, n, reads_b, tmp_col, tmpB, outB):
            act(tmp_col, ss_col, AF.Ln, reads_b + [tmpB], [tmpB], scale=1.0 / n, bias=EPS)
            act(out_col, tmp_col, AF.Exp, [tmpB], [outB], scale=-0.5)

        dma("sp", identf[:], idf_d[:, :], [], [B_identf])
        dma("pool", identb[:], idf_d[:, :], [], [B_identb])
        dma("pool", pm[:], pm_d[:, :], [], [B_pm])
        memset("pool", onesb[:], 1.0, [B_onesb])
        dma("pool", tri[:, 0, :], tri_d[0], [], [B_tri])
        dma("pool", tri[:, 1, :], tri_d[1], [], [B_tri])
        dma("pool", smask[:], smask_d[:, :], [], [B_smask])
        dma("sp", sel[:], sel_d[:, :], [], [B_sel])
        dma("pool", wa2[:], wa2_d[:, :], [], [B_wa2])
        dma("sp", negba[:], ba_d[:, :], [], [B_negba])
        ts("dve", negba[:], negba[:], -1.0, None, ALU.mult, ALU.bypass, [B_negba], [B_negba])
        dma("sp", ggain[:], gg_d[:, :], [], [B_ggain])
        dma("sp", ccs[:], cc_d.rearrange("(k p) j -> p k j", p=128), [], [B_ccs], allow_slow_non_contiguous=True)
        dma("sp", gcol[:], g4_d.rearrange("a (k p) -> p a k", p=128), [], [B_gcol], allow_slow_non_contiguous=True)
        act(ccs[:], ccs[:], AF.Silu, [B_ccs], [B_ccs])
        for v_, b_ in Vc:
            memset("pool", v_[:], 1.0, [b_])

        for k in range(8):
            for hf in range(2):
                dma("pool", win[:, k, hf * 1552:(hf + 1) * 1552], win_d[k * 128:(k + 1) * 128, hf * 1552:(hf + 1) * 1552],
                    [], [B_win[k]] if hf == 0 else [B_win[k]])
        for k in range(8):
            dma("pool", wout[:, k, :], wout_d[k * 128:(k + 1) * 128, :], [], [B_wout[k]])

        B_modc_ps = B_ps[7]
        modc_ps = ps[7]
        for g in range(12):
            wm, bwm = wmring[g % 2], B_wm[g % 2]
            dma("sp", wm, wmod_d[:, g * 512:(g + 1) * 512].rearrange("(k p) n -> p k n", p=128), [], [bwm])
            dma("sp", bmg, bmod_d[0:1, g * 512:(g + 1) * 512].partition_broadcast(2), [], [B_bmg])
            for k in range(8):
                mm(ps[0][0:2, :], ccs[:, k, :], wm[:, k, :], k == 0, k == 7, [B_ccs, bwm], [B_ps[0]])
            tt("dve", modrow, ps[0][0:2, :], bmg, ALU.add, [B_ps[0], B_bmg], [B_modrow])
            for s in range(4):
                ci = g * 4 + s
                mm(modc_ps[:, ci * 2:ci * 2 + 2], modrow[:, s * 128:(s + 1) * 128], identf[0:2, 0:2], True, True,
                   [B_modrow, B_identf], [B_modc_ps])
            if g in (4, 5, 10, 11):
                hf = g % 2
                gi = 1 if g < 8 else 3
                mm(ps[1][:, :], sel[0:2, :], modrow, True, True, [B_sel, B_modrow], [B_ps[1]])
                dma("sp", xn[:, 0:512], g4_d[gi:gi + 1, hf * 512:(hf + 1) * 512].partition_broadcast(128), [], [B_xn])
                if g < 8:
                    tt("dve", gate1[:, hf * 512:(hf + 1) * 512], ps[1][:, :], xn[:, 0:512], ALU.mult, [B_ps[1], B_xn], [B_gate1])
                else:
                    tt("dve", xn[:, 512:1024], ps[1][:, :], xn[:, 0:512], ALU.mult, [B_ps[1], B_xn], [B_xn])
                    dma("sp", gscr_d[:, hf * 512:(hf + 1) * 512], xn[:, 512:1024], [B_xn], [B_gscr])
        cp("dve", modcol[:].rearrange("p a b -> p (a b)"), modc_ps[:, 0:96], [B_modc_ps], [B_modcol])
        cp("dve", sh1[:], modcol[:, 0:8, 0], [B_modcol], [B_sh1])
        cp("dve", csh1[:], modcol[:, 0:8, 1], [B_modcol], [B_csh1])
        cp("dve", sh2[:], modcol[:, 24:32, 0], [B_modcol], [B_sh2])
        stt(gm1[:], modcol[:, 8:16, 0], 1.0, gcol[:, 0, :], ALU.add, ALU.mult, [B_modcol, B_gcol], [B_gm1])
        stt(cgm1[:], modcol[:, 8:16, 1], 1.0, gcol[:, 0, :], ALU.add, ALU.mult, [B_modcol, B_gcol], [B_cgm1])
        stt(gm2[:], modcol[:, 32:40, 0], 1.0, gcol[:, 2, :], ALU.add, ALU.mult, [B_modcol, B_gcol], [B_gm2])

        for h in range(8):
            stg = wmring[h % 2].rearrange("p k n -> p (k n)")[:, 0:2048]
            dma("sp", stg, et_d[h], [], [B_wm[h % 2]])
            act(ETR[h % 2], stg, AF.Exp, [B_wm[h % 2]], [B_ETR[h % 2]])
            dma("sp", etscr_d[h], ETR[h % 2], [B_ETR[h % 2]], [B_etscr[h]])

        for dj_ in range(4):
            memset("pool", kinv[dj_][0], 0.0, [kinv[dj_][1]])
        memset("pool", SbfZ[:], 0.0, [B_SbfZ])
        for i_ in range(12):
            P.add("pool", lambda e, i_=i_: e.memset(VR[i_][:, :, 1, :], 1.0), [], [B_VR[i_]] + B_wm + B_KTR + B_QTR)

        xcnt = [0]

        def norm_transpose(src_rows_ap, gm, bgm, shc, bsh, dstT, bdst, col0):
            xtile, bx = xt[xcnt[0] % 2]
            xcnt[0] += 1
            dma("sp", xtile[:], src_rows_ap, [], [bx])
            P.add("act", lambda e: e.activation(out=xn[:], in_=xtile[:], func=AF.Square, accum_out=colt[:, 0:1]), [bx], [B_xn, B_colt],
                  cost=1100.0)
            rstd_col(colt[:, 2:3], colt[:, 0:1], float(D), [B_colt], colt[:, 1:2], B_colt, B_colt)
            ts("dve", xn[:], xtile[:], colt[:, 2:3], None, ALU.mult, ALU.bypass, [bx, B_colt], [B_xn])
            for half in range(2):
                pb, bpb = ps[2 + half], B_ps[2 + half]
                for kk in range(4):
                    k = half * 4 + kk
                    tr(pb[:, kk * 128:(kk + 1) * 128], xn[:, k * 128:(k + 1) * 128], identf[:], [B_xn, B_identf], [bpb])
                for kk in range(4):
                    k = half * 4 + kk
                    if kk % 2 == 0:
                        act(dstT[:, k, col0:col0 + 128], pb[:, kk * 128:(kk + 1) * 128], AF.Identity,
                            [bpb, bgm, bsh], [bdst], scale=gm[:, k:k + 1], bias=shc[:, k:k + 1])
                    else:
                        ts("dve", dstT[:, k, col0:col0 + 128], pb[:, kk * 128:(kk + 1) * 128], gm[:, k:k + 1], shc[:, k:k + 1],
                           ALU.mult, ALU.add, [bpb, bgm, bsh], [bdst])
            return xtile, bx

        pcnt = [0]

        def proj_fm(hTt, bh, col, M, N):
            pb, bpb = ps[pcnt[0] % 2], B_ps[pcnt[0] % 2]
            pcnt[0] += 1
            for k in range(8):
                mm(pb[0:M, 0:N], win[:, k, col:col + M], hTt[:, k, 0:N], k == 0, k == 7, [B_win[k], bh], [bpb])
            return pb, bpb

        def proj_tm(hTt, bh, tcol, col):
            pb, bpb = ps[pcnt[0] % 2], B_ps[pcnt[0] % 2]
            pcnt[0] += 1
            for k in range(8):
                mm(pb[:, :], hTt[:, k, tcol:tcol + 128], win[:, k, col:col + 512], k == 0, k == 7, [B_win[k], bh], [bpb])
            return pb, bpb

        def gla_gates(dj, N, backward, afsrc=None):
            af_, baf_ = afsrc if afsrc is not None else (afT, B_afT)
            pz, bpz = ps[7], B_ps[7]
            mm(pz[:, 0:N], wa2[:, dj * 128:(dj + 1) * 128], af_[:, 0:N], True, True, [B_wa2, baf_], [bpz])
            act(f1[:, 0:N], pz[:, 0:N], AF.Exp, [bpz, B_negba], [B_f1], scale=-1.0, bias=negba[:, dj:dj + 1])
            act(f1[:, 0:N], f1[:, 0:N], AF.Ln, [B_f1], [B_f1], scale=1.0, bias=1.0)
            P.add("dve", lambda e: e.tensor_tensor_scan(out=f2[:, 0:N], data0=smask[:, 0:N], data1=f1[:, 0:N], initial=0.0,
                                                         op0=ALU.mult, op1=ALU.add), [B_smask, B_f1], [B_f2], cost=70.0 + 2.35 * N)
            X, BX = f2, B_f2
            if backward:
                tt("dve", f1[:, 0:N], f1[:, 0:N], f2[:, 0:N], ALU.subtract, [B_f1, B_f2], [B_f1])
                for c in range(N // 128):
                    ts("dve", f1[:, c * 128:(c + 1) * 128], f1[:, c * 128:(c + 1) * 128], f2[:, c * 128 + 127:c * 128 + 128], None,
                       ALU.add, ALU.bypass, [B_f1, B_f2], [B_f1])
                X, BX = f1, B_f1
            act(E1[:, 0:N], X[:, 0:N], AF.Exp, [BX], [B_E1], scale=-1.0 / 16.0)
            for c in range(N // 128):
                col = c * 128 if backward else c * 128 + 127
                cp("dve", dec[:, dj, c:c + 1], E1[:, col:col + 1], [B_E1], [B_dec])
            return X, BX

        def gla_k(dj, N, ksrc, bk, want_kinv, kend_too=True, X=None, BX=None):
            act(E2[:, 0:N], X[:, 0:N], AF.Exp, [BX], [B_E2], scale=1.0 / 16.0)
            if want_kinv:
                for hh in range(2):
                    tt("dve", kinv[dj][0][hh * 64:(hh + 1) * 64, hh, 0:N], ksrc[hh * 64:(hh + 1) * 64, 0:N], E2[hh * 64:(hh + 1) * 64, 0:N],
                       ALU.mult, [bk, B_E2], [kinv[dj][1]])
            if not kend_too:
                return
            for c in range(N // 128):
                stt(kend[dj][0][:, c * 128:(c + 1) * 128], ksrc[:, c * 128:(c + 1) * 128], dec[:, dj, c:c + 1],
                    E2[:, c * 128:(c + 1) * 128], ALU.mult, ALU.mult, [bk, B_dec, B_E2], [kend[dj][1]])
            gla_kendT1(dj, N)

        def gla_kendT(djs, N):
            pass

        def gla_kendT1(dj, N):
            pb, bpb = ps[5], B_ps[5]
            pbb = pb[:, :].bitcast(BF16)
            for c in range(N // 128):
                tr(pbb[:, c * 128:(c + 1) * 128], kend[dj][0][:, c * 128:(c + 1) * 128], identb[:], [kend[dj][1], B_identb], [bpb])
            for c in range(N // 128):
                cp("act" if c % 2 == 0 else "dve", kendT[:, c, dj * 128:(dj + 1) * 128], pbb[:, c * 128:(c + 1) * 128], [bpb], [B_kendT])

        def gla_kv(c, djs, vtile, bv):
            pb, bpb = ps[7], B_ps[7]
            for dj in djs:
                pair = dj % 2
                for hh in range(2):
                    head = pair * 2 + hh
                    mm(pb[hh * 64:(hh + 1) * 64, dj * 128:(dj + 1) * 128], kendT[:, c, dj * 128 + hh * 64: dj * 128 + hh * 64 + 64],
                       vtile[:, head * 128:(head + 1) * 128], True, True, [B_kendT, bv], [bpb])
            return pb, bpb

        def scan_update(dj, c, pkv, bpkv):
            s_, bs = Sst[dj]
            stt(s_[:], s_[:], dec[:, dj, c:c + 1], pkv[:, dj * 128:(dj + 1) * 128], ALU.mult, ALU.add, [bs, B_dec, bpkv], [bs])

        B_out = [Buf() for _ in range(NT)]
        if kstop <= 1:
            P.add("sp", lambda e: e.nop(), B_out, [])
            P.emit(nc, st)
            return nc
        hcT, B_hcT = hT[1]
        for tt_ in range(2):
            norm_transpose(ctx_d[tt_ * 128:(tt_ + 1) * 128, :], cgm1, B_cgm1, csh1, B_csh1, hcT, B_hcT, tt_ * 128)
        for j in range(4):
            pb, bpb = proj_fm(hcT, B_hcT, 512 + j * 128, 128, 256)
            cp("act", KcT[:, j, :], pb[:, 0:256], [bpb], [B_KcT])
        for tt_ in range(2):
            pb, bpb = proj_tm(hcT, B_hcT, tt_ * 128, 1024)
            cp("act", Vc[tt_][0][:, :, 0, :], pb[:, :].rearrange("p (h d) -> p h d", h=8), [bpb], [Vc[tt_][1]])
            pb, bpb = proj_tm(hcT, B_hcT, tt_ * 128, 2048)
            cp("dve", gvt[:, tt_, :], pb[:, :], [bpb], [B_gvt])
        for j in range(2):
            pb, bpb = proj_fm(hcT, B_hcT, 1792 + j * 128, 128, 256)
            cp("act", grope[2 + j][0][:, 0:256], pb[:, 0:256], [bpb], [grope[2 + j][1]])
        pb, bpb = proj_fm(hcT, B_hcT, 3072, 32, 256)
        cp("act", afT[:, 0:256], pb[0:32, 0:256], [bpb], [B_afT])
        for dj in range(4):
            memset("pool", Sst[dj][0][:], 0.0, [Sst[dj][1]])
        for dj in range(4):
            X_, BX_ = gla_gates(dj, 256, dj >= 2)
            gla_k(dj, 256, grope[2 + dj % 2][0], grope[2 + dj % 2][1], False, X=X_, BX=BX_)
        gla_kendT([0, 1, 2, 3], 256)
        for c in range(2):
            pkv, bpkv = gla_kv(c, [0, 1], gvt[:, c, :], B_gvt)
            for dj in (0, 1):
                scan_update(dj, c, pkv, bpkv)
        for c in (1, 0):
            pkv, bpkv = gla_kv(c, [2, 3], gvt[:, c, :], B_gvt)
            for dj in (2, 3):
                scan_update(dj, c, pkv, bpkv)

        def run_interleaved(ga, gb, ratio):
            a_live, b_live = ga is not None, gb is not None
            while a_live or b_live:
                if a_live:
                    for _ in range(ratio):
                        try:
                            next(ga)
                        except StopIteration:
                            a_live = False
                            break
                if b_live:
                    try:
                        next(gb)
                    except StopIteration:
                        b_live = False

        def block_front(i, slot, full, bset=None, parts=("norm", "proj", "rope")):
            gvt_, bgvt_, afT_, bafT_, gro_ = bset if bset is not None else (gvt, B_gvt, afT, B_afT, grope)
            hTt, bh = hT[slot]
            if "norm" in parts:
                for t4 in range(4):
                    r0 = i * 512 + t4 * 128
                    norm_transpose(x_d[r0:r0 + 128, :], gm1, B_gm1, sh1, B_sh1, hTt, bh, t4 * 128)
                    yield
            if "proj" in parts:
                yield from _bf_proj(i, full, hTt, bh, gvt_, bgvt_, afT_, bafT_)
            if "rope" in parts:
                yield from _bf_rope(i, full, hTt, bh, gro_)

        def _bf_proj(i, full, hTt, bh, gvt_, bgvt_, afT_, bafT_):
            if full:
                qt, bqt = QTR[i % 2], B_QTR[i % 2]
                kt, bkt = KTR[i % 3], B_KTR[i % 3]
                for j in range(4):
                    pb, bpb = proj_fm(hTt, bh, j * 128, 128, 512)
                    act(qt[:, j, :], pb[:, :], AF.Identity, [bpb], [bqt], scale=0.125)
                for j in range(4):
                    pb, bpb = proj_fm(hTt, bh, 512 + j * 128, 128, 512)
                    cp("dve", kt[:, j, :], pb[:, :], [bpb], [bkt])
                for j in range(4):
                    pb, bpb = proj_fm(hTt, bh, 2560 + j * 128, 128, 512)
                    act(srT[:, j, :], pb[:, :], AF.Silu, [bpb], [B_srT])
                for t4 in range(4):
                    vi = (i * 4 + t4) % 12
                    pb, bpb = proj_tm(hTt, bh, t4 * 128, 1024)
                    cp("act", VR[vi][:, :, 0, :], pb[:, :].rearrange("p (h d) -> p h d", h=8), [bpb], [B_VR[vi]])
            yield
            for t4 in range(4):
                pb, bpb = proj_tm(hTt, bh, t4 * 128, 2048)
                cp("dve", gvt_[:, t4, :], pb[:, :], [bpb], [bgvt_])
                yield
            pb, bpb = proj_fm(hTt, bh, 3072, 32, 512)
            cp("act", afT_[:, :], pb[0:32, :], [bpb], [bafT_])
            yield

        def _bf_rope(i, full, hTt, bh, gro_):
            dma("sp", cosb[:], cos_d[:, i * 512:(i + 1) * 512], [], [B_cosb])
            for idx in ([0, 1, 2, 3] if full else [2, 3]):
                col = (1536 if idx < 2 else 1792) + (idx % 2) * 128
                pb, bpb = proj_fm(hTt, bh, col, 128, 512)
                rw, brw = graw[idx]
                cp("act", rw[:], pb[:, :], [bpb], [brw])
                pr, bpr = ps[7], B_ps[7]
                mm(pr[:, :], pm[:], rw[:], True, True, [B_pm, brw], [bpr])
                dma("sp", sinb[:], sin_d[:, i * 512:(i + 1) * 512], [], [B_sinb])
                tt("pool", f1[:], rw[:], cosb[:], ALU.mult, [brw, B_cosb], [B_f1])
                tt("dve", f2[:], pr[:, :], sinb[:], ALU.mult, [bpr, B_sinb], [B_f2])
                tt("dve", gro_[idx][0][:], f1[:], f2[:], ALU.add, [B_f1, B_f2], [gro_[idx][1]])
                yield

        if kstop <= 2:
            P.add("sp", lambda e: e.nop(), B_out, [])
            P.emit(nc, st)
            return nc
        sblr = [_w2[11], _w2[12]]
        stg, B_stg = sblr[0]
        memset("pool", stg[:], 0.0, [B_stg])
        B_scrch = [Buf() for _ in range(NT)]
        mg1, bmg1 = mixG[1]
        set0 = (gvt, B_gvt, afT, B_afT, grope)
        gro1 = [None, None, (mg1[:, 1, :], bmg1), (mg1[:, 2, :], bmg1)]
        set1 = (srT, B_srT, mg1[0:32, 0, :], bmg1, gro1)
        psets = [set0, set1]

        def pre_b(i):
            gvt_, bgvt_, afT_, bafT_, gro_ = psets[i % 2]
            for dj in (2, 3):
                X_, BX_ = gla_gates(dj, 512, True, afsrc=(afT_, bafT_))
                yield
                gla_k(dj, 512, gro_[2 + dj % 2][0], gro_[2 + dj % 2][1], False, X=X_, BX=BX_)
                yield
            for c in (3, 2, 1, 0):
                ch = i * 4 + c
                for dj in (2, 3):
                    for hh in range(2):
                        head = (dj - 2) * 2 + hh
                        cp("act", stg[hh * 64:(hh + 1) * 64, head * 128:(head + 1) * 128], Sst[dj][0][hh * 64:(hh + 1) * 64, :],
                           [Sst[dj][1]], [B_stg])
                dma("sp", sbscr_d[ch], stg[:], [B_stg], [B_scrch[ch]])
                pkv, bpkv = gla_kv(c, [2, 3], gvt_[:, c, :], bgvt_)
                for dj in (2, 3):
                    scan_update(dj, c, pkv, bpkv)
                yield

        for _ in block_front(NB - 1, 0, False, psets[(NB - 1) % 2]):
            pass
        for i in range(NB - 1, -1, -1):
            ga = pre_b(i)
            gb = block_front(i - 1, 0, False, psets[(i - 1) % 2]) if i >= 1 else None
            run_interleaved(ga, gb, 1)
        if kstop <= 3:
            P.add("sp", lambda e: e.nop(), B_out, [])
            P.emit(nc, st)
            return nc

        def gla_main(i, slot):
            for dj in range(4):
                X_, BX_ = gla_gates(dj, 512, dj >= 2)
                pair = dj % 2
                stt(qdec[dj][0][:], grope[pair][0][:], 0.125, E1[:], ALU.mult, ALU.mult, [grope[pair][1], B_E1], [qdec[dj][1]])
                yield
                gla_k(dj, 512, grope[2 + pair][0], grope[2 + pair][1], True, kend_too=(dj < 2), X=X_, BX=BX_)
                yield
            gla_kendT([0, 1], 512)
            mx, bmx = mixG[slot]
            for c in range(4):
                ch = i * 4 + c
                sbl, B_sbl = sblr[ch % 2]
                dma("sp", sbl[:], sbscr_d[ch], [B_scrch[ch]], [B_sbl])
                for d in range(2):
                    pa, bpa = ps[(5, 7)[d]], B_ps[(5, 7)[d]]
                    for head in range(4):
                        pair, hh = head // 2, head % 2
                        dj = d * 2 + pair
                        mm(pa[:, head * 128:(head + 1) * 128], kinv[dj][0][:, hh, c * 128:(c + 1) * 128],
                           qdec[dj][0][:, c * 128:(c + 1) * 128], True, True, [kinv[dj][1], qdec[dj][1]], [bpa])
                    tt("dve", Asb[d][0][:], pa[:, :], tri[:, d, :], ALU.mult, [bpa, B_tri], [Asb[d][1]])
                    yield
                for dj in (0, 1):
                    for hh in range(2):
                        head = dj * 2 + hh
                        cp("act", SbfZ[hh * 64:(hh + 1) * 64, head * 128:(head + 1) * 128], Sst[dj][0][hh * 64:(hh + 1) * 64, :],
                           [Sst[dj][1]], [B_SbfZ])
                po, bpo = ps[6], B_ps[6]
                for head in range(4):
                    pair, hh = head // 2, head % 2
                    osl = po[:, head * 128:(head + 1) * 128]
                    mm(osl, gvt[:, c, head * 128:(head + 1) * 128], Asb[0][0][:, head * 128:(head + 1) * 128], True, False, [B_gvt, Asb[0][1]], [bpo])
                    mm(osl, gvt[:, c, head * 128:(head + 1) * 128], Asb[1][0][:, head * 128:(head + 1) * 128], False, False, [B_gvt, Asb[1][1]], [bpo])
                    mm(osl, SbfZ[:, head * 128:(head + 1) * 128], qdec[pair][0][:, c * 128:(c + 1) * 128], False, False,
                       [B_SbfZ, qdec[pair][1]], [bpo])
                    mm(osl, sbl[:, head * 128:(head + 1) * 128], qdec[2 + pair][0][:, c * 128:(c + 1) * 128],
                       False, True, [B_sbl, qdec[2 + pair][1]], [bpo])
                yield
                act(sq[:], po[:, :], AF.Square, [bpo], [B_sq])
                pn, bpn = ps[5], B_ps[5]
                mm(pn[:, :], onesb[:], sq[:], True, True, [B_onesb, B_sq], [bpn])
                act(f1[:], pn[:, :], AF.Ln, [bpn], [B_f1], scale=1.0 / 128.0, bias=EPS)
                act(f1[:], f1[:], AF.Exp, [B_f1], [B_f1], scale=-0.5)
                tt("dve", f2[:], po[:, :], f1[:], ALU.mult, [bpo, B_f1], [B_f2])
                for head in range(4):
                    stt(mx[:, head, c * 128:(c + 1) * 128], f2[:, head * 128:(head + 1) * 128], ggain[:, head:head + 1],
                        srT[:, head, c * 128:(c + 1) * 128], ALU.mult, ALU.mult, [B_f2, B_ggain, B_srT], [bmx])
                yield
                pkv, bpkv = gla_kv(c, [0, 1], gvt[:, c, :], B_gvt)
                for dj in (0, 1):
                    scan_update(dj, c, pkv, bpkv)
                yield

        def na_block(m, slot):
            mx, bmx = mixN
            qt, bqt = QTR[m % 2], B_QTR[m % 2]
            plan = na_plan(ROWS, m)
            n_pt = 0
            for h in range(8):
                j, hh = h // 2, h % 2
                qh = qt[hh * 64:(hh + 1) * 64, j, :]
                po, bpo = ps[4], B_ps[4]
                etb, betb = ETR[h % 2], B_ETR[h % 2]
                dma("sp", etb, etscr_d[h], [B_etscr[h]], [betb])
                items = [("c", 0), ("c", 1)] + [("l", e) for e in plan]
                pend = None
                for n, (kind, e) in enumerate(items):
                    bi = (n % 2) if hh == 0 else (2 + n % 2)
                    psS, bpsS = ps[bi], B_ps[bi]
                    pt, bpt = Pt[n_pt % 2]
                    n_pt += 1
                    if kind == "c":
                        c0, c1 = 0, 512
                        kop = KcT[hh * 64:(hh + 1) * 64, j, e * 128:(e + 1) * 128]
                        bk = B_KcT
                        vt, bv = Vc[e]
                    else:
                        t, runs, rlo, rhi = e
                        c0, c1 = (rlo - 8 * m) * 64, (rhi - 8 * m + 1) * 64
                        blk, tin = t // 4, t % 4
                        kop = KTR[blk % 3][hh * 64:(hh + 1) * 64, j, tin * 128:(tin + 1) * 128]
                        bk = B_KTR[blk % 3]
                        vt, bv = VR[t % 12], B_VR[t % 12]
                    mm(psS[:, c0:c1], kop, qh[:, c0:c1], True, True, [bk, bqt], [bpsS])
                    act(pt[:, c0:c1], psS[:, c0:c1], AF.Exp, [bpsS], [bpt])
                    if kind == "l":
                        for ty, ra, rb in runs:
                            a0, a1 = (ra - 8 * m) * 64, (rb - 8 * m + 1) * 64
                            j0 = 7 - 2 * t + ra
                            tb = etb[:, ty * 1024 + j0 * 64: ty * 1024 + j0 * 64 + (a1 - a0)]
                            tt("pool" if n % 2 == 0 else "dve", pt[:, a0:a1], pt[:, a0:a1], tb, ALU.mult, [bpt, betb], [bpt])
                    if pend is not None:
                        pend()
                    def _pv(po=po, vt=vt, pt=pt, c0=c0, c1=c1, n=n, bv=bv, bpt=bpt, bpo=bpo, nitems=len(items), h=h):
                        mm(po[:, c0:c1], vt[:, h, :, :].rearrange("p t d -> p (t d)"), pt[:, c0:c1], n == 0, n == nitems - 1,
                           [bv, bpt], [bpo])
                    pend = _pv
                    yield
                pend()
                act(rd[64:128, 0:512], po[64:128, :], AF.Ln, [bpo], [B_rd])
                act(rd[64:128, 0:512], rd[64:128, 0:512], AF.Exp, [B_rd], [B_rd], scale=-1.0)
                tt("dve", mx[hh * 64:(hh + 1) * 64, j, :], po[0:64, :], rd[64:128, 0:512], ALU.mult, [bpo, B_rd], [bmx])
                yield

        def out_block(m, slot):
            mg, bmg_ = mixG[slot]
            mn, bmn = mixN
            for t4 in range(4):
                r0 = m * 512 + t4 * 128
                xtile, bx = xt[xcnt[0] % 2]
                xcnt[0] += 1
                dma("sp", xtile[:], x_d[r0:r0 + 128, :], [], [bx])
                epilogue(lambda k, hf: ((mn[:, k, t4 * 128:(t4 + 1) * 128] if k < 4 else mg[:, k - 4, t4 * 128:(t4 + 1) * 128]),
                                        wout[:, k, hf * 512:(hf + 1) * 512], [bmn if k < 4 else bmg_, B_wout[k]]), 8,
                         xtile, bx, gate1, B_gate1, r0, pbase=(0 if t4 % 2 == 0 else 2))
                yield

        def epilogue(opnd, nk, xtile, bx, gate, bgate, r0, pbase=0):
            for hf in range(2):
                pb, bpb = ps[pbase + hf], B_ps[pbase + hf]
                for k in range(nk):
                    l_, r_, rb_ = opnd(k, hf)
                    mm(pb[:, :], l_, r_, k == 0, k == nk - 1, rb_, [bpb])
                act(xn[:, hf * 512:(hf + 1) * 512], pb[:, :], AF.Square, [bpb], [B_xn])
            P.add("dve", lambda e: e.tensor_reduce(out=colt[:, 4:5], in_=xn[:], axis=AX.X, op=ALU.add), [B_xn], [B_colt], cost=1200.0)
            rstd_col(colt[:, 6:7], colt[:, 4:5], float(D), [B_colt], colt[:, 5:6], B_colt, B_colt)
            for hf in range(2):
                stt(xn[:, hf * 512:(hf + 1) * 512], ps[pbase + hf][:, :], colt[:, 6:7], gate[:, hf * 512:(hf + 1) * 512], ALU.mult, ALU.mult,
                    [B_ps[pbase + hf], B_colt, bgate], [B_xn])
            tt("pool", xtile[:], xtile[:], xn[:], ALU.add, [bx, B_xn], [bx])
            dma("sp", out_d[r0:r0 + 128, :], xtile[:], [bx], [B_out[r0 // 128]])

        def stream_a(m):
            yield from na_block(m, m % 2)
            yield from out_block(m, m % 2)

        import os as _os2
        RA_ = int(_os2.environ.get('K_RA', '3'))
        EC_ = int(_os2.environ.get('K_EC', '6'))

        def run3(ga, gb, gc, flag, ra, every_c):
            live = [ga is not None, gb is not None, gc is not None]
            rnd = 0
            while any(live):
                if live[0]:
                    for _ in range(ra):
                        try:
                            next(ga)
                        except StopIteration:
                            live[0] = False
                            break
                if live[1]:
                    try:
                        next(gb)
                    except StopIteration:
                        live[1] = False
                if live[2] and (flag[0] or not live[1]) and (rnd % every_c == 0 or not (live[0] or live[1])):
                    try:
                        next(gc)
                    except StopIteration:
                        live[2] = False
                rnd += 1

        for _ in block_front(0, 0, True, parts=("norm",)):
            pass
        for i in range(NB + 1):
            if i < NB:
                for _ in block_front(i, 0, True, parts=("proj",)):
                    pass
            flag = [False]
            ga = stream_a(i - 1) if i >= 1 else None
            gb = None
            if i < NB:
                def stream_b(i=i, flag=flag):
                    yield from block_front(i, 0, True, parts=("rope",))
                    flag[0] = True
                    yield from gla_main(i, i % 2)
                gb = stream_b()
            gc = block_front(i + 1, 0, True, parts=("norm",)) if i + 1 < NB else None
            run3(ga, gb, gc, flag, RA_, EC_)

        if kstop <= 4:
            P.add("sp", lambda e: e.nop(), B_out, [])
            P.emit(nc, st)
            return nc
        B_wguq = [[Buf() for _ in range(4)] for _ in range(8)]
        B_fenceF = Buf()
        P.add("pool", lambda e: e.nop(), [], [B_fenceF] + B_arenaM)
        for q in (0, 2, 1, 3):
            for k in range(8):
                dma("pool", wgu[:, k, q * 1408:(q + 1) * 1408], wgu_d[k * 128:(k + 1) * 128, q * 1408:(q + 1) * 1408],
                    [B_fenceF], [B_wguq[k][q]])
        for j in range(NJ):
            dma("pool", wdn[:, j, :], wdn_d[j * 128:(j + 1) * 128, :], [B_fenceF], [B_wdn[j]])
        aT, B_aT = wk[:, :].rearrange("p (j n) -> p j n", j=NJ), Buf()
        h2b, B_h2b = wk2[:, 0:4096].rearrange("p (k n) -> p k n", k=8), Buf()
        xt2, B_xt2 = wk2[:, 4096:6144].bitcast(F32), Buf()
        P.add("dve", lambda e: e.memset(aT[:, 0, 0:2], 0.0), [], [B_aT, B_h2b, B_xt2] + B_wk + B_wk2)
        dma("sp", gate2[:], gscr_d[:, :], [B_gscr], [B_gate2])
        H2 = [hT[0], (h2b, B_h2b)]
        B_colt_n = Buf()

        def f_norm(i):
            h2, bh2 = H2[i % 2]
            for t4 in range(4):
                r0 = i * 512 + t4 * 128
                dma("sp", xt2, out_d[r0:r0 + 128, :], [B_out[r0 // 128]], [B_xt2])
                P.add("act", lambda e: e.activation(out=xn[:], in_=xt2, func=AF.Square, accum_out=colt[:, 0:1]), [B_xt2], [B_xn, B_colt_n],
                      cost=1100.0)
                rstd_col(colt[:, 2:3], colt[:, 0:1], float(D), [B_colt_n], colt[:, 1:2], B_colt_n, B_colt_n)
                ts("dve", xt2, xt2, colt[:, 2:3], None, ALU.mult, ALU.bypass, [B_xt2, B_colt_n], [B_xt2])
                yield
                for half in range(2):
                    pb, bpb = ps[2 + half], B_ps[2 + half]
                    for kk in range(4):
                        k = half * 4 + kk
                        tr(pb[:, kk * 128:(kk + 1) * 128], xt2[:, k * 128:(k + 1) * 128], identf[:], [B_xt2, B_identf], [bpb])
                    for kk in range(4):
                        k = half * 4 + kk
                        if kk % 2 == 0:
                            act(h2[:, k, t4 * 128:(t4 + 1) * 128], pb[:, kk * 128:(kk + 1) * 128], AF.Identity,
                                [bpb, B_gm2, B_sh2], [bh2], scale=gm2[:, k:k + 1], bias=sh2[:, k:k + 1])
                        else:
                            ts("dve", h2[:, k, t4 * 128:(t4 + 1) * 128], pb[:, kk * 128:(kk + 1) * 128], gm2[:, k:k + 1], sh2[:, k:k + 1],
                               ALU.mult, ALU.add, [bpb, B_gm2, B_sh2], [bh2])
                    yield

        def f_mm(i):
            h2, bh2 = H2[i % 2]
            for j in range(NJ):
                pg, bpg = ps[4 + (j % 2) * 2], B_ps[4 + (j % 2) * 2]
                pu, bpu = ps[5 + (j % 2) * 2], B_ps[5 + (j % 2) * 2]
                for k in range(8):
                    mm(pg[:, :], wgu[:, k, j * 128:(j + 1) * 128], h2[:, k, :], k == 0, k == 7, [B_wguq[k][(j * 128) // 1408], bh2], [bpg])
                for k in range(8):
                    mm(pu[:, :], wgu[:, k, FFN + j * 128:FFN + (j + 1) * 128], h2[:, k, :], k == 0, k == 7,
                       [B_wguq[k][(FFN + j * 128) // 1408], bh2], [bpu])
                fb, bfb = (f1, B_f1) if j % 2 == 0 else (f2, B_f2)
                act(fb[:], pg[:, :], AF.Silu, [bpg], [bfb])
                tt("dve", aT[:, j, :], pu[:, :], fb[:], ALU.mult, [bpu, bfb], [B_aT])
                yield
            for t4 in range(4):
                r0 = i * 512 + t4 * 128
                xtile, bx = xt[0]
                dma("sp", xtile[:], out_d[r0:r0 + 128, :], [B_out[r0 // 128]], [bx])
                epilogue(lambda k, hf: (aT[:, k, t4 * 128:(t4 + 1) * 128], wdn[:, k, hf * 512:(hf + 1) * 512], [B_aT, B_wdn[k]]), NJ,
                         xtile, bx, gate2, B_gate2, r0, pbase=(0 if t4 % 2 == 0 else 2))
                yield

        for _ in f_norm(0):
            pass
        for i in range(NB):
            run_interleaved(f_mm(i), f_norm(i + 1) if i + 1 < NB else None, 2)

        P.add("sp", lambda e: e.nop(), B_out, [])
        print("sbuf bytes remaining:", nc.sbuf_bytes_remaining)
        P.emit(nc, st)
    return nc


def host_inputs(L, x, c, ctx, c_ctx, w_mod, b_mod, norm_pre_mix, norm_post_mix, norm_pre_ffn, norm_post_ffn, w_in, na_rpb,
                gla_wa2_f, gla_ba_f, gla_wa2_b, gla_ba_b, gla_norm, w_out, w_gate_up, w_down):
    f = lambda a: np.ascontiguousarray(np.asarray(a, dtype=np.float32))
    B = x.shape[0]
    cosT, sinT, pm = _rope_tables(L)
    wa2blk = np.zeros((32, 512), np.float32)
    wa2blk[0:16, 0:256] = f(gla_wa2_f)[0]
    wa2blk[16:32, 256:512] = f(gla_wa2_b)[0]
    ba = np.concatenate([f(gla_ba_f)[0], f(gla_ba_b)[0]])
    bacol = np.ascontiguousarray(ba.reshape(4, 128).T)
    ggain = np.ascontiguousarray(f(gla_norm)[0].reshape(4, 128).T)
    tri = np.zeros((2, 128, 512), np.float32)
    s = np.arange(128)[:, None]
    t = np.arange(128)[None, :]
    tri[0] = np.tile((s <= t).astype(np.float32), (1, 4))
    tri[1] = np.tile((s >= t).astype(np.float32), (1, 4))
    smask = np.ones((128, 512), np.float32)
    smask[:, 0::128] = 0.0
    sel = np.zeros((2, 128), np.float32)
    sel[0, :] = 1.0
    common = {
        "w_mod": f(w_mod)[0], "b_mod": f(b_mod)[0][None, :],
        "g4": np.stack([f(norm_pre_mix)[0], f(norm_post_mix)[0], f(norm_pre_ffn)[0], f(norm_post_ffn)[0]]),
        "w_in": f(w_in)[0], "ettab": _et_tables(f(na_rpb)[0]), "wa2blk": wa2blk, "bacol": bacol, "glagain": ggain,
        "cosT": cosT, "sinT": sinT, "pm": pm, "identf": np.eye(128, dtype=np.float32), "tri4": tri, "scanmask": smask,
        "sel": sel, "w_out": f(w_out)[0], "w_gu": f(w_gate_up)[0], "w_down": f(w_down)[0],
    }
    maps = []
    for b in range(B):
        m = dict(common)
        m["x"] = f(x[b])
        m["ctx"] = f(ctx[b])
        m["cc"] = np.ascontiguousarray(np.stack([f(c[b]), f(c_ctx)], axis=1))
        maps.append(m)
    return maps


def kernel(**inputs):
    x = np.asarray(inputs["x"])
    L = x.shape[1]
    maps = host_inputs(L, **inputs)
    nc = build(L)
    res = run_bass_kernel_spmd(nc, maps, core_ids=list(range(len(maps))))
    return np.stack([np.asarray(r["out"], dtype=np.float32) for r in res.results], axis=0)
```

```python
import numpy as np
import ml_dtypes
from contextlib import ExitStack
import concourse.bass as bass
import concourse.mybir as mybir
from concourse.bass_utils import run_bass_kernel_spmd

F32 = mybir.dt.float32
BF16 = mybir.dt.bfloat16
AF = mybir.ActivationFunctionType
ALU = mybir.AluOpType
AX = mybir.AxisListType

D = 1024
NKC = 8
CTX = 256
FFN = 2816
NJ = FFN // 128
EPS = 1e-6
NEG = -30000.0


class Buf:
    __slots__ = ("name", "w", "rs")

    def __init__(self, name=""):
        self.name = name
        self.w = None
        self.rs = []


class Op:
    __slots__ = ("eng", "fn", "deps", "seq", "sig", "semval", "is_dma", "sem", "waits", "idx", "hz", "cost", "lat", "succ", "nin", "rt", "fin")


class Prog:
    ENGS = ["pe", "act", "dve", "pool", "sp"]

    def __init__(self, ndma_sems=14):
        self.ops = []
        self.eng_ops = {e: [] for e in self.ENGS}
        self.dma_count = {e: 0 for e in self.ENGS}
        self.dma_hist = {e: [] for e in self.ENGS}
        self.ndma = ndma_sems
        self.do_sched = True

    def add(self, eng, fn, reads=(), writes=(), dma=False, cost=300.0, lat=0.0):
        op = Op()
        op.eng = eng
        op.fn = fn
        op.is_dma = dma
        op.sig = False
        op.idx = len(self.ops)
        op.cost = cost
        op.lat = lat
        deps = []
        hz = []
        for b in reads:
            w = b.w
            if w is not None:
                hz.append(w)
                if w.is_dma or dma or w.eng != eng or eng != "pe":
                    deps.append(w)
        for b in writes:
            w = b.w
            if w is not None:
                hz.append(w)
                if w.is_dma or dma or w.eng != eng or eng != "pe":
                    deps.append(w)
            for r in b.rs:
                hz.append(r)
                if r.is_dma or dma or r.eng != eng or eng != "pe":
                    deps.append(r)
        op.deps = deps
        op.hz = hz
        for b in reads:
            b.rs.append(op)
        for b in writes:
            b.w = op
            b.rs = []
        self.ops.append(op)
        self.eng_ops[eng].append(op)
        return op

    def schedule(self, window=40, xlat=150.0):
        ops = self.ops
        for op in ops:
            op.succ = []
            op.rt = 0.0
        for op in ops:
            seen = set()
            n = 0
            for d in op.hz:
                if d is op or d.idx in seen:
                    continue
                seen.add(d.idx)
                d.succ.append(op)
                n += 1
            op.nin = n
        pend = {e: list(self.eng_ops[e]) for e in self.ENGS}
        head = {e: 0 for e in self.ENGS}
        done = [False] * len(ops)
        tcl = {e: 0.0 for e in self.ENGS}
        order = {e: [] for e in self.ENGS}
        remaining = len(ops)
        while remaining:
            best = None
            for e in self.ENGS:
                lst = pend[e]
                h = head[e]
                while h < len(lst) and done[lst[h].idx]:
                    h += 1
                head[e] = h
                cnt = 0
                k = h
                te = tcl[e]
                while k < len(lst) and cnt < window:
                    o = lst[k]
                    k += 1
                    if done[o.idx]:
                        continue
                    cnt += 1
                    if o.nin == 0:
                        st = o.rt if o.rt > te else te
                        if best is None or st < best[0] or (st == best[0] and o.idx < best[1].idx):
                            best = (st, o)
                            if st <= te and cnt == 1:
                                break
            st, o = best
            e = o.eng
            done[o.idx] = True
            remaining -= 1
            order[e].append(o)
            if o.is_dma:
                tcl[e] = st + 100.0
                o.fin = st + 100.0 + o.lat
            else:
                tcl[e] = st + o.cost
                o.fin = st + o.cost
            for sc in o.succ:
                sc.nin -= 1
                r = o.fin + (xlat if (sc.eng != e or o.is_dma) else 0.0)
                if r > sc.rt:
                    sc.rt = r
        self.eng_ops = order
        self.est_ns = max(tcl.values())

    def finalize(self):
        if self.do_sched:
            self.schedule()
        for e in self.ENGS:
            hist = []
            for i_, op in enumerate(self.eng_ops[e]):
                op.seq = i_
                if op.is_dma:
                    n = len(hist)
                    if n >= self.ndma:
                        op.deps.append(hist[n - self.ndma])
                    op.sem = n % self.ndma
                    op.semval = 16 * (n // self.ndma + 1)
                    hist.append(op)
            self.dma_count[e] = len(hist)
        waited = {e: {p: -1 for p in self.ENGS} for e in self.ENGS}
        waited_dma = {e: set() for e in self.ENGS}
        for op in [o for e in self.ENGS for o in self.eng_ops[e]]:
            need = {}
            need_dma = []
            for d in op.deps:
                if d is op:
                    continue
                if d.is_dma:
                    if d.idx not in waited_dma[op.eng]:
                        waited_dma[op.eng].add(d.idx)
                        need_dma.append(d)
                else:
                    if d.seq > waited[op.eng][d.eng]:
                        if d.eng not in need or need[d.eng].seq < d.seq:
                            need[d.eng] = d
            ws = []
            for p, d in need.items():
                waited[op.eng][p] = d.seq
                d.sig = True
                ws.append(d)
            ws.extend(need_dma)
            op.waits = ws
        for e in self.ENGS:
            c = 0
            for op in self.eng_ops[e]:
                if not op.is_dma and op.sig:
                    c += 1
                    op.semval = c

    def emit(self, nc, stack):
        self.finalize()
        esem = {e: stack.enter_context(nc.semaphore("s_" + e)) for e in self.ENGS}
        dsem = {e: [stack.enter_context(nc.semaphore(f"d_{e}{i}")) for i in range(self.ndma)]
                for e in self.ENGS if self.dma_count[e] > 0}
        block = stack.enter_context(nc.Block())

        def run(e, eng):
            for op in self.eng_ops[e]:
                for d in op.waits:
                    if d.is_dma:
                        eng.wait_ge(dsem[d.eng][d.sem], d.semval)
                    else:
                        eng.wait_ge(esem[d.eng], d.semval)
                ins = op.fn(eng)
                if op.is_dma:
                    ins.then_inc(dsem[e][op.sem], 16)
                elif op.sig:
                    ins.then_inc(esem[e], 1)

        @block.tensor
        def _(eng):
            run("pe", eng)

        @block.scalar
        def _(eng):
            run("act", eng)

        @block.vector
        def _(eng):
            run("dve", eng)

        @block.gpsimd
        def _(eng):
            run("pool", eng)

        @block.sync
        def _(eng):
            run("sp", eng)


def _rope_tables(L):
    half = 16
    inv = (10000.0 ** (-np.arange(half, dtype=np.float32) / half)).astype(np.float32)
    t = np.arange(L)
    cosT = np.zeros((128, L), np.float32)
    sinT = np.zeros((128, L), np.float32)
    for p in range(128):
        d = p % 64
        pos = (t // 64) if d < 32 else (t % 64)
        dd = d % 32
        j = dd % 16
        ang = pos.astype(np.float32) * inv[j]
        cosT[p] = np.cos(ang)
        s = np.sin(ang)
        sinT[p] = -s if dd < 16 else s
    pm = np.zeros((128, 128), np.float32)
    for m in range(128):
        d = m % 64
        dd = d % 32
        partner = m + 16 if dd < 16 else m - 16
        pm[partner, m] = 1.0
    return cosT, sinT, pm


def _et_tables(rpb):
    H = rpb.shape[0]
    out = np.full((H, 128, 2, 16, 64), NEG, np.float32)
    c = np.arange(64)
    cs = np.clip(c - 8, 0, 48)
    for p in range(128):
        half, kc = p // 64, p % 64
        colvalid = (kc >= cs) & (kc <= cs + 15)
        cidx = np.clip(kc - c + 15, 0, 30)
        for j in range(16):
            dr = 14 - j + half
            if dr < 0 or dr > 14:
                continue
            vals = rpb[:, dr, :][:, cidx]
            vals = np.where(colvalid[None, :], vals, NEG)
            out[:, p, 0, j, :] = vals
            if 3 <= dr <= 10:
                out[:, p, 1, j, :] = vals
    return out.reshape(H, 128, 2048)


def na_plan(ROWS, m):
    def rs_of(r):
        return min(max(r - 4, 0), ROWS - 8)
    plan = []
    for t in range(ROWS // 2):
        rows = [r for r in range(8 * m, 8 * m + 8) if not (2 * t + 1 < rs_of(r) or 2 * t > rs_of(r) + 7)]
        if not rows:
            continue
        runs = []
        for r in rows:
            ty = 1 if (4 <= r <= ROWS - 4) else 0
            if runs and runs[-1][0] == ty and runs[-1][2] == r - 1:
                runs[-1][2] = r
            else:
                runs.append([ty, r, r])
        plan.append((t, runs, rows[0], rows[-1]))
    return plan


def build(L, kstop=99):
    ROWS = L // 64
    NB = L // 512
    NT = L // 128
    nc = bass.Bass("TRN2", target_bir_lowering=False)
    dr = lambda name, shape, dt, kind="ExternalInput": nc.dram_tensor(name, shape, dt, kind=kind).ap()
    x_d = dr("x", [L, D], F32)
    ctx_d = dr("ctx", [CTX, D], F32)
    cc_d = dr("cc", [D, 2], F32)
    wmod_d = dr("w_mod", [D, 6 * D], F32)
    bmod_d = dr("b_mod", [1, 6 * D], F32)
    g4_d = dr("g4", [4, D], F32)
    win_d = dr("w_in", [D, 3104], F32)
    et_d = dr("ettab", [8, 128, 2048], F32)
    wa2_d = dr("wa2blk", [32, 512], F32)
    ba_d = dr("bacol", [128, 4], F32)
    gg_d = dr("glagain", [128, 4], F32)
    cos_d = dr("cosT", [128, L], F32)
    sin_d = dr("sinT", [128, L], F32)
    pm_d = dr("pm", [128, 128], F32)
    idf_d = dr("identf", [128, 128], F32)
    tri_d = dr("tri4", [2, 128, 512], F32)
    smask_d = dr("scanmask", [128, 512], F32)
    sel_d = dr("sel", [2, 128], F32)
    wout_d = dr("w_out", [D, D], F32)
    wgu_d = dr("w_gu", [D, 2 * FFN], F32)
    wdn_d = dr("w_down", [FFN, D], F32)
    out_d = dr("out", [L, D], F32, kind="ExternalOutput")
    sbscr_d = dr("sbscr", [NT, 128, 512], BF16, kind="Internal")
    etscr_d = dr("etscr", [8, 128, 2048], BF16, kind="Internal")
    gscr_d = dr("gscr", [128, 1024], F32, kind="Internal")

    P = Prog()
    st = ExitStack()
    with st:
        def sb(name, shape, dt):
            return st.enter_context(nc.sbuf_tensor("sb_" + name, shape, dt))

        ARENA = 67840
        arena = sb("arena", [128, ARENA], BF16)
        win = arena[:, 0:24832].rearrange("p (k n) -> p k n", k=8)
        wout = arena[:, 24832:33024].rearrange("p (k n) -> p k n", k=8)
        ETR = [arena[:, 33024 + i * 2048: 33024 + (i + 1) * 2048] for i in range(2)]
        wmring = [arena[:, 37120 + i * 8192: 37120 + (i + 1) * 8192].bitcast(F32).rearrange("p (k n) -> p k n", k=8)
                  for i in range(2)]
        VR = [arena[:, 37120 + i * 1024: 37120 + (i + 1) * 1024].rearrange("p (h t d) -> p h t d", h=8, t=2)
              for i in range(12)]
        o_ = 37120 + 12 * 1024
        KTR = [arena[:, o_ + i * 2048: o_ + (i + 1) * 2048].rearrange("p (j n) -> p j n", j=4) for i in range(3)]
        o_ += 3 * 2048
        QTR = [arena[:, o_ + i * 2048: o_ + (i + 1) * 2048].rearrange("p (j n) -> p j n", j=4) for i in range(2)]
        o_ += 2 * 2048
        qdec_v = [arena[:, o_ + i * 512: o_ + (i + 1) * 512] for i in range(4)]
        o_ += 2048
        kinv_v = [arena[:, o_ + i * 1024: o_ + (i + 1) * 1024].rearrange("p (h n) -> p h n", h=2) for i in range(4)]
        o_ += 4096
        kendT = arena[:, o_: o_ + 2048].rearrange("p (c n) -> p c n", c=4)
        o_ += 2048
        assert o_ <= ARENA, o_
        wgu = arena[:, 0:45056].rearrange("p (k n) -> p k n", k=8)
        wdn = arena[:, 45056:67584].rearrange("p (j n) -> p j n", j=NJ)
        B_win = [Buf() for _ in range(8)]
        B_wout = [Buf() for _ in range(8)]
        B_ETR = [Buf(), Buf()]
        B_etscr = [Buf() for _ in range(8)]
        B_gscr = Buf()
        B_wm = [Buf(), Buf()]
        B_VR = [Buf() for _ in range(12)]
        B_KTR = [Buf() for _ in range(3)]
        B_QTR = [Buf() for _ in range(2)]
        qdec = [(qdec_v[i], Buf()) for i in range(4)]
        kinv = [(kinv_v[i], Buf()) for i in range(4)]
        B_kendT = Buf()
        B_arenaM = (B_win + B_wout + B_ETR + B_wm + B_VR + B_KTR + B_QTR + [b for _, b in qdec + kinv] + [B_kendT])
        B_wgu = [Buf() for _ in range(8)]
        B_wdn = [Buf() for _ in range(NJ)]

        ps = [st.enter_context(nc.psum_tensor(f"ps{i}", [128, 512], F32)) for i in range(8)]
        B_ps = [Buf(f"ps{i}") for i in range(8)]

        def T(name, shape, dt):
            return sb(name, shape, dt), Buf(name)

        identf, B_identf = T("identf", [128, 128], F32)
        identb, B_identb = T("identb", [128, 128], BF16)
        pm, B_pm = T("pm", [128, 128], BF16)
        onesb, B_onesb = T("onesb", [128, 128], BF16)
        tri, B_tri = T("tri", [128, 2, 512], BF16)
        smask, B_smask = T("smask", [128, 512], BF16)
        sel, B_sel = T("sel", [2, 128], F32)
        wa2, B_wa2 = T("wa2", [32, 512], BF16)
        negba, B_negba = T("negba", [128, 4], F32)
        ggain, B_ggain = T("ggain", [128, 4], F32)
        ccs, B_ccs = T("ccs", [128, 8, 2], F32)
        modcol, B_modcol = T("modcol", [128, 48, 2], F32)
        gcol, B_gcol = T("gcol", [128, 4, 8], F32)
        gm1, B_gm1 = T("gm1", [128, 8], F32)
        cgm1, B_cgm1 = T("cgm1", [128, 8], F32)
        gm2, B_gm2 = T("gm2", [128, 8], F32)
        sh1, B_sh1 = T("sh1", [128, 8], F32)
        csh1, B_csh1 = T("csh1", [128, 8], F32)
        sh2, B_sh2 = T("sh2", [128, 8], F32)
        gate1, B_gate1 = T("gate1", [128, 1024], F32)
        gate2, B_gate2 = gate1, B_gate1
        xt = [T(f"xt{i}", [128, 1024], F32) for i in range(1)] * 2
        xn, B_xn = T("xn", [128, 1024], F32)
        modrow, B_modrow = xt[0][0][0:2, 0:512], xt[0][1]
        bmg, B_bmg = xt[0][0][0:2, 512:1024], xt[0][1]
        colt, B_colt = T("colt", [128, 8], F32)
        hT = [T(f"hT{i}", [128, 8, 512], BF16) for i in range(1)] * 2
        wk = sb("wk", [128, 11264], BF16)
        mixN = (wk[:, 0:2048].rearrange("p (k n) -> p k n", k=4), Buf())
        mixG1_t = sb("mixG1", [128, 4, 512], BF16)
        mixG = [(wk[:, 2048:4096].rearrange("p (k n) -> p k n", k=4), Buf()), (mixG1_t, Buf())]
        srT, B_srT = wk[:, 4096:6144].rearrange("p (k n) -> p k n", k=4), Buf()
        gvt, B_gvt = wk[:, 6144:8192].rearrange("p (k n) -> p k n", k=4), Buf()
        KcT, B_KcT = wk[:, 8192:9216].rearrange("p (k n) -> p k n", k=4), Buf()
        Vc = [(wk[:, 9216 + i * 1024: 9216 + (i + 1) * 1024].rearrange("p (h t d) -> p h t d", h=8, t=2), Buf()) for i in range(2)]
        B_wk = [mixN[1], mixG[0][1], B_srT, B_gvt, B_KcT, Vc[0][1], Vc[1][1]]
        wk2 = sb("wk2", [128, 13 * 512], BF16)
        _w2 = [(wk2[:, i * 512:(i + 1) * 512], Buf()) for i in range(13)]
        Pt = [_w2[0], _w2[1]] * 2
        graw = [_w2[2]] * 4
        grope = [_w2[3], _w2[4], _w2[5], _w2[6]]
        B_wk2 = [b for _, b in _w2]
        f1, B_f1 = T("f1", [128, 512], F32)
        f2, B_f2 = T("f2", [128, 512], F32)
        rd, B_rd = xn, B_xn
        E1, B_E1 = T("E1", [128, 512], F32)
        E2, B_E2 = E1, B_E1
        cosb, B_cosb = E1, B_E1
        sinb, B_sinb = f2, B_f2
        afT, B_afT = T("afT", [32, 512], BF16)
        kend = [_w2[7]] * 4
        dec, B_dec = T("dec", [128, 4, 4], F32)
        Sst = [T(f"Sst{i}", [128, 128], F32) for i in range(4)]
        SbfZ, B_SbfZ = _w2[8]
        Asb = [_w2[9], _w2[10]]
        sq, B_sq = Asb[0]

        def _n(ap):
            sh = ap.shape
            n = 1
            for v in sh[1:]:
                n *= int(v)
            return n

        def _ec(eng, n):
            return {"act": 220.0 + 0.75 * n, "dve": 70.0 + 1.25 * n, "pool": 150.0 + 1.9 * n}[eng]

        def dma(eng, out, in_, reads, writes, **kw):
            nbytes = _n(out) * int(out.shape[0]) * 4
            return P.add(eng, lambda e: e.dma_start(out=out, in_=in_, **kw), reads, writes, dma=True, lat=2000.0 + nbytes / 150.0)

        def mm(out, lhsT, rhs, start, stop, reads, writes):
            c = max(64, _n(rhs)) * 0.42 + 8.0
            if rhs.dtype == F32:
                c *= 4.0
            return P.add("pe", lambda e: e.matmul(out=out, lhsT=lhsT, rhs=rhs, start=start, stop=stop), reads, writes, cost=c)

        def tr(out, in_, ident, reads, writes):
            return P.add("pe", lambda e: e.transpose(out=out, in_=in_, identity=ident), reads, writes, cost=130.0)

        def act(out, in_, func, reads, writes, scale=1.0, bias=0.0):
            return P.add("act", lambda e: e.activation(out=out, in_=in_, func=func, bias=bias, scale=scale), reads, writes,
                         cost=_ec("act", _n(out)))

        def tt(eng, out, in0, in1, op, reads, writes):
            return P.add(eng, lambda e: e.tensor_tensor(out=out, in0=in0, in1=in1, op=op), reads, writes, cost=_ec(eng, _n(out)))

        def ts(eng, out, in0, s1, s2, op0, op1, reads, writes):
            return P.add(eng, lambda e: e.tensor_scalar(out=out, in0=in0, scalar1=s1, scalar2=s2, op0=op0, op1=op1), reads, writes,
                         cost=_ec(eng, _n(out)))

        def stt(out, in0, scalar, in1, op0, op1, reads, writes):
            return P.add("dve", lambda e: e.scalar_tensor_tensor(out=out, in0=in0, scalar=scalar, in1=in1, op0=op0, op1=op1), reads, writes,
                         cost=_ec("dve", _n(out)))

        def cp(eng, out, in_, reads, writes):
            if eng == "act":
                return P.add("act", lambda e: e.copy(out=out, in_=in_), reads, writes, cost=_ec("act", _n(out)))
            return P.add(eng, lambda e: e.tensor_copy(out=out, in_=in_), reads, writes, cost=_ec(eng, _n(out)))

        def memset(eng, ap, val, writes):
            return P.add(eng, lambda e: e.memset(ap, val), (), writes, cost=_ec(eng, _n(ap)) * 0.5)

        def rstd_col(out_col, ss_col, n, reads_b, tmp_col, tmpB, outB):
            act(tmp_col, ss_col, AF.Ln, reads_b + [tmpB], [tmpB], scale=1.0 / n, bias=EPS)
            act(out_col, tmp_col, AF.Exp, [tmpB], [outB], scale=-0.5)

        dma("sp", identf[:], idf_d[:, :], [], [B_identf])
        dma("pool", identb[:], idf_d[:, :], [], [B_identb])
        dma("pool", pm[:], pm_d[:, :], [], [B_pm])
        memset("pool", onesb[:], 1.0, [B_onesb])
        dma("pool", tri[:, 0, :], tri_d[0], [], [B_tri])
        dma("pool", tri[:, 1, :], tri_d[1], [], [B_tri])
        dma("pool", smask[:], smask_d[:, :], [], [B_smask])
        dma("sp", sel[:], sel_d[:, :], [], [B_sel])
        dma("pool", wa2[:], wa2_d[:, :], [], [B_wa2])
        dma("sp", negba[:], ba_d[:, :], [], [B_negba])
        ts("dve", negba[:], negba[:], -1.0, None, ALU.mult, ALU.bypass, [B_negba], [B_negba])
        dma("sp", ggain[:], gg_d[:, :], [], [B_ggain])
        dma("sp", ccs[:], cc_d.rearrange("(k p) j -> p k j", p=128), [], [B_ccs], allow_slow_non_contiguous=True)
        dma("sp", gcol[:], g4_d.rearrange("a (k p) -> p a k", p=128), [], [B_gcol], allow_slow_non_contiguous=True)
        act(ccs[:], ccs[:], AF.Silu, [B_ccs], [B_ccs])
        for v_, b_ in Vc:
            memset("pool", v_[:], 1.0, [b_])

        for k in range(8):
            for hf in range(2):
                dma("pool", win[:, k, hf * 1552:(hf + 1) * 1552], win_d[k * 128:(k + 1) * 128, hf * 1552:(hf + 1) * 1552],
                    [], [B_win[k]] if hf == 0 else [B_win[k]])
        for k in range(8):
            dma("pool", wout[:, k, :], wout_d[k * 128:(k + 1) * 128, :], [], [B_wout[k]])

        B_modc_ps = B_ps[7]
        modc_ps = ps[7]
        for g in range(12):
            wm, bwm = wmring[g % 2], B_wm[g % 2]
            dma("sp", wm, wmod_d[:, g * 512:(g + 1) * 512].rearrange("(k p) n -> p k n", p=128), [], [bwm])
            dma("sp", bmg, bmod_d[0:1, g * 512:(g + 1) * 512].partition_broadcast(2), [], [B_bmg])
            for k in range(8):
                mm(ps[0][0:2, :], ccs[:, k, :], wm[:, k, :], k == 0, k == 7, [B_ccs, bwm], [B_ps[0]])
            tt("dve", modrow, ps[0][0:2, :], bmg, ALU.add, [B_ps[0], B_bmg], [B_modrow])
            for s in range(4):
                ci = g * 4 + s
                mm(modc_ps[:, ci * 2:ci * 2 + 2], modrow[:, s * 128:(s + 1) * 128], identf[0:2, 0:2], True, True,
                   [B_modrow, B_identf], [B_modc_ps])
            if g in (4, 5, 10, 11):
                hf = g % 2
                gi = 1 if g < 8 else 3
                mm(ps[1][:, :], sel[0:2, :], modrow, True, True, [B_sel, B_modrow], [B_ps[1]])
                dma("sp", xn[:, 0:512], g4_d[gi:gi + 1, hf * 512:(hf + 1) * 512].partition_broadcast(128), [], [B_xn])
                if g < 8:
                    tt("dve", gate1[:, hf * 512:(hf + 1) * 512], ps[1][:, :], xn[:, 0:512], ALU.mult, [B_ps[1], B_xn], [B_gate1])
                else:
                    tt("dve", xn[:, 512:1024], ps[1][:, :], xn[:, 0:512], ALU.mult, [B_ps[1], B_xn], [B_xn])
                    dma("sp", gscr_d[:, hf * 512:(hf + 1) * 512], xn[:, 512:1024], [B_xn], [B_gscr])
        cp("dve", modcol[:].rearrange("p a b -> p (a b)"), modc_ps[:, 0:96], [B_modc_ps], [B_modcol])
        cp("dve", sh1[:], modcol[:, 0:8, 0], [B_modcol], [B_sh1])
        cp("dve", csh1[:], modcol[:, 0:8, 1], [B_modcol], [B_csh1])
        cp("dve", sh2[:], modcol[:, 24:32, 0], [B_modcol], [B_sh2])
        stt(gm1[:], modcol[:, 8:16, 0], 1.0, gcol[:, 0, :], ALU.add, ALU.mult, [B_modcol, B_gcol], [B_gm1])
        stt(cgm1[:], modcol[:, 8:16, 1], 1.0, gcol[:, 0, :], ALU.add, ALU.mult, [B_modcol, B_gcol], [B_cgm1])
        stt(gm2[:], modcol[:, 32:40, 0], 1.0, gcol[:, 2, :], ALU.add, ALU.mult, [B_modcol, B_gcol], [B_gm2])

        for h in range(8):
            stg = wmring[h % 2].rearrange("p k n -> p (k n)")[:, 0:2048]
            dma("sp", stg, et_d[h], [], [B_wm[h % 2]])
            act(ETR[h % 2], stg, AF.Exp, [B_wm[h % 2]], [B_ETR[h % 2]])
            dma("sp", etscr_d[h], ETR[h % 2], [B_ETR[h % 2]], [B_etscr[h]])

        for dj_ in range(4):
            memset("pool", kinv[dj_][0], 0.0, [kinv[dj_][1]])
        memset("pool", SbfZ[:], 0.0, [B_SbfZ])
        for i_ in range(12):
            P.add("pool", lambda e, i_=i_: e.memset(VR[i_][:, :, 1, :], 1.0), [], [B_VR[i_]] + B_wm + B_KTR + B_QTR)

        xcnt = [0]

        def norm_transpose(src_rows_ap, gm, bgm, shc, bsh, dstT, bdst, col0):
            xtile, bx = xt[xcnt[0] % 2]
            xcnt[0] += 1
            dma("sp", xtile[:], src_rows_ap, [], [bx])
            P.add("act", lambda e: e.activation(out=xn[:], in_=xtile[:], func=AF.Square, accum_out=colt[:, 0:1]), [bx], [B_xn, B_colt],
                  cost=1100.0)
            rstd_col(colt[:, 2:3], colt[:, 0:1], float(D), [B_colt], colt[:, 1:2], B_colt, B_colt)
            ts("dve", xn[:], xtile[:], colt[:, 2:3], None, ALU.mult, ALU.bypass, [bx, B_colt], [B_xn])
            for half in range(2):
                pb, bpb = ps[2 + half], B_ps[2 + half]
                for kk in range(4):
                    k = half * 4 + kk
                    tr(pb[:, kk * 128:(kk + 1) * 128], xn[:, k * 128:(k + 1) * 128], identf[:], [B_xn, B_identf], [bpb])
                for kk in range(4):
                    k = half * 4 + kk
                    if kk % 2 == 0:
                        act(dstT[:, k, col0:col0 + 128], pb[:, kk * 128:(kk + 1) * 128], AF.Identity,
                            [bpb, bgm, bsh], [bdst], scale=gm[:, k:k + 1], bias=shc[:, k:k + 1])
                    else:
                        ts("dve", dstT[:, k, col0:col0 + 128], pb[:, kk * 128:(kk + 1) * 128], gm[:, k:k + 1], shc[:, k:k + 1],
                           ALU.mult, ALU.add, [bpb, bgm, bsh], [bdst])
            return xtile, bx

        pcnt = [0]

        def proj_fm(hTt, bh, col, M, N):
            pb, bpb = ps[pcnt[0] % 2], B_ps[pcnt[0] % 2]
            pcnt[0] += 1
            for k in range(8):
                mm(pb[0:M, 0:N], win[:, k, col:col + M], hTt[:, k, 0:N], k == 0, k == 7, [B_win[k], bh], [bpb])
            return pb, bpb

        def proj_tm(hTt, bh, tcol, col):
            pb, bpb = ps[pcnt[0] % 2], B_ps[pcnt[0] % 2]
            pcnt[0] += 1
            for k in range(8):
                mm(pb[:, :], hTt[:, k, tcol:tcol + 128], win[:, k, col:col + 512], k == 0, k == 7, [B_win[k], bh], [bpb])
            return pb, bpb

        def gla_gates(dj, N, backward, afsrc=None):
            af_, baf_ = afsrc if afsrc is not None else (afT, B_afT)
            pz, bpz = ps[7], B_ps[7]
            mm(pz[:, 0:N], wa2[:, dj * 128:(dj + 1) * 128], af_[:, 0:N], True, True, [B_wa2, baf_], [bpz])
            act(f1[:, 0:N], pz[:, 0:N], AF.Exp, [bpz, B_negba], [B_f1], scale=-1.0, bias=negba[:, dj:dj + 1])
            act(f1[:, 0:N], f1[:, 0:N], AF.Ln, [B_f1], [B_f1], scale=1.0, bias=1.0)
            P.add("dve", lambda e: e.tensor_tensor_scan(out=f2[:, 0:N], data0=smask[:, 0:N], data1=f1[:, 0:N], initial=0.0,
                                                         op0=ALU.mult, op1=ALU.add), [B_smask, B_f1], [B_f2], cost=70.0 + 2.35 * N)
            X, BX = f2, B_f2
            if backward:
                tt("dve", f1[:, 0:N], f1[:, 0:N], f2[:, 0:N], ALU.subtract, [B_f1, B_f2], [B_f1])
                for c in range(N // 128):
                    ts("dve", f1[:, c * 128:(c + 1) * 128], f1[:, c * 128:(c + 1) * 128], f2[:, c * 128 + 127:c * 128 + 128], None,
                       ALU.add, ALU.bypass, [B_f1, B_f2], [B_f1])
                X, BX = f1, B_f1
            act(E1[:, 0:N], X[:, 0:N], AF.Exp, [BX], [B_E1], scale=-1.0 / 16.0)
            for c in range(N // 128):
                col = c * 128 if backward else c * 128 + 127
                cp("dve", dec[:, dj, c:c + 1], E1[:, col:col + 1], [B_E1], [B_dec])
            return X, BX

        def gla_k(dj, N, ksrc, bk, want_kinv, kend_too=True, X=None, BX=None):
            act(E2[:, 0:N], X[:, 0:N], AF.Exp, [BX], [B_E2], scale=1.0 / 16.0)
            if want_kinv:
                for hh in range(2):
                    tt("dve", kinv[dj][0][hh * 64:(hh + 1) * 64, hh, 0:N], ksrc[hh * 64:(hh + 1) * 64, 0:N], E2[hh * 64:(hh + 1) * 64, 0:N],
                       ALU.mult, [bk, B_E2], [kinv[dj][1]])
            if not kend_too:
                return
            for c in range(N // 128):
                stt(kend[dj][0][:, c * 128:(c + 1) * 128], ksrc[:, c * 128:(c + 1) * 128], dec[:, dj, c:c + 1],
                    E2[:, c * 128:(c + 1) * 128], ALU.mult, ALU.mult, [bk, B_dec, B_E2], [kend[dj][1]])
            gla_kendT1(dj, N)

        def gla_kendT(djs, N):
            pass

        def gla_kendT1(dj, N):
            pb, bpb = ps[5], B_ps[5]
            pbb = pb[:, :].bitcast(BF16)
            for c in range(N // 128):
                tr(pbb[:, c * 128:(c + 1) * 128], kend[dj][0][:, c * 128:(c + 1) * 128], identb[:], [kend[dj][1], B_identb], [bpb])
            for c in range(N // 128):
                cp("act" if c % 2 == 0 else "dve", kendT[:, c, dj * 128:(dj + 1) * 128], pbb[:, c * 128:(c + 1) * 128], [bpb], [B_kendT])

        def gla_kv(c, djs, vtile, bv):
            pb, bpb = ps[7], B_ps[7]
            for dj in djs:
                pair = dj % 2
                for hh in range(2):
                    head = pair * 2 + hh
                    mm(pb[hh * 64:(hh + 1) * 64, dj * 128:(dj + 1) * 128], kendT[:, c, dj * 128 + hh * 64: dj * 128 + hh * 64 + 64],
                       vtile[:, head * 128:(head + 1) * 128], True, True, [B_kendT, bv], [bpb])
            return pb, bpb

        def scan_update(dj, c, pkv, bpkv):
            s_, bs = Sst[dj]
            stt(s_[:], s_[:], dec[:, dj, c:c + 1], pkv[:, dj * 128:(dj + 1) * 128], ALU.mult, ALU.add, [bs, B_dec, bpkv], [bs])

        B_out = [Buf() for _ in range(NT)]
        if kstop <= 1:
            P.add("sp", lambda e: e.nop(), B_out, [])
            P.emit(nc, st)
            return nc
        hcT, B_hcT = hT[1]
        for tt_ in range(2):
            norm_transpose(ctx_d[tt_ * 128:(tt_ + 1) * 128, :], cgm1, B_cgm1, csh1, B_csh1, hcT, B_hcT, tt_ * 128)
        for j in range(4):
            pb, bpb = proj_fm(hcT, B_hcT, 512 + j * 128, 128, 256)
            cp("act", KcT[:, j, :], pb[:, 0:256], [bpb], [B_KcT])
        for tt_ in range(2):
            pb, bpb = proj_tm(hcT, B_hcT, tt_ * 128, 1024)
            cp("act", Vc[tt_][0][:, :, 0, :], pb[:, :].rearrange("p (h d) -> p h d", h=8), [bpb], [Vc[tt_][1]])
            pb, bpb = proj_tm(hcT, B_hcT, tt_ * 128, 2048)
            cp("dve", gvt[:, tt_, :], pb[:, :], [bpb], [B_gvt])
        for j in range(2):
            pb, bpb = proj_fm(hcT, B_hcT, 1792 + j * 128, 128, 256)
            cp("act", grope[2 + j][0][:, 0:256], pb[:, 0:256], [bpb], [grope[2 + j][1]])
        pb, bpb = proj_fm(hcT, B_hcT, 3072, 32, 256)
        cp("act", afT[:, 0:256], pb[0:32, 0:256], [bpb], [B_afT])
        for dj in range(4):
            memset("pool", Sst[dj][0][:], 0.0, [Sst[dj][1]])
        for dj in range(4):
            X_, BX_ = gla_gates(dj, 256, dj >= 2)
            gla_k(dj, 256, grope[2 + dj % 2][0], grope[2 + dj % 2][1], False, X=X_, BX=BX_)
        gla_kendT([0, 1, 2, 3], 256)
        for c in range(2):
            pkv, bpkv = gla_kv(c, [0, 1], gvt[:, c, :], B_gvt)
            for dj in (0, 1):
                scan_update(dj, c, pkv, bpkv)
        for c in (1, 0):
            pkv, bpkv = gla_kv(c, [2, 3], gvt[:, c, :], B_gvt)
            for dj in (2, 3):
                scan_update(dj, c, pkv, bpkv)

        def run_interleaved(ga, gb, ratio):
            a_live, b_live = ga is not None, gb is not None
            while a_live or b_live:
                if a_live:
                    for _ in range(ratio):
                        try:
                            next(ga)
                        except StopIteration:
                            a_live = False
                            break
                if b_live:
                    try:
                        next(gb)
                    except StopIteration:
                        b_live = False

        def block_front(i, slot, full, bset=None, parts=("norm", "proj", "rope")):
            gvt_, bgvt_, afT_, bafT_, gro_ = bset if bset is not None else (gvt, B_gvt, afT, B_afT, grope)
            hTt, bh = hT[slot]
            if "norm" in parts:
                for t4 in range(4):
                    r0 = i * 512 + t4 * 128
                    norm_transpose(x_d[r0:r0 + 128, :], gm1, B_gm1, sh1, B_sh1, hTt, bh, t4 * 128)
                    yield
            if "proj" in parts:
                yield from _bf_proj(i, full, hTt, bh, gvt_, bgvt_, afT_, bafT_)
            if "rope" in parts:
                yield from _bf_rope(i, full, hTt, bh, gro_)

        def _bf_proj(i, full, hTt, bh, gvt_, bgvt_, afT_, bafT_):
            if full:
                qt, bqt = QTR[i % 2], B_QTR[i % 2]
                kt, bkt = KTR[i % 3], B_KTR[i % 3]
                for j in range(4):
                    pb, bpb = proj_fm(hTt, bh, j * 128, 128, 512)
                    act(qt[:, j, :], pb[:, :], AF.Identity, [bpb], [bqt], scale=0.125)
                for j in range(4):
                    pb, bpb = proj_fm(hTt, bh, 512 + j * 128, 128, 512)
                    cp("dve", kt[:, j, :], pb[:, :], [bpb], [bkt])
                for j in range(4):
                    pb, bpb = proj_fm(hTt, bh, 2560 + j * 128, 128, 512)
                    act(srT[:, j, :], pb[:, :], AF.Silu, [bpb], [B_srT])
                for t4 in range(4):
                    vi = (i * 4 + t4) % 12
                    pb, bpb = proj_tm(hTt, bh, t4 * 128, 1024)
                    cp("act", VR[vi][:, :, 0, :], pb[:, :].rearrange("p (h d) -> p h d", h=8), [bpb], [B_VR[vi]])
            yield
            for t4 in range(4):
                pb, bpb = proj_tm(hTt, bh, t4 * 128, 2048)
                cp("dve", gvt_[:, t4, :], pb[:, :], [bpb], [bgvt_])
                yield
            pb, bpb = proj_fm(hTt, bh, 3072, 32, 512)
            cp("act", afT_[:, :], pb[0:32, :], [bpb], [bafT_])
            yield

        def _bf_rope(i, full, hTt, bh, gro_):
            dma("sp", cosb[:], cos_d[:, i * 512:(i + 1) * 512], [], [B_cosb])
            for idx in ([0, 1, 2, 3] if full else [2, 3]):
                col = (1536 if idx < 2 else 1792) + (idx % 2) * 128
                pb, bpb = proj_fm(hTt, bh, col, 128, 512)
                rw, brw = graw[idx]
                cp("act", rw[:], pb[:, :], [bpb], [brw])
                pr, bpr = ps[7], B_ps[7]
                mm(pr[:, :], pm[:], rw[:], True, True, [B_pm, brw], [bpr])
                dma("sp", sinb[:], sin_d[:, i * 512:(i + 1) * 512], [], [B_sinb])
                tt("pool", f1[:], rw[:], cosb[:], ALU.mult, [brw, B_cosb], [B_f1])
                tt("dve", f2[:], pr[:, :], sinb[:], ALU.mult, [bpr, B_sinb], [B_f2])
                tt("dve", gro_[idx][0][:], f1[:], f2[:], ALU.add, [B_f1, B_f2], [gro_[idx][1]])
                yield

        if kstop <= 2:
            P.add("sp", lambda e: e.nop(), B_out, [])
            P.emit(nc, st)
            return nc
        sblr = [_w2[11], _w2[12]]
        stg, B_stg = sblr[0]
        memset("pool", stg[:], 0.0, [B_stg])
        B_scrch = [Buf() for _ in range(NT)]
        mg1, bmg1 = mixG[1]
        set0 = (gvt, B_gvt, afT, B_afT, grope)
        gro1 = [None, None, (mg1[:, 1, :], bmg1), (mg1[:, 2, :], bmg1)]
        set1 = (srT, B_srT, mg1[0:32, 0, :], bmg1, gro1)
        psets = [set0, set1]

        def pre_b(i):
            gvt_, bgvt_, afT_, bafT_, gro_ = psets[i % 2]
            for dj in (2, 3):
                X_, BX_ = gla_gates(dj, 512, True, afsrc=(afT_, bafT_))
                yield
                gla_k(dj, 512, gro_[2 + dj % 2][0], gro_[2 + dj % 2][1], False, X=X_, BX=BX_)
                yield
            for c in (3, 2, 1, 0):
                ch = i * 4 + c
                for dj in (2, 3):
                    for hh in range(2):
                        head = (dj - 2) * 2 + hh
                        cp("act", stg[hh * 64:(hh + 1) * 64, head * 128:(head + 1) * 128], Sst[dj][0][hh * 64:(hh + 1) * 64, :],
                           [Sst[dj][1]], [B_stg])
                dma("sp", sbscr_d[ch], stg[:], [B_stg], [B_scrch[ch]])
                pkv, bpkv = gla_kv(c, [2, 3], gvt_[:, c, :], bgvt_)
                for dj in (2, 3):
                    scan_update(dj, c, pkv, bpkv)
                yield

        for _ in block_front(NB - 1, 0, False, psets[(NB - 1) % 2]):
            pass
        for i in range(NB - 1, -1, -1):
            ga = pre_b(i)
            gb = block_front(i - 1, 0, False, psets[(i - 1) % 2]) if i >= 1 else None
            run_interleaved(ga, gb, 1)
        if kstop <= 3:
            P.add("sp", lambda e: e.nop(), B_out, [])
            P.emit(nc, st)
            return nc

        def gla_main(i, slot):
            for dj in range(4):
                X_, BX_ = gla_gates(dj, 512, dj >= 2)
                pair = dj % 2
                stt(qdec[dj][0][:], grope[pair][0][:], 0.125, E1[:], ALU.mult, ALU.mult, [grope[pair][1], B_E1], [qdec[dj][1]])
                yield
                gla_k(dj, 512, grope[2 + pair][0], grope[2 + pair][1], True, kend_too=(dj < 2), X=X_, BX=BX_)
                yield
            gla_kendT([0, 1], 512)
            mx, bmx = mixG[slot]
            for c in range(4):
                ch = i * 4 + c
                sbl, B_sbl = sblr[ch % 2]
                dma("sp", sbl[:], sbscr_d[ch], [B_scrch[ch]], [B_sbl])
                for d in range(2):
                    pa, bpa = ps[(5, 7)[d]], B_ps[(5, 7)[d]]
                    for head in range(4):
                        pair, hh = head // 2, head % 2
                        dj = d * 2 + pair
                        mm(pa[:, head * 128:(head + 1) * 128], kinv[dj][0][:, hh, c * 128:(c + 1) * 128],
                           qdec[dj][0][:, c * 128:(c + 1) * 128], True, True, [kinv[dj][1], qdec[dj][1]], [bpa])
                    tt("dve", Asb[d][0][:], pa[:, :], tri[:, d, :], ALU.mult, [bpa, B_tri], [Asb[d][1]])
                    yield
                for dj in (0, 1):
                    for hh in range(2):
                        head = dj * 2 + hh
                        cp("act", SbfZ[hh * 64:(hh + 1) * 64, head * 128:(head + 1) * 128], Sst[dj][0][hh * 64:(hh + 1) * 64, :],
                           [Sst[dj][1]], [B_SbfZ])
                po, bpo = ps[6], B_ps[6]
                for head in range(4):
                    pair, hh = head // 2, head % 2
                    osl = po[:, head * 128:(head + 1) * 128]
                    mm(osl, gvt[:, c, head * 128:(head + 1) * 128], Asb[0][0][:, head * 128:(head + 1) * 128], True, False, [B_gvt, Asb[0][1]], [bpo])
                    mm(osl, gvt[:, c, head * 128:(head + 1) * 128], Asb[1][0][:, head * 128:(head + 1) * 128], False, False, [B_gvt, Asb[1][1]], [bpo])
                    mm(osl, SbfZ[:, head * 128:(head + 1) * 128], qdec[pair][0][:, c * 128:(c + 1) * 128], False, False,
                       [B_SbfZ, qdec[pair][1]], [bpo])
                    mm(osl, sbl[:, head * 128:(head + 1) * 128], qdec[2 + pair][0][:, c * 128:(c + 1) * 128],
                       False, True, [B_sbl, qdec[2 + pair][1]], [bpo])
                yield
                act(sq[:], po[:, :], AF.Square, [bpo], [B_sq])
                pn, bpn = ps[5], B_ps[5]
                mm(pn[:, :], onesb[:], sq[:], True, True, [B_onesb, B_sq], [bpn])
                act(f1[:], pn[:, :], AF.Ln, [bpn], [B_f1], scale=1.0 / 128.0, bias=EPS)
                act(f1[:], f1[:], AF.Exp, [B_f1], [B_f1], scale=-0.5)
                tt("dve", f2[:], po[:, :], f1[:], ALU.mult, [bpo, B_f1], [B_f2])
                for head in range(4):
                    stt(mx[:, head, c * 128:(c + 1) * 128], f2[:, head * 128:(head + 1) * 128], ggain[:, head:head + 1],
                        srT[:, head, c * 128:(c + 1) * 128], ALU.mult, ALU.mult, [B_f2, B_ggain, B_srT], [bmx])
                yield
                pkv, bpkv = gla_kv(c, [0, 1], gvt[:, c, :], B_gvt)
                for dj in (0, 1):
                    scan_update(dj, c, pkv, bpkv)
                yield

        def na_block(m, slot):
            mx, bmx = mixN
            qt, bqt = QTR[m % 2], B_QTR[m % 2]
            plan = na_plan(ROWS, m)
            n_pt = 0
            for h in range(8):
                j, hh = h // 2, h % 2
                qh = qt[hh * 64:(hh + 1) * 64, j, :]
                po, bpo = ps[4], B_ps[4]
                etb, betb = ETR[h % 2], B_ETR[h % 2]
                dma("sp", etb, etscr_d[h], [B_etscr[h]], [betb])
                items = [("c", 0), ("c", 1)] + [("l", e) for e in plan]
                pend = None
                for n, (kind, e) in enumerate(items):
                    bi = (n % 2) if hh == 0 else (2 + n % 2)
                    psS, bpsS = ps[bi], B_ps[bi]
                    pt, bpt = Pt[n_pt % 2]
                    n_pt += 1
                    if kind == "c":
                        c0, c1 = 0, 512
                        kop = KcT[hh * 64:(hh + 1) * 64, j, e * 128:(e + 1) * 128]
                        bk = B_KcT
                        vt, bv = Vc[e]
                    else:
                        t, runs, rlo, rhi = e
                        c0, c1 = (rlo - 8 * m) * 64, (rhi - 8 * m + 1) * 64
                        blk, tin = t // 4, t % 4
                        kop = KTR[blk % 3][hh * 64:(hh + 1) * 64, j, tin * 128:(tin + 1) * 128]
                        bk = B_KTR[blk % 3]
                        vt, bv = VR[t % 12], B_VR[t % 12]
                    mm(psS[:, c0:c1], kop, qh[:, c0:c1], True, True, [bk, bqt], [bpsS])
                    act(pt[:, c0:c1], psS[:, c0:c1], AF.Exp, [bpsS], [bpt])
                    if kind == "l":
                        for ty, ra, rb in runs:
                            a0, a1 = (ra - 8 * m) * 64, (rb - 8 * m + 1) * 64
                            j0 = 7 - 2 * t + ra
                            tb = etb[:, ty * 1024 + j0 * 64: ty * 1024 + j0 * 64 + (a1 - a0)]
                            tt("pool" if n % 2 == 0 else "dve", pt[:, a0:a1], pt[:, a0:a1], tb, ALU.mult, [bpt, betb], [bpt])
                    if pend is not None:
                        pend()
                    def _pv(po=po, vt=vt, pt=pt, c0=c0, c1=c1, n=n, bv=bv, bpt=bpt, bpo=bpo, nitems=len(items), h=h):
                        mm(po[:, c0:c1], vt[:, h, :, :].rearrange("p t d -> p (t d)"), pt[:, c0:c1], n == 0, n == nitems - 1,
                           [bv, bpt], [bpo])
                    pend = _pv
                    yield
                pend()
                act(rd[64:128, 0:512], po[64:128, :], AF.Ln, [bpo], [B_rd])
                act(rd[64:128, 0:512], rd[64:128, 0:512], AF.Exp, [B_rd], [B_rd], scale=-1.0)
                tt("dve", mx[hh * 64:(hh + 1) * 64, j, :], po[0:64, :], rd[64:128, 0:512], ALU.mult, [bpo, B_rd], [bmx])
                yield

        def out_block(m, slot):
            mg, bmg_ = mixG[slot]
            mn, bmn = mixN
            for t4 in range(4):
                r0 = m * 512 + t4 * 128
                xtile, bx = xt[xcnt[0] % 2]
                xcnt[0] += 1
                dma("sp", xtile[:], x_d[r0:r0 + 128, :], [], [bx])
                epilogue(lambda k, hf: ((mn[:, k, t4 * 128:(t4 + 1) * 128] if k < 4 else mg[:, k - 4, t4 * 128:(t4 + 1) * 128]),
                                        wout[:, k, hf * 512:(hf + 1) * 512], [bmn if k < 4 else bmg_, B_wout[k]]), 8,
                         xtile, bx, gate1, B_gate1, r0, pbase=(0 if t4 % 2 == 0 else 2))
                yield

        def epilogue(opnd, nk, xtile, bx, gate, bgate, r0, pbase=0):
            for hf in range(2):
                pb, bpb = ps[pbase + hf], B_ps[pbase + hf]
                for k in range(nk):
                    l_, r_, rb_ = opnd(k, hf)
                    mm(pb[:, :], l_, r_, k == 0, k == nk - 1, rb_, [bpb])
                P.add("act", lambda e, hf=hf, pb=pb: e.activation(out=xn[:, hf * 512:(hf + 1) * 512], in_=pb[:, :], func=AF.Square,
                                                                    accum_out=colt[:, 3 + hf:4 + hf]), [bpb], [B_xn, B_colt], cost=700.0)
            tt("dve", colt[:, 4:5], colt[:, 3:4], colt[:, 4:5], ALU.add, [B_colt], [B_colt])
            rstd_col(colt[:, 6:7], colt[:, 4:5], float(D), [B_colt], colt[:, 5:6], B_colt, B_colt)
            for hf in range(2):
                stt(xn[:, hf * 512:(hf + 1) * 512], ps[pbase + hf][:, :], colt[:, 6:7], gate[:, hf * 512:(hf + 1) * 512], ALU.mult, ALU.mult,
                    [B_ps[pbase + hf], B_colt, bgate], [B_xn])
            tt("pool", xtile[:], xtile[:], xn[:], ALU.add, [bx, B_xn], [bx])
            dma("sp", out_d[r0:r0 + 128, :], xtile[:], [bx], [B_out[r0 // 128]])

        def stream_a(m):
            yield from na_block(m, m % 2)
            yield from out_block(m, m % 2)

        import os as _os2
        RA_ = int(_os2.environ.get('K_RA', '3'))
        EC_ = int(_os2.environ.get('K_EC', '6'))

        def run3(ga, gb, gc, flag, ra, every_c):
            live = [ga is not None, gb is not None, gc is not None]
            rnd = 0
            while any(live):
                if live[0]:
                    for _ in range(ra):
                        try:
                            next(ga)
                        except StopIteration:
                            live[0] = False
                            break
                if live[1]:
                    try:
                        next(gb)
                    except StopIteration:
                        live[1] = False
                if live[2] and (flag[0] or not live[1]) and (rnd % every_c == 0 or not (live[0] or live[1])):
                    try:
                        next(gc)
                    except StopIteration:
                        live[2] = False
                rnd += 1

        for _ in block_front(0, 0, True, parts=("norm",)):
            pass
        for i in range(NB + 1):
            if i < NB:
                for _ in block_front(i, 0, True, parts=("proj",)):
                    pass
            flag = [False]
            ga = stream_a(i - 1) if i >= 1 else None
            gb = None
            if i < NB:
                def stream_b(i=i, flag=flag):
                    yield from block_front(i, 0, True, parts=("rope",))
                    flag[0] = True
                    yield from gla_main(i, i % 2)
                gb = stream_b()
            gc = block_front(i + 1, 0, True, parts=("norm",)) if i + 1 < NB else None
            run3(ga, gb, gc, flag, RA_, EC_)

        if kstop <= 4:
            P.add("sp", lambda e: e.nop(), B_out, [])
            P.emit(nc, st)
            return nc
        B_wguq = [[Buf() for _ in range(4)] for _ in range(8)]
        B_fenceF = Buf()
        P.add("pool", lambda e: e.nop(), [], [B_fenceF] + B_arenaM)
        for q in (0, 2, 1, 3):
            for k in range(8):
                dma("pool", wgu[:, k, q * 1408:(q + 1) * 1408], wgu_d[k * 128:(k + 1) * 128, q * 1408:(q + 1) * 1408],
                    [B_fenceF], [B_wguq[k][q]])
        for j in range(NJ):
            dma("pool", wdn[:, j, :], wdn_d[j * 128:(j + 1) * 128, :], [B_fenceF], [B_wdn[j]])
        aT, B_aT = wk[:, :].rearrange("p (j n) -> p j n", j=NJ), Buf()
        h2b, B_h2b = wk2[:, 0:4096].rearrange("p (k n) -> p k n", k=8), Buf()
        xt2, B_xt2 = wk2[:, 4096:6144].bitcast(F32), Buf()
        P.add("dve", lambda e: e.memset(aT[:, 0, 0:2], 0.0), [], [B_aT, B_h2b, B_xt2] + B_wk + B_wk2)
        dma("sp", gate2[:], gscr_d[:, :], [B_gscr], [B_gate2])
        H2 = [hT[0], (h2b, B_h2b)]
        B_colt_n = Buf()

        def f_norm(i):
            h2, bh2 = H2[i % 2]
            for t4 in range(4):
                r0 = i * 512 + t4 * 128
                dma("sp", xt2, out_d[r0:r0 + 128, :], [B_out[r0 // 128]], [B_xt2])
                P.add("act", lambda e: e.activation(out=xn[:], in_=xt2, func=AF.Square, accum_out=colt[:, 0:1]), [B_xt2], [B_xn, B_colt_n],
                      cost=1100.0)
                rstd_col(colt[:, 2:3], colt[:, 0:1], float(D), [B_colt_n], colt[:, 1:2], B_colt_n, B_colt_n)
                ts("dve", xt2, xt2, colt[:, 2:3], None, ALU.mult, ALU.bypass, [B_xt2, B_colt_n], [B_xt2])
                yield
                for half in range(2):
                    pb, bpb = ps[2 + half], B_ps[2 + half]
                    for kk in range(4):
                        k = half * 4 + kk
                        tr(pb[:, kk * 128:(kk + 1) * 128], xt2[:, k * 128:(k + 1) * 128], identf[:], [B_xt2, B_identf], [bpb])
                    for kk in range(4):
                        k = half * 4 + kk
                        if kk % 2 == 0:
                            act(h2[:, k, t4 * 128:(t4 + 1) * 128], pb[:, kk * 128:(kk + 1) * 128], AF.Identity,
                                [bpb, B_gm2, B_sh2], [bh2], scale=gm2[:, k:k + 1], bias=sh2[:, k:k + 1])
                        else:
                            ts("dve", h2[:, k, t4 * 128:(t4 + 1) * 128], pb[:, kk * 128:(kk + 1) * 128], gm2[:, k:k + 1], sh2[:, k:k + 1],
                               ALU.mult, ALU.add, [bpb, B_gm2, B_sh2], [bh2])
                    yield

        def f_mm(i):
            h2, bh2 = H2[i % 2]
            for j in range(NJ):
                pg, bpg = ps[4 + (j % 2) * 2], B_ps[4 + (j % 2) * 2]
                pu, bpu = ps[5 + (j % 2) * 2], B_ps[5 + (j % 2) * 2]
                for k in range(8):
                    mm(pg[:, :], wgu[:, k, j * 128:(j + 1) * 128], h2[:, k, :], k == 0, k == 7, [B_wguq[k][(j * 128) // 1408], bh2], [bpg])
                for k in range(8):
                    mm(pu[:, :], wgu[:, k, FFN + j * 128:FFN + (j + 1) * 128], h2[:, k, :], k == 0, k == 7,
                       [B_wguq[k][(FFN + j * 128) // 1408], bh2], [bpu])
                fb, bfb = (f1, B_f1) if j % 2 == 0 else (f2, B_f2)
                act(fb[:], pg[:, :], AF.Silu, [bpg], [bfb])
                tt("dve", aT[:, j, :], pu[:, :], fb[:], ALU.mult, [bpu, bfb], [B_aT])
                yield
            for t4 in range(4):
                r0 = i * 512 + t4 * 128
                xtile, bx = xt[0]
                dma("sp", xtile[:], out_d[r0:r0 + 128, :], [B_out[r0 // 128]], [bx])
                epilogue(lambda k, hf: (aT[:, k, t4 * 128:(t4 + 1) * 128], wdn[:, k, hf * 512:(hf + 1) * 512], [B_aT, B_wdn[k]]), NJ,
                         xtile, bx, gate2, B_gate2, r0, pbase=(0 if t4 % 2 == 0 else 2))
                yield

        for _ in f_norm(0):
            pass
        for i in range(NB):
            run_interleaved(f_mm(i), f_norm(i + 1) if i + 1 < NB else None, 2)

        P.add("sp", lambda e: e.nop(), B_out, [])
        print("sbuf bytes remaining:", nc.sbuf_bytes_remaining)
        P.emit(nc, st)
    return nc


def host_inputs(L, x, c, ctx, c_ctx, w_mod, b_mod, norm_pre_mix, norm_post_mix, norm_pre_ffn, norm_post_ffn, w_in, na_rpb,
                gla_wa2_f, gla_ba_f, gla_wa2_b, gla_ba_b, gla_norm, w_out, w_gate_up, w_down):
    f = lambda a: np.ascontiguousarray(np.asarray(a, dtype=np.float32))
    B = x.shape[0]
    cosT, sinT, pm = _rope_tables(L)
    wa2blk = np.zeros((32, 512), np.float32)
    wa2blk[0:16, 0:256] = f(gla_wa2_f)[0]
    wa2blk[16:32, 256:512] = f(gla_wa2_b)[0]
    ba = np.concatenate([f(gla_ba_f)[0], f(gla_ba_b)[0]])
    bacol = np.ascontiguousarray(ba.reshape(4, 128).T)
    ggain = np.ascontiguousarray(f(gla_norm)[0].reshape(4, 128).T)
    tri = np.zeros((2, 128, 512), np.float32)
    s = np.arange(128)[:, None]
    t = np.arange(128)[None, :]
    tri[0] = np.tile((s <= t).astype(np.float32), (1, 4))
    tri[1] = np.tile((s >= t).astype(np.float32), (1, 4))
    smask = np.ones((128, 512), np.float32)
    smask[:, 0::128] = 0.0
    sel = np.zeros((2, 128), np.float32)
    sel[0, :] = 1.0
    common = {
        "w_mod": f(w_mod)[0], "b_mod": f(b_mod)[0][None, :],
        "g4": np.stack([f(norm_pre_mix)[0], f(norm_post_mix)[0], f(norm_pre_ffn)[0], f(norm_post_ffn)[0]]),
        "w_in": f(w_in)[0], "ettab": _et_tables(f(na_rpb)[0]), "wa2blk": wa2blk, "bacol": bacol, "glagain": ggain,
        "cosT": cosT, "sinT": sinT, "pm": pm, "identf": np.eye(128, dtype=np.float32), "tri4": tri, "scanmask": smask,
        "sel": sel, "w_out": f(w_out)[0], "w_gu": f(w_gate_up)[0], "w_down": f(w_down)[0],
    }
    maps = []
    for b in range(B):
        m = dict(common)
        m["x"] = f(x[b])
        m["ctx"] = f(ctx[b])
        m["cc"] = np.ascontiguousarray(np.stack([f(c[b]), f(c_ctx)], axis=1))
        maps.append(m)
    return maps


def kernel(**inputs):
    x = np.asarray(inputs["x"])
    L = x.shape[1]
    maps = host_inputs(L, **inputs)
    nc = build(L)
    res = run_bass_kernel_spmd(nc, maps, core_ids=list(range(len(maps))))
    return np.stack([np.asarray(r["out"], dtype=np.float32) for r in res.results], axis=0)
```
